# Optimizing a Trainium2 kernel written in Bass

```python
import math, functools
import jax, jax.numpy as jnp
from jax import lax
import numpy as np

D_MODEL = 1024
BATCH = 8
SEQ = 2048
DEPTH = 1
DEC_BATCH = 32
DEC_SEQ = 8
PAST_LEN = 8192
PAGE_SIZE = 128

N_HEADS = 8
N_KV_HEADS = 2
HEAD_DIM = 64
GROUP = N_HEADS // N_KV_HEADS
ATTN_WIDTH = N_HEADS * HEAD_DIM
KV_WIDTH = N_KV_HEADS * HEAD_DIM
ATTN_SCALE = HEAD_DIM ** -0.5
N_IDX_HEADS = 4
IDX_DIM = 64
IDX_SCALE = (IDX_DIM * N_IDX_HEADS) ** -0.5
TOPK_MAX = 256
Q_BLOCK = 128
N_BUCKETS = 32
MAX_DISTANCE = 128
D_CONV = 512
CONV_WIDTH = 3
N_KEYS = 128
N_EXPERTS = N_KEYS * N_KEYS
PEER_HEADS = 8
PEER_KEY_DIM = 128
PEER_HALF = PEER_KEY_DIM // 2
PEER_TOPK_HALF = 16
PEER_TOPK = 16
PEER_BLOCK = 128
DN_ALPHA = (2 * DEPTH) ** 0.25
DN_BETA = (8 * DEPTH) ** -0.25
LN_EPS = 1e-5
MIX_WIDTHS = (ATTN_WIDTH, KV_WIDTH, KV_WIDTH, N_IDX_HEADS * IDX_DIM, N_IDX_HEADS, IDX_DIM,
              D_CONV, D_CONV, D_CONV, D_MODEL, D_MODEL)
D_MIX_IN = sum(MIX_WIDTHS)

kernel_name = "dsa_shortconv_peer_deepnorm_adaln_step"


def layer_norm(x, g, b):
    xf = x.astype(jnp.float32)
    mu = jnp.mean(xf, axis=-1, keepdims=True)
    var = jnp.mean(jnp.square(xf - mu), axis=-1, keepdims=True)
    return ((xf - mu) * lax.rsqrt(var + LN_EPS)).astype(x.dtype) * g + b


def split_columns(proj):
    parts, start = [], 0
    for w in MIX_WIDTHS:
        parts.append(proj[..., start:start + w])
        start += w
    return parts


def t5_bucket(dist):
    n = jnp.maximum(dist, 0)
    max_exact = N_BUCKETS // 2
    nf = jnp.maximum(n, 1).astype(jnp.float32)
    large = max_exact + (jnp.log(nf / max_exact) / math.log(MAX_DISTANCE / max_exact)
                         * (N_BUCKETS - max_exact)).astype(jnp.int32)
    large = jnp.minimum(large, N_BUCKETS - 1)
    return jnp.where(n < max_exact, n, large)


def indexer_scores(qi, wi, ki, q_pos, k_pos):
    dots = jnp.einsum('bqhd,bld->bqhl', qi, ki).astype(jnp.float32)
    s = jnp.einsum('bqhl,bqh->bql', jax.nn.relu(dots), wi.astype(jnp.float32)) * IDX_SCALE
    mask = k_pos[None, :] <= q_pos[:, None]
    return jnp.where(mask[None], s, -jnp.inf)


def sparse_attend(q, kg, vg, q_pos, sel, rel_bias):
    logits = jnp.einsum('bqgrd,bqkgd->bqgrk', q, kg).astype(jnp.float32) * ATTN_SCALE
    dist = q_pos[None, :, None] - sel
    bias = rel_bias[t5_bucket(dist)]
    bias = bias.reshape(*sel.shape, N_KV_HEADS, GROUP).transpose(0, 1, 3, 4, 2)
    logits = jnp.where((dist >= 0)[:, :, None, None, :], logits + bias.astype(jnp.float32), -jnp.inf)
    p = jax.nn.softmax(logits, axis=-1).astype(vg.dtype)
    o = jnp.einsum('bqgrk,bqkgd->bqgrd', p, vg)
    return o.reshape(o.shape[0], o.shape[1], ATTN_WIDTH)


def attn_prompt(q, k, v, qi, wi, ki, rel_bias):
    B, S = q.shape[0], q.shape[1]
    n_sel = min(TOPK_MAX, S // 4)
    nblk = S // Q_BLOCK
    k_pos = jnp.arange(S)

    def block(args):
        qb, qib, wib, q_pos = args
        scores = indexer_scores(qib, wib, ki, q_pos, k_pos)
        _, sel = lax.top_k(scores, n_sel)
        kg = jax.vmap(lambda kk, ii: kk[ii])(k, sel)
        vg = jax.vmap(lambda vv, ii: vv[ii])(v, sel)
        return sparse_attend(qb, kg, vg, q_pos, sel, rel_bias)

    def to_blocks(a):
        return jnp.swapaxes(a.reshape(B, nblk, Q_BLOCK, *a.shape[2:]), 0, 1)

    pos = jnp.arange(S).reshape(nblk, Q_BLOCK)
    out = lax.map(block, (to_blocks(q), to_blocks(qi), to_blocks(wi), pos))
    return jnp.swapaxes(out, 0, 1).reshape(B, S, ATTN_WIDTH)


def gather_paged(pool, page_table, new_rows, sel):
    B, T, K = sel.shape
    past = page_table.shape[1] * PAGE_SIZE
    is_past = sel < past
    p = jnp.minimum(sel, past - 1)
    phys = jnp.take_along_axis(page_table, (p // PAGE_SIZE).reshape(B, T * K), axis=1).reshape(B, T, K)
    from_past = pool[phys, p % PAGE_SIZE]
    j = jnp.clip(sel - past, 0, new_rows.shape[1] - 1)
    from_new = jax.vmap(lambda n, i: n[i])(new_rows, j)
    return jnp.where(is_past[..., None, None], from_past, from_new)


def attn_sample(q, k_new, v_new, qi, wi, ki_new, cache_k, cache_v, cache_ki, page_table, rel_bias):
    B, T = q.shape[0], q.shape[1]
    past = page_table.shape[1] * PAGE_SIZE
    L = past + T
    n_sel = min(TOPK_MAX, L // 4)
    ki_past = cache_ki[page_table].reshape(B, past, IDX_DIM)
    ki_all = jnp.concatenate([ki_past, ki_new], axis=1)
    q_pos = past + jnp.arange(T)
    scores = indexer_scores(qi, wi, ki_all, q_pos, jnp.arange(L))
    _, sel = lax.top_k(scores, n_sel)
    kg = gather_paged(cache_k, page_table, k_new, sel)
    vg = gather_paged(cache_v, page_table, v_new, sel)
    return sparse_attend(q, kg, vg, q_pos, sel, rel_bias)


def short_conv(bg, cg, xin, prev, conv_w, conv_b):
    u = cg * xin
    up = jnp.concatenate([prev, u], axis=1)
    T = u.shape[1]
    y = conv_b + sum(up[:, j:j + T] * conv_w[j] for j in range(CONV_WIDTH))
    return bg * y, up[:, up.shape[1] - (CONV_WIDTH - 1):]


def peer_ffn(h, wq, k1, k2, u_tab, v_tab):
    lead = h.shape[:-1]
    hf = h.reshape(-1, D_MODEL)
    n = hf.shape[0]
    hf = jnp.pad(hf, ((0, (-n) % PEER_BLOCK), (0, 0)))

    def block(hb):
        q = (hb @ wq).reshape(-1, PEER_HEADS, 2, PEER_HALF)
        s1 = jnp.einsum('nhd,kd->nhk', q[:, :, 0], k1).astype(jnp.float32)
        s2 = jnp.einsum('nhd,kd->nhk', q[:, :, 1], k2).astype(jnp.float32)
        v1, i1 = lax.top_k(s1, PEER_TOPK_HALF)
        v2, i2 = lax.top_k(s2, PEER_TOPK_HALF)
        cand = (v1[..., :, None] + v2[..., None, :]).reshape(*v1.shape[:-1], -1)
        cidx = (i1[..., :, None] * N_KEYS + i2[..., None, :]).reshape(*i1.shape[:-1], -1)
        sv, si = lax.top_k(cand, PEER_TOPK)
        eidx = jnp.take_along_axis(cidx, si, axis=-1)
        g = jax.nn.softmax(sv, axis=-1)
        a = jnp.einsum('nd,nhkd->nhk', hb, u_tab[eidx])
        act = (jax.nn.gelu(a.astype(jnp.float32)) * g).astype(hb.dtype)
        return jnp.einsum('nhk,nhkd->nd', act, v_tab[eidx])

    out = lax.map(block, hf.reshape(-1, PEER_BLOCK, D_MODEL))
    return out.reshape(-1, D_MODEL)[:n].reshape(*lead, D_MODEL)


def decoder_layer(x, c, prev_conv, attn_fn, w_ada, b_ada, w_in, conv_w, conv_b, w_o_attn, w_o_conv,
                  w_out, ln1_g, ln1_b, ln2_g, ln2_b, peer_wq, peer_k1, peer_k2, peer_u, peer_v):
    B, T = x.shape[0], x.shape[1]
    mod = (c @ w_ada + b_ada)[:, None, :]
    sh1, sc1, g1, sh2, sc2, g2 = jnp.split(mod, 6, axis=-1)
    h = x * (1 + sc1) + sh1
    q, k, v, qi, wi, ki, bg, cg, xin, ga, gb = split_columns(h @ w_in)
    q = q.reshape(B, T, N_KV_HEADS, GROUP, HEAD_DIM)
    k = k.reshape(B, T, N_KV_HEADS, HEAD_DIM)
    v = v.reshape(B, T, N_KV_HEADS, HEAD_DIM)
    qi = qi.reshape(B, T, N_IDX_HEADS, IDX_DIM)
    o_attn = attn_fn(q, k, v, qi, wi, ki)
    o_conv, conv_state = short_conv(bg, cg, xin, prev_conv, conv_w, conv_b)
    merged = jax.nn.sigmoid(ga) * (o_attn @ w_o_attn) + jax.nn.sigmoid(gb) * (o_conv @ w_o_conv)
    x = layer_norm(DN_ALPHA * x + g1 * (merged @ w_out), ln1_g, ln1_b)
    h2 = x * (1 + sc2) + sh2
    x = layer_norm(DN_ALPHA * x + g2 * peer_ffn(h2, peer_wq, peer_k1, peer_k2, peer_u, peer_v), ln2_g, ln2_b)
    return x, k, v, ki, conv_state


def setup_inputs(seed: int = 0) -> dict:
    key = jax.random.key(seed)
    ks = jax.random.split(key, 32)
    f32 = jnp.float32

    def nrm(k, shape, s=1.0):
        return jax.random.normal(k, shape, f32) * s

    n_pages = PAST_LEN // PAGE_SIZE
    n_used = DEC_BATCH * n_pages
    n_pool = n_used + max(1, n_used // 4)
    page_table = jax.random.permutation(ks[0], n_pool)[:n_used].reshape(DEC_BATCH, n_pages).astype(jnp.int32)
    return {
        "x_prompt": nrm(ks[1], (BATCH, SEQ, D_MODEL)),
        "x_sample": nrm(ks[2], (DEC_BATCH, DEC_SEQ, D_MODEL)),
        "c_prompt": nrm(ks[3], (BATCH, D_MODEL)),
        "c_sample": nrm(ks[4], (DEC_BATCH, D_MODEL)),
        "cache_k": nrm(ks[5], (DEPTH, n_pool, PAGE_SIZE, N_KV_HEADS, HEAD_DIM)),
        "cache_v": nrm(ks[6], (DEPTH, n_pool, PAGE_SIZE, N_KV_HEADS, HEAD_DIM)),
        "cache_kidx": nrm(ks[7], (DEPTH, n_pool, PAGE_SIZE, IDX_DIM)),
        "state_conv": nrm(ks[8], (DEPTH, DEC_BATCH, CONV_WIDTH - 1, D_CONV)),
        "page_table": page_table,
        "rel_bias": nrm(ks[9], (N_BUCKETS, N_HEADS), 0.5),
        "w_ada": nrm(ks[10], (DEPTH, D_MODEL, 6 * D_MODEL), 0.5 * D_MODEL ** -0.5),
        "b_ada": nrm(ks[11], (DEPTH, 6 * D_MODEL), 0.01),
        "w_in": nrm(ks[12], (DEPTH, D_MODEL, D_MIX_IN), D_MODEL ** -0.5),
        "conv_w": nrm(ks[13], (DEPTH, CONV_WIDTH, D_CONV), CONV_WIDTH ** -0.5),
        "conv_b": nrm(ks[14], (DEPTH, D_CONV), 0.01),
        "w_o_attn": nrm(ks[15], (DEPTH, ATTN_WIDTH, D_MODEL), DN_BETA * ATTN_WIDTH ** -0.5),
        "w_o_conv": nrm(ks[16], (DEPTH, D_CONV, D_MODEL), DN_BETA * D_CONV ** -0.5),
        "w_out": nrm(ks[17], (DEPTH, D_MODEL, D_MODEL), DN_BETA * D_MODEL ** -0.5),
        "ln1_g": 1.0 + nrm(ks[18], (DEPTH, D_MODEL), 0.02),
        "ln1_b": nrm(ks[19], (DEPTH, D_MODEL), 0.02),
        "ln2_g": 1.0 + nrm(ks[20], (DEPTH, D_MODEL), 0.02),
        "ln2_b": nrm(ks[21], (DEPTH, D_MODEL), 0.02),
        "peer_wq": nrm(ks[22], (DEPTH, D_MODEL, PEER_HEADS * PEER_KEY_DIM), D_MODEL ** -0.5),
        "peer_k1": nrm(ks[23], (DEPTH, N_KEYS, PEER_HALF), PEER_HALF ** -0.5),
        "peer_k2": nrm(ks[24], (DEPTH, N_KEYS, PEER_HALF), PEER_HALF ** -0.5),
        "peer_u": nrm(ks[25], (DEPTH, N_EXPERTS, D_MODEL), D_MODEL ** -0.5),
        "peer_v": nrm(ks[26], (DEPTH, N_EXPERTS, D_MODEL), DN_BETA * (PEER_HEADS * PEER_TOPK) ** -0.5),
    }


def reference(x_prompt, x_sample, c_prompt, c_sample, cache_k, cache_v, cache_kidx, state_conv, page_table,
              rel_bias, w_ada, b_ada, w_in, conv_w, conv_b, w_o_attn, w_o_conv, w_out, ln1_g, ln1_b,
              ln2_g, ln2_b, peer_wq, peer_k1, peer_k2, peer_u, peer_v):
    xp, xs = x_prompt, x_sample
    kp_l, vp_l, kip_l, cp_l, ks_l, vs_l, kis_l, cs_l = [], [], [], [], [], [], [], []
    for l in range(DEPTH):
        lw = (w_ada[l], b_ada[l], w_in[l], conv_w[l], conv_b[l], w_o_attn[l], w_o_conv[l], w_out[l],
              ln1_g[l], ln1_b[l], ln2_g[l], ln2_b[l], peer_wq[l], peer_k1[l], peer_k2[l], peer_u[l], peer_v[l])
        prev0 = jnp.zeros((xp.shape[0], CONV_WIDTH - 1, D_CONV), xp.dtype)
        fn_p = functools.partial(attn_prompt, rel_bias=rel_bias)
        xp, kp, vp, kip, cp = decoder_layer(xp, c_prompt, prev0, fn_p, *lw)
        fn_s = functools.partial(attn_sample, cache_k=cache_k[l], cache_v=cache_v[l], cache_ki=cache_kidx[l],
                                 page_table=page_table, rel_bias=rel_bias)
        xs, ks_, vs_, kis, cs = decoder_layer(xs, c_sample, state_conv[l], fn_s, *lw)
        kp_l.append(kp); vp_l.append(vp); kip_l.append(kip); cp_l.append(cp)
        ks_l.append(ks_); vs_l.append(vs_); kis_l.append(kis); cs_l.append(cs)
    return (xp, xs, jnp.stack(kp_l), jnp.stack(vp_l), jnp.stack(kip_l), jnp.stack(cp_l),
            jnp.stack(ks_l), jnp.stack(vs_l), jnp.stack(kis_l), jnp.stack(cs_l))
```

```python
import math
from contextlib import ExitStack
import numpy as np
import concourse.bass as bass
import concourse.mybir as mybir
from concourse.bass_utils import run_bass_kernel_spmd

F32 = mybir.dt.float32
BF16 = mybir.dt.bfloat16
U32 = mybir.dt.uint32
I32 = mybir.dt.int32
ALU = mybir.AluOpType
AF = mybir.ActivationFunctionType
AX = mybir.AxisListType

D = 1024
SEQ = 2048
NTOK = 2080
NEG = -30000.0
ATTN_SCALE = 64 ** -0.5
IDX_SCALE = 256 ** -0.5
ALPHA = 2 ** 0.25
LN_EPS = 1e-5
NBIS = 26
STOP_AFTER = [99]
DBG = dict(strict=True, pool_rows=2560 * 128, prep=128, tblocks=list(range(16)), batches=list(range(4)), npages=64)


class _Stop(Exception):
    pass


_DISCARD = [False]


def stop_check(k):
    if STOP_AFTER[0] <= k:
        _DISCARD[0] = True
EPOCH = 16000


class Buf:
    __slots__ = ("name", "w", "r", "dsem", "dcnt")

    def __init__(self, name):
        self.name = name
        self.w = []
        self.r = []
        self.dsem = None
        self.dcnt = 0


class Tok:
    __slots__ = ("sem", "val", "eng", "grp")

    def __init__(self, sem, val, eng, grp=False):
        self.sem, self.val, self.eng, self.grp = sem, val, eng, grp


class Sched:
    ENG = ("pe", "act", "dve", "pool", "sp")

    def __init__(self, nc, es):
        self.nc = nc
        self.es = es
        self.q = {e: [] for e in self.ENG}
        self.n = {e: 0 for e in self.ENG}
        self.esem = {e: None for e in self.ENG}
        self.waited = {e: {} for e in self.ENG}
        self.semid = 0
        self.dma_sems = []
        self.last = {}
        self.ninstr = 0

    def new_sem(self, tag):
        self.semid += 1
        return self.es.enter_context(self.nc.semaphore(f"s{self.semid}_{tag}"))

    def _deps(self, eng, reads, writes, add):
        strict = DBG.get("strict")
        pe_chain = (eng == "pe" and add)
        deps = []
        for b in reads:
            for t in b.w:
                if t.eng == eng and eng == "pe" and not strict:
                    continue
                deps.append(t)
        for b in writes:
            for t in b.r:
                if t.eng == eng and t.eng is not None and (not strict or pe_chain):
                    continue
                deps.append(t)
            for t in b.w:
                if add and t.grp:
                    continue
                if t.eng == eng and t.eng is not None and (not strict or pe_chain):
                    continue
                deps.append(t)
        best = {}
        for t in deps:
            k = id(t.sem)
            if k not in best or best[k].val < t.val:
                best[k] = t
        out = []
        wd = self.waited[eng]
        for k, t in best.items():
            if wd.get(k, -1) >= t.val:
                continue
            wd[k] = t.val
            out.append((t.sem, t.val))
        return out

    def _commit(self, tok, reads, writes, add):
        for b in reads:
            b.r = [t for t in b.r if t.sem is not tok.sem] + [tok]
        for b in writes:
            if add:
                b.w = [t for t in b.w if t.sem is not tok.sem] + [tok]
            else:
                b.w = [tok]
            b.r = []

    def op(self, eng, fn, reads=(), writes=(), add=False):
        if _DISCARD[0]:
            return
        waits = self._deps(eng, reads, writes, add)
        if self.esem[eng] is None or self.n[eng] >= EPOCH:
            self.esem[eng] = self.new_sem(eng)
            self.n[eng] = 0
        self.n[eng] += 1
        sem, val = self.esem[eng], self.n[eng]
        tok = Tok(sem, val, eng, add)
        self.last[eng] = tok
        self._commit(tok, reads, writes, add)

        def thunk(e, fn=fn, waits=waits, sem=sem):
            for (s, v) in waits:
                e.wait_ge(s, v)
            fn(e).then_inc(sem, 1)
        self.q[eng].append(thunk)
        self.ninstr += 1

    def dma(self, eng, fn, reads=(), writes=(), sembuf=None, add=False):
        if _DISCARD[0]:
            return
        waits = self._deps(eng, reads, writes, add)
        sb = sembuf if sembuf is not None else (writes[0] if writes else reads[0])
        kind = "sw" if eng == "pool" else "hw"
        if sb.dsem is None:
            sb.dsem = {}
        if kind not in sb.dsem:
            ent = [self.new_sem("d" + kind), 0]
            sb.dsem[kind] = ent
            self.dma_sems.append(ent)
        ent = sb.dsem[kind]
        ent[1] += 16
        tok = Tok(ent[0], ent[1], None, add)
        self._commit(tok, reads, writes, add)

        def thunk(e, fn=fn, waits=waits, sem=ent[0]):
            for (s, v) in waits:
                e.wait_ge(s, v)
            fn(e).then_inc(sem, 16)
        self.q[eng].append(thunk)
        self.ninstr += 1

    def barrier(self, force=False, flush=False):
        if not (_DISCARD[0] and not force):
            toks = [t for t in self.last.values()]
            toks += [Tok(ent[0], ent[1], None) for ent in self.dma_sems]
            for eng in self.ENG:
                waits = []
                wd = self.waited[eng]
                for t in toks:
                    if t.eng == eng:
                        continue
                    k = id(t.sem)
                    if wd.get(k, -1) >= t.val:
                        continue
                    wd[k] = t.val
                    waits.append((t.sem, t.val))

                def thunk(e, waits=waits):
                    for (s, v) in waits:
                        e.wait_ge(s, v)
                self.q[eng].append(thunk)
        if flush:
            self.flush()

    def flush(self):
        nc = self.nc
        q = self.q
        if not any(q[e] for e in self.ENG):
            return
        with nc.Block() as block:
            @block.tensor
            def _(e):
                for t in q["pe"]:
                    t(e)

            @block.scalar
            def _(e):
                for t in q["act"]:
                    t(e)

            @block.vector
            def _(e):
                for t in q["dve"]:
                    t(e)

            @block.gpsimd
            def _(e):
                for t in q["pool"]:
                    t(e)

            @block.sync
            def _(e):
                for t in q["sp"]:
                    t(e)
        self.q = {e: [] for e in self.ENG}

    def finish(self):
        self.barrier(force=True)
        self.flush()


class TL:
    def __init__(self, t, name):
        self.t = t
        self.b = Buf(name)

    def __getitem__(self, k):
        return self.t[k]


def t5_bucket_np(d):
    n = np.maximum(d, 0)
    nf = np.maximum(n, 1).astype(np.float32)
    large = 16 + (np.log(nf / np.float32(16)) / np.float32(math.log(128 / 16)) * np.float32(16)).astype(np.int32)
    large = np.minimum(large, 31)
    return np.where(n < 16, n, large)


def build_program():
    nc = bass.Bass("TRN2", target_bir_lowering=False)
    _DISCARD[0] = False

    def din(name, shape, dt=F32):
        return nc.dram_tensor(name, list(shape), dt, kind="ExternalInput").ap()

    def dout(name, shape, dt=F32):
        return nc.dram_tensor(name, list(shape), dt, kind="ExternalOutput").ap()

    xp = din("xp", [SEQ, D]); xs = din("xs", [32, D]); cvec = din("cvec", [5, D])
    ck = din("ck", [DBG["pool_rows"], 128]); cvv = din("cv", [DBG["pool_rows"], 128]); cki = din("cki", [DBG["pool_rows"], 64])
    sconv = din("sconv", [8, 512]); ptab = din("ptab", [4, 64], I32)
    rel_bias = din("rel_bias", [32, 8])
    w_ada = din("w_ada", [D, 6 * D]); b_ada = din("b_ada", [1, 6 * D])
    w_in = din("w_in", [D, 4676])
    conv_w = din("conv_w", [3, 512]); conv_b = din("conv_b", [1, 512])
    w_o_attn = din("w_o_attn", [512, D]); w_o_conv = din("w_o_conv", [512, D]); w_out = din("w_out", [D, D])
    ln1_g = din("ln1_g", [1, D]); ln1_b = din("ln1_b", [1, D]); ln2_g = din("ln2_g", [1, D]); ln2_b = din("ln2_b", [1, D])
    peer_wq = din("peer_wq", [D, D]); peer_k1 = din("peer_k1", [128, 64]); peer_k2 = din("peer_k2", [128, 64])
    peer_u = din("peer_u", [16384, D]); peer_v = din("peer_v", [16384, D])
    ohb = din("ohb", [32, 383]); negrow = din("negrow", [1, 383])
    sel_p = din("sel_p", [5, 128]); sel_s = din("sel_s", [5, 32])

    y_p = dout("y_p", [SEQ, D]); y_s = dout("y_s", [32, D])
    k_p = dout("k_p", [SEQ, 128]); v_p = dout("v_p", [SEQ, 128]); ki_p = dout("ki_p", [SEQ, 64])
    conv_o = dout("conv_o", [10, 512])
    k_s = dout("k_s", [32, 128]); v_s = dout("v_s", [32, 128]); ki_s = dout("ki_s", [32, 64])

    x1_scr = nc.dram_tensor("x1_scr", [NTOK, D], F32).ap()
    pe_dbg = DBG.get("ydump") == "pe"
    tsc = nc.dram_tensor("tsc", [8, 128, 383], F32).ap()
    ut_scr = nc.dram_tensor("ut_scr", [128, 128, 1024], BF16).ap()
    v_scr = nc.dram_tensor("v_scr", [16384, D], BF16).ap()
    b_x1scr = Buf("x1scr"); b_tsc = Buf("tsc"); b_utscr = Buf("utscr"); b_vscr = Buf("vscr")
    b_out = Buf("outputs")

    w_in_r = w_in.rearrange("(kc p) n -> p kc n", p=128)

    with ExitStack() as es:
        S = Sched(nc, es)
        cnt = [0]

        def sb(stack, shape, dt, name=None):
            cnt[0] += 1
            nm = f"{name or 't'}{cnt[0]}"
            return TL(stack.enter_context(nc.sbuf_tensor(nm, list(shape), dt)), nm)

        PS = []
        for i in range(8):
            PS.append(TL(es.enter_context(nc.psum_tensor(f"ps{i}", [128, 512], F32)), f"ps{i}"))
        rr = {}

        def psr(key, banks):
            i = rr.get(key, 0)
            rr[key] = i + 1
            return PS[banks[i % len(banks)]]

        evq = [0]

        def evac(out_ap, out_b, in_ap, in_b, engs=("act", "dve")):
            e = engs[evq[0] % len(engs)]
            evq[0] += 1
            if e == "act":
                S.op("act", lambda en: en.activation(out_ap, in_ap, AF.Copy), reads=[in_b], writes=[out_b], add=True)
            else:
                S.op("dve", lambda en: en.tensor_copy(out_ap, in_ap), reads=[in_b], writes=[out_b], add=True)

        cs = es
        io = sb(cs, [128, 128], F32, "io")
        ident = sb(cs, [128, 128], F32, "ident")
        negA = sb(cs, [128, 128], F32, "negA")
        iota16 = sb(cs, [128, 16], F32, "iota16")
        iotap = sb(cs, [128, 1], F32, "iotap")
        ones_f = sb(cs, [128, 128], F32, "ones")
        modT = sb(cs, [128, 48, 5], F32, "modT")
        modrow_g = sb(cs, [5, 2, 1024], F32, "modrowg")
        Bm = sb(cs, [128, 8, 2, 128], F32, "Bm")
        KBD = sb(cs, [128, 256], BF16, "KBD")
        selp_sb = sb(cs, [5, 128], F32, "selp")
        sels_sb = sb(cs, [5, 32], F32, "sels")
        S.op("pool", lambda e: e.iota(io[:], [[1, 128]], base=0, channel_multiplier=-1, allow_small_or_imprecise_dtypes=True), writes=[io.b])
        S.op("pool", lambda e: e.iota(iota16[:], [[1, 16]], base=0, channel_multiplier=0, allow_small_or_imprecise_dtypes=True), writes=[iota16.b])
        S.op("pool", lambda e: e.iota(iotap[:], [[0, 1]], base=0, channel_multiplier=1, allow_small_or_imprecise_dtypes=True), writes=[iotap.b])
        S.op("dve", lambda e: e.tensor_scalar(ident[:], io[:], 0.0, None, op0=ALU.is_equal), reads=[io.b], writes=[ident.b])
        S.op("dve", lambda e: e.tensor_scalar(negA[:], io[:], 0.0, NEG, op0=ALU.is_gt, op1=ALU.mult), reads=[io.b], writes=[negA.b])
        S.op("dve", lambda e: e.memset(ones_f[:], 1.0), writes=[ones_f.b])
        S.op("pool", lambda e: e.memset(KBD[:], 0.0), writes=[KBD.b])
        S.dma("sp", lambda e: e.dma_start(out=selp_sb[:], in_=sel_p[:, :]), writes=[selp_sb.b])
        S.dma("sp", lambda e: e.dma_start(out=sels_sb[:], in_=sel_s[:, :]), writes=[sels_sb.b])

        try:
            with ExitStack() as p0:
                cv_sb = sb(p0, [5, D], F32, "cv")
                cT = sb(p0, [128, 8, 5], F32, "cT")
                S.dma("sp", lambda e: e.dma_start(out=cv_sb[:], in_=cvec[:, :]), writes=[cv_sb.b])
                ps = PS[0]
                for kc in range(8):
                    S.op("pe", lambda e, kc=kc: e.transpose(ps[:, kc * 5:(kc + 1) * 5], cv_sb[:, kc * 128:(kc + 1) * 128], ident[:5, :5]),
                         reads=[cv_sb.b, ident.b], writes=[ps.b], add=True)
                S.op("dve", lambda e: e.tensor_copy(cT[:].rearrange("p k c -> p (k c)"), ps[:, 0:40]), reads=[ps.b], writes=[cT.b])
                modrow = sb(p0, [5, 6 * D], F32, "modrow")
                was = [sb(p0, [128, 8, 512], F32, "wa") for _ in range(2)]
                bas = [sb(p0, [1, 512], F32, "ba") for _ in range(2)]
                w_ada_r = w_ada.rearrange("(kc p) n -> p kc n", p=128)
                for cg in range(12):
                    wa = was[cg % 2]; ba = bas[cg % 2]
                    S.dma("sp", lambda e, wa=wa, cg=cg: e.dma_start(out=wa[:], in_=w_ada_r[:, :, cg * 512:(cg + 1) * 512]), writes=[wa.b])
                    S.dma("sp", lambda e, ba=ba, cg=cg: e.dma_start(out=ba[:], in_=b_ada[:, cg * 512:(cg + 1) * 512]), writes=[ba.b])
                    pm = psr("mod", [1, 2])
                    for kc in range(8):
                        S.op("pe", lambda e, kc=kc, wa=wa, pm=pm: e.matmul(pm[0:5, :], cT[:, kc, :], wa[:, kc, :], start=(kc == 0), stop=False),
                             reads=[cT.b, wa.b], writes=[pm.b], add=(kc > 0))
                    S.op("pe", lambda e, ba=ba, pm=pm: e.matmul(pm[0:5, :], ones_f[0:1, 0:5], ba[0:1, :], start=False, stop=True),
                         reads=[ones_f.b, ba.b], writes=[pm.b], add=True)
                    S.op("act", lambda e, pm=pm, cg=cg: e.activation(modrow[:, cg * 512:(cg + 1) * 512], pm[0:5, :], AF.Copy),
                         reads=[pm.b], writes=[modrow.b], add=True)
                S.op("dve", lambda e: e.tensor_copy(modrow_g[:, 0, :], modrow[:, 2 * D:3 * D]), reads=[modrow.b], writes=[modrow_g.b])
                S.op("dve", lambda e: e.tensor_copy(modrow_g[:, 1, :], modrow[:, 5 * D:6 * D]), reads=[modrow.b], writes=[modrow_g.b], add=True)
                for jg in range(2):
                    pm = psr("mod", [1, 2])
                    for jj in range(24):
                        j = jg * 24 + jj
                        S.op("pe", lambda e, j=j, jj=jj, pm=pm: e.transpose(pm[:, jj * 5:(jj + 1) * 5], modrow[:, j * 128:(j + 1) * 128], ident[:5, :5]),
                             reads=[modrow.b, ident.b], writes=[pm.b], add=(jj > 0))
                    S.op("dve", lambda e, jg=jg, pm=pm: e.tensor_copy(modT[:, jg * 24:(jg + 1) * 24, :].rearrange("p j c -> p (j c)"), pm[:, 0:120]),
                         reads=[pm.b], writes=[modT.b], add=True)
                for j0 in (8, 32):
                    S.op("dve", lambda e, j0=j0: e.tensor_scalar(modT[:, j0:j0 + 8, :], modT[:, j0:j0 + 8, :], 1.0, None, op0=ALU.add),
                         reads=[modT.b], writes=[modT.b])
                rb_sb = sb(p0, [32, 8], F32, "rb")
                rbrep = sb(p0, [32, 8, 128], F32, "rbrep")
                ohb_sb = sb(p0, [32, 383], F32, "ohb")
                neg_sb = sb(p0, [1, 383], F32, "negr")
                tv = sb(p0, [128, 8, 383], F32, "tv")
                b31 = sb(p0, [128, 8], F32, "b31")
                S.dma("sp", lambda e: e.dma_start(out=rb_sb[:], in_=rel_bias[:, :]), writes=[rb_sb.b])
                S.dma("sp", lambda e: e.dma_start(out=ohb_sb[:], in_=ohb[:, :]), writes=[ohb_sb.b])
                S.dma("sp", lambda e: e.dma_start(out=neg_sb[:], in_=negrow[:, :]), writes=[neg_sb.b])
                for h in range(8):
                    S.op("dve", lambda e, h=h: e.tensor_copy(rbrep[:, h, :], rb_sb[:, h:h + 1].to_broadcast([32, 128])),
                         reads=[rb_sb.b], writes=[rbrep.b], add=True)
                for h in range(8):
                    pm = psr("mod", [1, 2])
                    S.op("pe", lambda e, h=h, pm=pm: e.matmul(pm[:, 0:383], rbrep[:, h, :], ohb_sb[:, :], start=True, stop=False),
                         reads=[rbrep.b, ohb_sb.b], writes=[pm.b])
                    S.op("pe", lambda e, pm=pm: e.matmul(pm[:, 0:383], ones_f[0:1, :], neg_sb[0:1, :], start=False, stop=True),
                         reads=[ones_f.b, neg_sb.b], writes=[pm.b], add=True)
                    S.op("act", lambda e, h=h, pm=pm: e.activation(b31[:, h:h + 1], pm[:, 382:383], AF.Copy), reads=[pm.b], writes=[b31.b], add=True)
                    S.op("dve", lambda e, h=h, pm=pm: e.tensor_scalar(tv[:, h, :], pm[:, 0:383], b31[:, h:h + 1], None, op0=ALU.subtract),
                         reads=[pm.b, b31.b], writes=[tv.b], add=True)
                S.dma("sp", lambda e: e.dma_start(out=tsc.rearrange("h p j -> p h j"), in_=tv[:]), reads=[tv.b], writes=[b_tsc])
                for h in range(8):
                    for ci, c in enumerate((0, 128)):
                        src = bass.AP(tsc.tensor, h * 128 * 383 + 127 + c, [[382, 128], [1, 128]])
                        S.dma("sp", lambda e, h=h, ci=ci, src=src: e.dma_start(out=Bm[:, h, ci, :], in_=src), reads=[b_tsc], writes=[Bm.b], add=True)
                k12 = sb(p0, [128, 128], F32, "k12")
                S.dma("sp", lambda e: e.dma_start(out=k12[:, 0:64], in_=peer_k1[:, :]), writes=[k12.b])
                S.dma("sp", lambda e: e.dma_start(out=k12[:, 64:128], in_=peer_k2[:, :]), writes=[k12.b], add=True)
                pm = psr("mod", [1, 2])
                S.op("pe", lambda e, pm=pm: e.transpose(pm[:, 0:128], k12[:], ident[:]), reads=[k12.b, ident.b], writes=[pm.b])
                S.op("dve", lambda e, pm=pm: e.tensor_copy(KBD[0:64, 0:128], pm[0:64, 0:128]), reads=[pm.b], writes=[KBD.b], add=True)
                S.op("dve", lambda e, pm=pm: e.tensor_copy(KBD[64:128, 128:256], pm[64:128, 0:128]), reads=[pm.b], writes=[KBD.b], add=True)

                S.barrier(flush=True)
            S.barrier()

            stop_check(0)
            TILES = [(128 * t, 128, 0) for t in range(16)] + [(2048 + 8 * b, 8, 1 + b) for b in range(4)]

            def load_h_T(stack_tiles, dstT, dst_b, col0, tok0, nt, cidx, shj, scj, src_ap):
                xt = stack_tiles[rr.get("xt", 0) % len(stack_tiles)]
                rr["xt"] = rr.get("xt", 0) + 1
                S.dma("sp", lambda e: e.dma_start(out=xt[:nt, :], in_=src_ap), reads=[b_x1scr], writes=[xt.b])
                for hb in range(2):
                    pm = psr("tr", [0, 1])
                    for k4 in range(4):
                        kc = hb * 4 + k4
                        S.op("pe", lambda e, kc=kc, k4=k4, pm=pm: e.transpose(pm[:, k4 * 128:k4 * 128 + nt], xt[:nt, kc * 128:(kc + 1) * 128], ident[:nt, :nt]),
                             reads=[xt.b, ident.b], writes=[pm.b], add=(k4 > 0))
                    for k4 in range(4):
                        kc = hb * 4 + k4
                        if isinstance(cidx, int):
                            segs = [(0, nt, cidx)]
                        else:
                            segs = cidx
                        for (c0, cn, ci) in segs:
                            S.op("act", lambda e, kc=kc, k4=k4, pm=pm, c0=c0, cn=cn, ci=ci: e.activation(
                                dstT[:, kc, col0 + c0:col0 + c0 + cn], pm[:, k4 * 128 + c0:k4 * 128 + c0 + cn], AF.Identity,
                                bias=modT[:, shj + kc, ci:ci + 1], scale=modT[:, scj + kc, ci:ci + 1]),
                                reads=[pm.b, modT.b], writes=[dst_b], add=True)

            act_es = ExitStack()
            o_convT = sb(act_es, [128, 4, NTOK], BF16, "oconvT")
            o_attnT = sb(act_es, [64, 8, NTOK], BF16, "oattnT")
            att_es = ExitStack()
            qT = sb(att_es, [128, 4, NTOK], BF16, "qT")
            kTd = sb(att_es, [128, 2, NTOK], BF16, "kTd")
            qiT = sb(att_es, [128, 2, NTOK], BF16, "qiT")
            kiTd = sb(att_es, [128, NTOK], BF16, "kiTd")
            Vx = sb(att_es, [128, 20, 2, 65], BF16, "Vx")
            wi_sb = sb(att_es, [128, 20, 4], F32, "wi")
            S.op("pool", lambda e: e.memset(Vx[:], 1.0), writes=[Vx.b])

            GROUPS = [(0, 512), (512, 512), (1024, 512), (1536, 512), (2048, 32)]

            with ExitStack() as p1:
                h1T = sb(p1, [128, 8, NTOK], BF16, "h1T")
                xts = [sb(p1, [128, D], F32, "xt") for _ in range(1)]
                for (tok0, nt, ci) in TILES:
                    src = xp[tok0:tok0 + nt, :] if tok0 < 2048 else xs[tok0 - 2048:tok0 - 2048 + nt, :]
                    load_h_T(xts, h1T, h1T.b, tok0, tok0, nt, ci, 0, 8, src)
                wq = sb(p1, [128, 8, 512], BF16, "wq")
                wkd = sb(p1, [128, 8, 2, 128], BF16, "wkd")
                wqi = sb(p1, [128, 8, 256], BF16, "wqi")
                wkid = sb(p1, [128, 8, 128], BF16, "wkid")
                wtm = sb(p1, [128, 8, 324], BF16, "wtm")
                S.dma("pool", lambda e: e.dma_start(out=wq[:], in_=w_in_r[:, :, 0:512]), writes=[wq.b])
                for g in range(2):
                    for hf in range(2):
                        S.dma("pool", lambda e, g=g, hf=hf: e.dma_start(out=wkd[:, :, g, hf * 64:(hf + 1) * 64], in_=w_in_r[:, :, 512 + 64 * g:576 + 64 * g]),
                              writes=[wkd.b], add=True)
                S.dma("pool", lambda e: e.dma_start(out=wqi[:], in_=w_in_r[:, :, 768:1024]), writes=[wqi.b])
                for hf in range(2):
                    S.dma("pool", lambda e, hf=hf: e.dma_start(out=wkid[:, :, hf * 64:(hf + 1) * 64], in_=w_in_r[:, :, 1028:1092]), writes=[wkid.b], add=True)
                S.dma("pool", lambda e: e.dma_start(out=wtm[:, :, 0:256], in_=w_in_r[:, :, 512:768]), writes=[wtm.b], add=True)
                S.dma("pool", lambda e: e.dma_start(out=wtm[:, :, 256:320], in_=w_in_r[:, :, 1028:1092]), writes=[wtm.b], add=True)
                S.dma("pool", lambda e: e.dma_start(out=wtm[:, :, 320:324], in_=w_in_r[:, :, 1024:1028]), writes=[wtm.b], add=True)
                fm_sets = []
                for j in range(4):
                    fm_sets.append((lambda kc, j=j: wq[:, kc, j * 128:(j + 1) * 128], wq.b, lambda c0, n, j=j: qT[:, j, c0:c0 + n], qT.b))
                for g in range(2):
                    fm_sets.append((lambda kc, g=g: wkd[:, kc, g, :], wkd.b, lambda c0, n, g=g: kTd[:, g, c0:c0 + n], kTd.b))
                for j in range(2):
                    fm_sets.append((lambda kc, j=j: wqi[:, kc, j * 128:(j + 1) * 128], wqi.b, lambda c0, n, j=j: qiT[:, j, c0:c0 + n], qiT.b))
                fm_sets.append((lambda kc: wkid[:, kc, :], wkid.b, lambda c0, n: kiTd[:, c0:c0 + n], kiTd.b))
                for (wf, wb, df, db) in fm_sets:
                    for (g0, gn) in GROUPS:
                        pm = psr("fm", [2, 3, 4, 5])
                        for kc in range(8):
                            S.op("pe", lambda e, kc=kc, pm=pm, wf=wf, g0=g0, gn=gn: e.matmul(pm[:, 0:gn], wf(kc), h1T[:, kc, g0:g0 + gn], start=(kc == 0), stop=(kc == 7)),
                                 reads=[wb, h1T.b], writes=[pm.b], add=(kc > 0))
                        evac(df(g0, gn), db, pm[:, 0:gn], pm.b)
                kvst = [sb(p1, [128, 324], F32, "kvst") for _ in range(2)]
                for ti, (tok0, nt, ci) in enumerate(TILES):
                    pm = psr("fm", [2, 3, 4, 5])
                    kv = kvst[ti % 2]
                    for kc in range(8):
                        S.op("pe", lambda e, kc=kc, pm=pm, tok0=tok0, nt=nt: e.matmul(pm[:nt, 0:324], h1T[:, kc, tok0:tok0 + nt], wtm[:, kc, :], start=(kc == 0), stop=(kc == 7)),
                             reads=[wtm.b, h1T.b], writes=[pm.b], add=(kc > 0))
                    S.op("act", lambda e, pm=pm, kv=kv, nt=nt: e.activation(kv[:nt, :], pm[:nt, 0:324], AF.Copy), reads=[pm.b], writes=[kv.b])
                    if tok0 < 2048:
                        ko, vo, kio, r0 = k_p, v_p, ki_p, tok0
                    else:
                        ko, vo, kio, r0 = k_s, v_s, ki_s, tok0 - 2048
                    S.dma("sp", lambda e, kv=kv, nt=nt, ko=ko, r0=r0: e.dma_start(out=ko[r0:r0 + nt, :], in_=kv[:nt, 0:128]), reads=[kv.b])
                    S.dma("sp", lambda e, kv=kv, nt=nt, vo=vo, r0=r0: e.dma_start(out=vo[r0:r0 + nt, :], in_=kv[:nt, 128:256]), reads=[kv.b])
                    S.dma("sp", lambda e, kv=kv, nt=nt, kio=kio, r0=r0: e.dma_start(out=kio[r0:r0 + nt, :], in_=kv[:nt, 256:320]), reads=[kv.b])
                    S.op("dve", lambda e, kv=kv, nt=nt, ti=ti: e.tensor_copy(Vx[:nt, ti, :, 0:64], kv[:nt, 128:256].rearrange("p (g d) -> p g d", g=2)),
                         reads=[kv.b], writes=[Vx.b], add=True)
                    S.op("dve", lambda e, kv=kv, nt=nt, ti=ti: e.tensor_copy(wi_sb[:nt, ti, :], kv[:nt, 320:324]), reads=[kv.b], writes=[wi_sb.b], add=True)

                cw = sb(p1, [128, 4, 3], F32, "cw"); cb = sb(p1, [128, 4], F32, "cb")
                for cc in range(4):
                    S.dma("sp", lambda e, cc=cc: e.dma_start(out=cw[:, cc, :], in_=conv_w[:, cc * 128:(cc + 1) * 128].rearrange("j p -> p j"), allow_slow_non_contiguous=True), writes=[cw.b], add=True)
                    S.dma("sp", lambda e, cc=cc: e.dma_start(out=cb[:, cc:cc + 1], in_=conv_b[:, cc * 128:(cc + 1) * 128].rearrange("o p -> p o"), allow_slow_non_contiguous=True), writes=[cb.b], add=True)
                Up = sb(p1, [128, 2050], F32, "Up"); Us = sb(p1, [128, 4, 10], F32, "Us")
                Useq = lambda sq: (Up[:, :] if sq == 0 else Us[:, sq - 1, :])
                Ub = lambda sq: (Up.b if sq == 0 else Us.b)
                bgT = sb(p1, [128, NTOK], BF16, "bgT")
                ycv = sb(p1, [128, 2048], F32, "ycv")
                cgs = [sb(p1, [128, 512], F32, "cgs") for _ in range(1)]
                lastU = sb(p1, [128, 4, 10], F32, "lastU")
                wcv = [sb(p1, [128, 8, 3, 128], BF16, "wcv") for _ in range(1)]
                for cc in range(4):
                    wc = wcv[0]
                    for k3 in range(3):
                        S.dma("pool", lambda e, wc=wc, k3=k3, cc=cc: e.dma_start(out=wc[:, :, k3, :], in_=w_in_r[:, :, 1092 + 512 * k3 + 128 * cc:1092 + 512 * k3 + 128 * (cc + 1)]),
                              writes=[wc.b], add=(k3 > 0))
                    S.op("pool", lambda e: e.memset(Up[:, 0:2], 0.0), writes=[Up.b], add=True)
                    for b in range(4):
                        S.dma("sp", lambda e, cc=cc, b=b: e.dma_start(out=Us[:, b, 0:2], in_=sconv[2 * b:2 * b + 2, cc * 128:(cc + 1) * 128].rearrange("t p -> p t"), allow_slow_non_contiguous=True),
                              writes=[Us.b], add=True)
                    for (g0, gn) in GROUPS:
                        pb = psr("fm", [2, 3, 4, 5]); pc = psr("fm", [2, 3, 4, 5]); px = psr("fm", [2, 3, 4, 5])
                        for k3, pp in enumerate((pb, pc, px)):
                            for kc in range(8):
                                S.op("pe", lambda e, kc=kc, pp=pp, k3=k3, wc=wc, g0=g0, gn=gn: e.matmul(pp[:, 0:gn], wc[:, kc, k3, :], h1T[:, kc, g0:g0 + gn], start=(kc == 0), stop=(kc == 7)),
                                     reads=[wc.b, h1T.b], writes=[pp.b], add=(kc > 0))
                        cgx = cgs[0]; rr["cg"] = rr.get("cg", 0) + 1
                        S.op("act", lambda e, cgx=cgx, pc=pc, gn=gn: e.activation(cgx[:, 0:gn], pc[:, 0:gn], AF.Copy), reads=[pc.b], writes=[cgx.b])
                        S.op("act", lambda e, pb=pb, g0=g0, gn=gn: e.activation(bgT[:, g0:g0 + gn], pb[:, 0:gn], AF.Copy), reads=[pb.b], writes=[bgT.b], add=True)
                        if g0 < 2048:
                            S.op("dve", lambda e, cgx=cgx, px=px, g0=g0, gn=gn: e.tensor_tensor(out=Up[:, 2 + g0:2 + g0 + gn], in0=cgx[:, 0:gn], in1=px[:, 0:gn], op=ALU.mult),
                                 reads=[cgx.b, px.b], writes=[Up.b], add=True)
                        else:
                            S.op("dve", lambda e, cgx=cgx, px=px: e.tensor_tensor(out=Us[:, :, 2:10], in0=cgx[:, 0:32].rearrange("p (b t) -> p b t", b=4),
                                                                                 in1=px[:, 0:32].rearrange("p (b t) -> p b t", b=4), op=ALU.mult),
                                 reads=[cgx.b, px.b], writes=[Us.b], add=True)
                    for (sq, T_, c0) in [(0, 2048, 0)] + [(1 + b, 8, 2048 + 8 * b) for b in range(4)]:
                        S.op("dve", lambda e, sq=sq, T_=T_, cc=cc: e.tensor_scalar(ycv[:, 0:T_], Useq(sq)[:, 0:T_], cw[:, cc, 0:1], cb[:, cc:cc + 1], op0=ALU.mult, op1=ALU.add),
                             reads=[Ub(sq), cw.b, cb.b], writes=[ycv.b])
                        for j in (1, 2):
                            S.op("dve", lambda e, sq=sq, T_=T_, cc=cc, j=j: e.scalar_tensor_tensor(out=ycv[:, 0:T_], in0=Useq(sq)[:, j:j + T_], scalar=cw[:, cc, j:j + 1], in1=ycv[:, 0:T_], op0=ALU.mult, op1=ALU.add),
                                 reads=[Ub(sq), cw.b, ycv.b], writes=[ycv.b])
                        S.op("dve", lambda e, T_=T_, cc=cc, c0=c0: e.tensor_tensor(out=o_convT[:, cc, c0:c0 + T_], in0=ycv[:, 0:T_], in1=bgT[:, c0:c0 + T_], op=ALU.mult),
                             reads=[ycv.b, bgT.b], writes=[o_convT.b], add=True)
                        S.op("act", lambda e, sq=sq, T_=T_, cc=cc: e.activation(lastU[:, cc, 2 * sq:2 * sq + 2], Useq(sq)[:, T_:T_ + 2], AF.Copy), reads=[Ub(sq)], writes=[lastU.b], add=True)
                pm = psr("fm", [2, 3, 4, 5])
                for cc in range(4):
                    S.op("pe", lambda e, cc=cc, pm=pm: e.transpose(pm[0:10, cc * 128:(cc + 1) * 128], lastU[:, cc, :], ident[:, :]),
                         reads=[lastU.b, ident.b], writes=[pm.b], add=(cc > 0))
                cst = sb(p1, [10, 512], F32, "cst")
                S.op("act", lambda e, pm=pm: e.activation(cst[:], pm[0:10, :], AF.Copy), reads=[pm.b], writes=[cst.b])
                S.dma("sp", lambda e: e.dma_start(out=conv_o[:, :], in_=cst[:]), reads=[cst.b])
                S.barrier(flush=True)
            S.barrier()

            stop_check(2)
            def attention(st, blocks, nq, qtok0, wi_ap, L, need_topk, S_sc, kiT_ap_fn):
                rt, cntt, lo, W, mid, tt, mn, mx_ = st["rt"], st["cnt"], st["lo"], st["W"], st["mid"], st["tt"], st["mn"], st["mx"]
                junk = st["junk"]
                k0 = 0
                while k0 < L:
                    n = min(512, L - k0)
                    kap, kb = kiT_ap_fn(k0, n)
                    for h in range(4):
                        pm = PS[h % 2]
                        hp = (h % 2) * 64
                        S.op("pe", lambda e, pm=pm, h=h, hp=hp, kap=kap, n=n: e.matmul(pm[:nq, 0:n], qiT[hp:hp + 64, h // 2, qtok0:qtok0 + nq], kap[hp:hp + 64, :], start=True, stop=True),
                             reads=[qiT.b, kb], writes=[pm.b])
                        r = rt[rr.get("rt", 0) % 2]; rr["rt"] = rr.get("rt", 0) + 1
                        S.op("act", lambda e, pm=pm, r=r, n=n: e.activation(r[:nq, 0:n], pm[:nq, 0:n], AF.Relu, scale=IDX_SCALE), reads=[pm.b], writes=[r.b])
                        if h == 0:
                            S.op("dve", lambda e, r=r, n=n, k0=k0: e.tensor_scalar(S_sc[:nq, k0:k0 + n], r[:nq, 0:n], wi_ap[:, 0:1], None, op0=ALU.mult),
                                 reads=[r.b, wi_sb.b], writes=[S_sc.b], add=True)
                        else:
                            S.op("dve", lambda e, r=r, n=n, k0=k0, h=h: e.scalar_tensor_tensor(out=S_sc[:nq, k0:k0 + n], in0=r[:nq, 0:n], scalar=wi_ap[:, h:h + 1], in1=S_sc[:nq, k0:k0 + n], op0=ALU.mult, op1=ALU.add),
                                 reads=[r.b, wi_sb.b, S_sc.b], writes=[S_sc.b])
                    k0 += n
                stop_check(2.1)
                nk0 = blocks[-1]["nk"]
                if need_topk:
                    S.op("dve", lambda e: e.tensor_reduce(out=mx_[:nq, :], in_=S_sc[:nq, 0:L], axis=AX.X, op=ALU.max), reads=[S_sc.b], writes=[mx_.b])
                    S.op("dve", lambda e: e.tensor_reduce(out=mn[:nq, :], in_=S_sc[:nq, 0:L], axis=AX.X, op=ALU.min), reads=[S_sc.b], writes=[mn.b])
                    S.op("dve", lambda e: e.tensor_scalar(lo[:nq, :], mn[:nq, :], -1.0, None, op0=ALU.add), reads=[mn.b], writes=[lo.b])
                    S.op("dve", lambda e: e.scalar_tensor_tensor(out=W[:nq, :], in0=mx_[:nq, :], scalar=2.0, in1=mn[:nq, :], op0=ALU.add, op1=ALU.subtract),
                         reads=[mx_.b, mn.b], writes=[W.b])
                S.op("dve", lambda e: e.tensor_tensor(out=S_sc[:nq, L - nk0:L], in0=S_sc[:nq, L - nk0:L], in1=negA[:nq, :nk0], op=ALU.add),
                     reads=[S_sc.b, negA.b], writes=[S_sc.b])
                if need_topk:
                    S.op("dve", lambda e: e.memset(cntt[:], 0.0), writes=[cntt.b])
                    for it in range(NBIS):
                        ck_ = 2.0 ** -(it + 1)
                        S.op("dve", lambda e, ck_=ck_: e.scalar_tensor_tensor(out=mid[:nq, :], in0=W[:nq, :], scalar=ck_, in1=lo[:nq, :], op0=ALU.mult, op1=ALU.add),
                             reads=[W.b, lo.b], writes=[mid.b])
                        S.op("dve", lambda e, it=it: e.tensor_scalar(junk[:nq, 0:L], S_sc[:nq, 0:L], mid[:nq, 0:1], 0.0, op0=ALU.is_gt, op1=ALU.add, accum_out=cntt[:nq, it:it + 1]),
                             reads=[S_sc.b, mid.b, cntt.b], writes=[junk.b, cntt.b])
                        S.op("dve", lambda e, it=it, ck_=ck_: e.tensor_scalar(tt[:nq, :], cntt[:nq, it:it + 1], 256.0, ck_, op0=ALU.is_ge, op1=ALU.mult),
                             reads=[cntt.b], writes=[tt.b])
                        S.op("dve", lambda e: e.scalar_tensor_tensor(out=lo[:nq, :], in0=tt[:nq, :], scalar=W[:nq, 0:1], in1=lo[:nq, :], op0=ALU.mult, op1=ALU.add),
                             reads=[tt.b, W.b, lo.b], writes=[lo.b])
                    S.op("dve", lambda e: e.tensor_scalar(S_sc[:nq, 0:L], S_sc[:nq, 0:L], lo[:nq, 0:1], None, op0=ALU.is_gt), reads=[S_sc.b, lo.b], writes=[S_sc.b])
                else:
                    S.op("dve", lambda e: e.tensor_scalar(S_sc[:nq, 0:L], S_sc[:nq, 0:L], -10000.0, None, op0=ALU.is_gt), reads=[S_sc.b], writes=[S_sc.b])
                stop_check(2.2)
                oacc = [PS[6], PS[7]]
                kpos = 0
                nb = len(blocks)
                for bi, blk in enumerate(blocks):
                    nk = blk["nk"]
                    if blk.get("prep"):
                        blk["prep"]()
                    pmk = psr("mk", [2])
                    S.op("pe", lambda e, pmk=pmk, kpos=kpos, nk=nk: e.transpose(pmk[:nk, 0:nq], S_sc[:nq, kpos:kpos + nk], ident[:nq, :nq]),
                         reads=[S_sc.b, ident.b], writes=[pmk.b])
                    stop_check(2.22)
                    addm = st["addm"][bi % 2]
                    S.op("dve", lambda e, pmk=pmk, addm=addm, nk=nk: e.tensor_scalar(addm[:nk, 0:nq], pmk[:nk, 0:nq], -1.0, -NEG, op0=ALU.add, op1=ALU.mult),
                         reads=[pmk.b], writes=[addm.b])
                    stop_check(2.25)
                    pqE = PS[4]; pqO = PS[5]
                    for h in (0, 2, 4, 6, 1, 3, 5, 7):
                        g = h // 4
                        hp = (h % 2) * 64
                        pq = pqE if hp == 0 else pqO
                        col = (h // 2) * nq
                        kap, kb = blk["KT"](g)
                        S.op("pe", lambda e, pq=pq, col=col, h=h, hp=hp, kap=kap, nk=nk: e.matmul(pq[:nk, col:col + nq], kap[hp:hp + 64, :], qT[hp:hp + 64, h // 2, qtok0:qtok0 + nq], start=True, stop=True),
                             reads=[kb, qT.b], writes=[pq.b], add=True)
                    stop_check(2.3)
                    for g in range(2):
                        hs = (4 * g, 4 * g + 2, 4 * g + 1, 4 * g + 3)
                        lg = st["lg"][rr.get("lg", 0) % 2]; rr["lg"] = rr.get("lg", 0) + 1
                        lgv = lg[:nk, 0:4 * nq].rearrange("p (s q) -> p s q", s=4)
                        for half, pq in enumerate((pqE, pqO)):
                            S.op("dve", lambda e, pq=pq, lgv=lgv, half=half, g=g, addm=addm, nk=nk: e.scalar_tensor_tensor(
                                out=lgv[:, 2 * half:2 * half + 2, :], in0=pq[:nk, 2 * g * nq:(2 * g + 2) * nq].rearrange("p (s q) -> p s q", s=2),
                                scalar=ATTN_SCALE, in1=addm[:nk, 0:nq].unsqueeze(1).to_broadcast([nk, 2, nq]), op0=ALU.mult, op1=ALU.add),
                                reads=[pq.b, addm.b], writes=[lg.b], add=True)
                        stop_check(2.32)
                        if blk["kind"] != "far":
                            ci = 0 if blk["kind"] == "near0" else 1
                            for sl in range(4):
                                S.op("pool", lambda e, lg=lg, sl=sl, hh_=hs[sl], ci=ci, nk=nk: e.tensor_tensor(out=lg[:nk, sl * nq:(sl + 1) * nq], in0=lg[:nk, sl * nq:(sl + 1) * nq], in1=Bm[:nk, hh_, ci, 0:nq], op=ALU.add),
                                     reads=[lg.b, Bm.b], writes=[lg.b])
                        stop_check(2.34)
                        pT = st["pT"][rr.get("pT", 0) % 2]; rr["pT"] = rr.get("pT", 0) + 1
                        S.op("act", lambda e, lg=lg, pT=pT, nk=nk: e.activation(pT[:nk, 0:4 * nq], lg[:nk, 0:4 * nq], AF.Exp), reads=[lg.b], writes=[pT.b])
                        stop_check(2.36)
                        vap, vb = blk["V"](g)
                        S.op("pe", lambda e, g=g, pT=pT, vap=vap, nk=nk, bi=bi: e.matmul(oacc[g][0:65, 0:4 * nq], vap, pT[:nk, 0:4 * nq], start=(bi == 0), stop=(bi == nb - 1)),
                             reads=[vb, pT.b], writes=[oacc[g].b], add=(bi > 0))
                    kpos += nk
                stop_check(2.4)
                for g in range(2):
                    den = st["den"]
                    S.op("act", lambda e, g=g: e.activation(den[64:65, 0:4 * nq], oacc[g][64:65, 0:4 * nq], AF.Copy), reads=[oacc[g].b], writes=[den.b])
                    pb_ = psr("mk", [2])
                    S.op("pe", lambda e, pb_=pb_: e.matmul(pb_[0:64, 0:4 * nq], ones_f[64:65, 0:64], den[64:65, 0:4 * nq], start=True, stop=True),
                         reads=[ones_f.b, den.b], writes=[pb_.b])
                    rec = st["rec"]
                    S.op("dve", lambda e, pb_=pb_: e.reciprocal(rec[0:64, 0:4 * nq], pb_[0:64, 0:4 * nq]), reads=[pb_.b], writes=[rec.b])
                    for half in range(2):
                        S.op("dve", lambda e, g=g, half=half: e.tensor_tensor(out=o_attnT[:, 4 * g + half:4 * g + 4:2, qtok0:qtok0 + nq],
                                                                              in0=oacc[g][0:64, 0:4 * nq].rearrange("p (h q) -> p h q", h=4)[:, 2 * half:2 * half + 2, :],
                                                                              in1=rec[0:64, 0:4 * nq].rearrange("p (h q) -> p h q", h=4)[:, 2 * half:2 * half + 2, :], op=ALU.mult),
                             reads=[oacc[g].b, rec.b], writes=[o_attnT.b], add=True)

            with ExitStack() as p3:
                st = dict(
                    rt=[sb(p3, [128, 512], F32, "rt") for _ in range(2)],
                    cnt=sb(p3, [128, NBIS], F32, "cnt"), lo=sb(p3, [128, 1], F32, "lo"), W=sb(p3, [128, 1], F32, "W"),
                    mid=sb(p3, [128, 1], F32, "mid"), tt=sb(p3, [128, 1], F32, "tt"), mn=sb(p3, [128, 1], F32, "mn"), mx=sb(p3, [128, 1], F32, "mx"),
                    addm=[sb(p3, [128, 128], F32, "addm") for _ in range(2)],
                    lg=[sb(p3, [128, 512], F32, "lg") for _ in range(2)],
                    pT=[sb(p3, [128, 512], BF16, "pT") for _ in range(2)],
                    den=sb(p3, [65, 512], F32, "den"), rec=sb(p3, [64, 512], F32, "rec"),
                )
                with ExitStack() as p3a:
                    S_p = sb(p3a, [128, 2048], F32, "S_p")
                    st["junk"] = sb(p3a, [128, 2048], BF16, "junkp")
                    ust = [sb(p3a, [128, D], F32, "ust") for _ in range(2)]
                    utb = [sb(p3a, [128, 8, 128], BF16, "utb") for _ in range(2)]
                    vst = [sb(p3a, [128, D], BF16, "vst") for _ in range(2)]

                    def prep_chunk(a):
                        us = ust[a % 2]; ub = utb[a % 2]; vs_ = vst[a % 2]
                        S.dma("sp", lambda e, us=us, a=a: e.dma_start(out=us[:], in_=peer_u[a * 128:(a + 1) * 128, :]), writes=[us.b])
                        for hb in range(2):
                            pm = PS[3]
                            for k4 in range(4):
                                kc = hb * 4 + k4
                                S.op("pe", lambda e, kc=kc, k4=k4, pm=pm, us=us: e.transpose(pm[:, k4 * 128:(k4 + 1) * 128], us[:, kc * 128:(kc + 1) * 128], ident[:]),
                                     reads=[us.b, ident.b], writes=[pm.b], add=(k4 > 0))
                            evac(ub[:, hb * 4:(hb + 1) * 4, :].rearrange("p k e -> p (k e)"), ub.b, pm[:, :], pm.b, engs=("act",))
                        S.dma("sp", lambda e, ub=ub, a=a: e.dma_start(out=ut_scr[a], in_=ub[:].rearrange("p k e -> p (k e)")), reads=[ub.b], writes=[b_utscr], sembuf=ub.b, add=True)
                        S.dma("pool", lambda e, vs_=vs_, a=a: e.dma_start(out=vs_[:], in_=peer_v[a * 128:(a + 1) * 128, :]), writes=[vs_.b])
                        S.dma("sp", lambda e, vs_=vs_, a=a: e.dma_start(out=v_scr[a * 128:(a + 1) * 128, :], in_=vs_[:]), reads=[vs_.b], writes=[b_vscr], sembuf=vs_.b, add=True)
                    prep_done = [0]
                    for t in DBG['tblocks']:
                        blocks = []
                        for j in range(t + 1):
                            kind = "near0" if j == t else ("near1" if j == t - 1 else "far")
                            blocks.append(dict(nk=128, kind=kind,
                                               KT=lambda g, j=j: (kTd[:, g, j * 128:(j + 1) * 128], kTd.b),
                                               V=lambda g, j=j: (Vx[:, j, g, :], Vx.b)))
                        attention(st, blocks, 128, 128 * t, wi_sb[:, t, :], 128 * (t + 1), t >= 2, S_p,
                                  lambda k0, n: (kiTd[:, k0:k0 + n], kiTd.b))
                        for _ in range(8):
                            if prep_done[0] < DBG['prep']:
                                prep_chunk(prep_done[0]); prep_done[0] += 1
                    while prep_done[0] < DBG['prep']:
                        prep_chunk(prep_done[0]); prep_done[0] += 1
                    S.barrier(flush=True)
                S.barrier()
                stop_check(2.5)
                with ExitStack() as p3b:
                    S_s = sb(p3b, [8, 8208], F32, "S_s")
                    st["junk"] = sb(p3b, [8, 8208], BF16, "junks")
                    kiT_s = sb(p3b, [128, 8200], BF16, "kiT_s")
                    ptb = sb(p3b, [128, 64], I32, "ptb")
                    idx = sb(p3b, [128, 64], I32, "idx")
                    kst = [sb(p3b, [128, 64], F32, "kst") for _ in range(3)]
                    kstd = [sb(p3b, [128, 2, 64], F32, "kstd") for _ in range(3)]
                    Kdd = [sb(p3b, [128, 2, 2, 64], F32, "Kdd") for _ in range(3)]
                    Kst = [sb(p3b, [128, 128], F32, "Kst") for _ in range(3)]
                    Vst = [sb(p3b, [128, 128], F32, "Vst") for _ in range(3)]
                    KTb = [sb(p3b, [128, 2, 128], BF16, "KTb") for _ in range(3)]
                    Vxb = [sb(p3b, [128, 2, 65], BF16, "Vxb") for _ in range(3)]
                    for vx in Vxb:
                        S.op("pool", lambda e, vx=vx: e.memset(vx[:], 1.0), writes=[vx.b])
                    for b in DBG['batches']:
                        src = bass.AP(ptab.tensor, b * 64, [[0, 128], [1, 64]])
                        S.dma("sp", lambda e, src=src: e.dma_start(out=ptb[:], in_=src), writes=[ptb.b])
                        S.op("dve", lambda e: e.tensor_scalar(idx[:], ptb[:], 128.0, iotap[:, 0:1], op0=ALU.mult, op1=ALU.add), reads=[ptb.b, iotap.b], writes=[idx.b])
                        for pg in range(64):
                            ks = kst[pg % 3]
                            S.dma("pool", lambda e, ks=ks, pg=pg: e.indirect_dma_start(out=ks[:], out_offset=None, in_=cki[:, :], in_offset=bass.IndirectOffsetOnAxis(ap=idx[:, pg:pg + 1], axis=0)),
                                  reads=[idx.b], writes=[ks.b])
                            ksd = kstd[pg % 3]
                            S.op("dve", lambda e, ks=ks, ksd=ksd: e.tensor_copy(ksd[:, :, :], ks[:].unsqueeze(1).to_broadcast([128, 2, 64])), reads=[ks.b], writes=[ksd.b])
                            pm = psr("ix", [0, 1])
                            S.op("pe", lambda e, pm=pm, ksd=ksd: e.transpose(pm[:, 0:128], ksd[:].rearrange("p r d -> p (r d)"), ident[:]),
                                 reads=[ksd.b, ident.b], writes=[pm.b])
                            evac(kiT_s[:, pg * 128:(pg + 1) * 128], kiT_s.b, pm[:, 0:128], pm.b)
                        S.op("dve", lambda e, b=b: e.tensor_copy(kiT_s[:, 8192:8200], kiTd[:, 2048 + 8 * b:2056 + 8 * b]), reads=[kiTd.b], writes=[kiT_s.b], add=True)
                        blocks = []
                        for pg in range(64):
                            def prep(pg=pg):
                                Ks = Kst[pg % 3]; Vs = Vst[pg % 3]; KT_ = KTb[pg % 3]; Vb = Vxb[pg % 3]
                                S.dma("pool", lambda e: e.indirect_dma_start(out=Ks[:], out_offset=None, in_=ck[:, :], in_offset=bass.IndirectOffsetOnAxis(ap=idx[:, pg:pg + 1], axis=0)),
                                      reads=[idx.b], writes=[Ks.b])
                                S.dma("pool", lambda e: e.indirect_dma_start(out=Vs[:], out_offset=None, in_=cvv[:, :], in_offset=bass.IndirectOffsetOnAxis(ap=idx[:, pg:pg + 1], axis=0)),
                                      reads=[idx.b], writes=[Vs.b])
                                Kd = Kdd[pg % 3]
                                S.op("dve", lambda e: e.tensor_copy(Kd[:, :, :, :], Ks[:].rearrange("p (g d) -> p g d", g=2).unsqueeze(2).to_broadcast([128, 2, 2, 64])), reads=[Ks.b], writes=[Kd.b])
                                for g in range(2):
                                    pm = psr("ix", [0, 1])
                                    S.op("pe", lambda e, pm=pm, g=g: e.transpose(pm[:, 0:128], Kd[:, g, :, :].rearrange("p r d -> p (r d)"), ident[:]),
                                         reads=[Kd.b, ident.b], writes=[pm.b])
                                    evac(KT_[:, g, :], KT_.b, pm[:, 0:128], pm.b)
                                S.op("dve", lambda e: e.tensor_copy(Vb[:, :, 0:64], Vs[:].rearrange("p (g d) -> p g d", g=2)), reads=[Vs.b], writes=[Vb.b], add=True)
                            blocks.append(dict(nk=128, kind=("near1" if pg == 63 else "far"), prep=prep,
                                               KT=lambda g, pg=pg: (KTb[pg % 3][:, g, :], KTb[pg % 3].b),
                                               V=lambda g, pg=pg: (Vxb[pg % 3][:, g, :], Vxb[pg % 3].b)))
                        blocks.append(dict(nk=8, kind="near0",
                                           KT=lambda g, b=b: (kTd[:, g, 2048 + 8 * b:2056 + 8 * b], kTd.b),
                                           V=lambda g, b=b: (Vx[0:8, 16 + b, g, :], Vx.b)))
                        attention(st, blocks, 8, 2048 + 8 * b, wi_sb[0:8, 16 + b, :], 8200, True, S_s,
                                  lambda k0, n: (kiT_s[:, k0:k0 + n], kiT_s.b))
                    S.barrier(flush=True)
            att_es.close()
            S.barrier()

            stop_check(3)
            if DBG.get('ydump') == 'oc':
                for cc in range(4):
                    dsto = bass.AP(y_p.tensor, cc * 128 * 2048, [[2048, 128], [1, 2048]])
                    S.dma("pool", lambda e, cc=cc, dsto=dsto: e.dma_start(out=dsto, in_=o_convT[:, cc, 0:2048]), reads=[o_convT.b])
                for h in range(8):
                    dsto = bass.AP(y_p.tensor, (512 + h * 64) * 2048, [[2048, 64], [1, 2048]])
                    S.dma("pool", lambda e, h=h, dsto=dsto: e.dma_start(out=dsto, in_=o_attnT[:, h, 0:2048]), reads=[o_attnT.b])
                S.barrier()
            stop_check(3.5)
            def bc_rows(dst, nrows, lhsT_ap, lhs_b, rhs_fn, rhs_b):
                for hf in range(2):
                    pm = psr("bc", [0, 1])
                    S.op("pe", lambda e, pm=pm, hf=hf: e.matmul(pm[:nrows, :], lhsT_ap, rhs_fn(hf), start=True, stop=True), reads=[lhs_b, rhs_b], writes=[pm.b])
                    S.op("act", lambda e, pm=pm, hf=hf: e.activation(dst[:nrows, hf * 512:(hf + 1) * 512], pm[:nrows, :], AF.Copy), reads=[pm.b], writes=[dst.b], add=True)

            def layer_norm(stk, z, nt, lg_bc, lb_bc, out_t):
                s1 = stk["s1"]; s2 = stk["s2"]; jk = stk["jk"]
                S.op("dve", lambda e: e.memset(s1[:], 0.0), writes=[s1.b])
                S.op("dve", lambda e: e.memset(s2[:], 0.0), writes=[s2.b])
                S.op("act", lambda e: e.activation(jk[:nt, :], z[:nt, :], AF.Identity, accum_out=s1[:nt, 0:1]), reads=[z.b, s1.b], writes=[jk.b, s1.b])
                S.op("act", lambda e: e.activation(jk[:nt, :], z[:nt, :], AF.Square, accum_out=s2[:nt, 0:1]), reads=[z.b, s2.b], writes=[jk.b, s2.b])
                mu = stk["mu"]; var = stk["var"]; rstd = stk["rstd"]
                S.op("dve", lambda e: e.tensor_scalar(mu[:nt, :], s1[:nt, :], 1.0 / D, None, op0=ALU.mult), reads=[s1.b], writes=[mu.b])
                S.op("dve", lambda e: e.tensor_tensor(out=var[:nt, :], in0=mu[:nt, :], in1=mu[:nt, :], op=ALU.mult), reads=[mu.b], writes=[var.b])
                S.op("dve", lambda e: e.scalar_tensor_tensor(out=var[:nt, :], in0=s2[:nt, :], scalar=1.0 / D, in1=var[:nt, :], op0=ALU.mult, op1=ALU.subtract),
                     reads=[s2.b, var.b], writes=[var.b])
                S.op("dve", lambda e: e.tensor_scalar(var[:nt, :], var[:nt, :], LN_EPS, None, op0=ALU.add), reads=[var.b], writes=[var.b])
                S.op("act", lambda e: e.activation(rstd[:nt, :], var[:nt, :], AF.Sqrt), reads=[var.b], writes=[rstd.b])
                S.op("dve", lambda e: e.reciprocal(rstd[:nt, :], rstd[:nt, :]), reads=[rstd.b], writes=[rstd.b])
                S.op("dve", lambda e: e.tensor_scalar(out_t[:nt, :], z[:nt, :], mu[:nt, 0:1], rstd[:nt, 0:1], op0=ALU.subtract, op1=ALU.mult),
                     reads=[z.b, mu.b, rstd.b], writes=[out_t.b])
                S.op("dve", lambda e: e.tensor_tensor(out=out_t[:nt, :], in0=out_t[:nt, :], in1=lg_bc[:nt, :], op=ALU.mult), reads=[out_t.b, lg_bc.b], writes=[out_t.b])
                S.op("dve", lambda e: e.tensor_tensor(out=out_t[:nt, :], in0=out_t[:nt, :], in1=lb_bc[:nt, :], op=ALU.add), reads=[out_t.b, lb_bc.b], writes=[out_t.b])

            LNT = [(128 * t, 128) for t in range(16)] + [(2048, 32)]
            SAMP_SEGS = [(8 * b, 8, 1 + b) for b in range(4)]

            with ExitStack() as p4:
                lnrow = sb(p4, [1, 2, D], F32, "lnrow")
                S.dma("sp", lambda e: e.dma_start(out=lnrow[:, 0, :], in_=ln1_g[:, :]), writes=[lnrow.b])
                S.dma("sp", lambda e: e.dma_start(out=lnrow[:, 1, :], in_=ln1_b[:, :]), writes=[lnrow.b], add=True)
                lg1 = sb(p4, [128, D], F32, "lg1"); lb1 = sb(p4, [128, D], F32, "lb1")
                g1p = sb(p4, [128, D], F32, "g1p"); g1s = sb(p4, [32, D], F32, "g1s")
                bc_rows(lg1, 128, ones_f[0:1, :], ones_f.b, lambda hf: lnrow[0:1, 0, hf * 512:(hf + 1) * 512], lnrow.b)
                bc_rows(lb1, 128, ones_f[0:1, :], ones_f.b, lambda hf: lnrow[0:1, 1, hf * 512:(hf + 1) * 512], lnrow.b)
                bc_rows(g1p, 128, selp_sb[:, :], selp_sb.b, lambda hf: modrow_g[:, 0, hf * 512:(hf + 1) * 512], modrow_g.b)
                bc_rows(g1s, 32, sels_sb[:, :], sels_sb.b, lambda hf: modrow_g[:, 0, hf * 512:(hf + 1) * 512], modrow_g.b)
                woa = sb(p4, [64, 8, D], BF16, "woa"); woc = sb(p4, [128, 4, D], BF16, "woc"); wout = sb(p4, [128, 8, D], BF16, "wout")
                S.dma("pool", lambda e: e.dma_start(out=woa[:], in_=w_o_attn.rearrange("(h p) n -> p h n", p=64)), writes=[woa.b])
                S.dma("pool", lambda e: e.dma_start(out=woc[:], in_=w_o_conv.rearrange("(c p) n -> p c n", p=128)), writes=[woc.b])
                S.dma("pool", lambda e: e.dma_start(out=wout[:], in_=w_out.rearrange("(c p) n -> p c n", p=128)), writes=[wout.b])
                wgs = [sb(p4, [128, 8, 2, 128], BF16, "wg") for _ in range(2)]
                h1g = sb(p4, [128, 8, 512], BF16, "h1g")
                mT = sb(p4, [128, 8, 512], BF16, "mT")
                xts = [sb(p4, [128, D], F32, "xt4") for _ in range(2)]
                sg = [sb(p4, [128, 512], F32, "sg") for _ in range(2)]
                m1 = [sb(p4, [128, 512], F32, "m1") for _ in range(2)]
                zt = [sb(p4, [128, D], F32, "zt") for _ in range(1)]
                x1t = [sb(p4, [128, D], F32, "x1t") for _ in range(2)]
                lnst = dict(s1=sb(p4, [128, 1], F32, "s1"), s2=sb(p4, [128, 1], F32, "s2"), jk=sb(p4, [128, D], BF16, "jk"),
                            mu=sb(p4, [128, 1], F32, "mu"), var=sb(p4, [128, 1], F32, "var"), rstd=sb(p4, [128, 1], F32, "rstd"))
                for (g0, gn) in GROUPS:
                    if g0 < 2048:
                        tl = [(g0 + 128 * i, 128) for i in range(4)]
                        for (tok0, nt) in tl:
                            load_h_T(xts, h1g, h1g.b, tok0 - g0, tok0, nt, 0, 0, 8, xp[tok0:tok0 + nt, :])
                    else:
                        tl = [(2048, 32)]
                        load_h_T(xts, h1g, h1g.b, 0, 2048, 32, SAMP_SEGS, 0, 8, xs[:, :])
                    for j in range(8):
                        wg = wgs[j % 2]
                        for k2 in range(2):
                            S.dma("pool", lambda e, wg=wg, k2=k2, j=j: e.dma_start(out=wg[:, :, k2, :], in_=w_in_r[:, :, 2628 + 1024 * k2 + 128 * j:2628 + 1024 * k2 + 128 * (j + 1)]),
                                  writes=[wg.b], add=(k2 > 0))
                        pa1 = psr("mg", [2, 3, 4, 5]); pa2 = psr("mg", [2, 3, 4, 5]); pga = psr("mg", [2, 3, 4, 5]); pgb = psr("mg", [2, 3, 4, 5])
                        for h in range(8):
                            S.op("pe", lambda e, h=h, j=j, pa1=pa1, g0=g0, gn=gn: e.matmul(pa1[:, 0:gn], woa[:, h, j * 128:(j + 1) * 128], o_attnT[:, h, g0:g0 + gn], start=(h == 0), stop=(h == 7)),
                                 reads=[woa.b, o_attnT.b], writes=[pa1.b], add=(h > 0))
                        for cc in range(4):
                            S.op("pe", lambda e, cc=cc, j=j, pa2=pa2, g0=g0, gn=gn: e.matmul(pa2[:, 0:gn], woc[:, cc, j * 128:(j + 1) * 128], o_convT[:, cc, g0:g0 + gn], start=(cc == 0), stop=(cc == 3)),
                                 reads=[woc.b, o_convT.b], writes=[pa2.b], add=(cc > 0))
                        for k2, pg_ in enumerate((pga, pgb)):
                            for kc in range(8):
                                S.op("pe", lambda e, kc=kc, k2=k2, pg_=pg_, wg=wg, gn=gn: e.matmul(pg_[:, 0:gn], wg[:, kc, k2, :], h1g[:, kc, 0:gn], start=(kc == 0), stop=(kc == 7)),
                                     reads=[wg.b, h1g.b], writes=[pg_.b], add=(kc > 0))
                        sa = sg[0]; sb_ = sg[1]; ma = m1[0]; mb = m1[1]
                        S.op("act", lambda e, pga=pga, sa=sa, gn=gn: e.activation(sa[:, 0:gn], pga[:, 0:gn], AF.Sigmoid), reads=[pga.b], writes=[sa.b])
                        S.op("act", lambda e, pgb=pgb, sb_=sb_, gn=gn: e.activation(sb_[:, 0:gn], pgb[:, 0:gn], AF.Sigmoid), reads=[pgb.b], writes=[sb_.b])
                        S.op("dve", lambda e, sa=sa, pa1=pa1, ma=ma, gn=gn: e.tensor_tensor(out=ma[:, 0:gn], in0=sa[:, 0:gn], in1=pa1[:, 0:gn], op=ALU.mult), reads=[sa.b, pa1.b], writes=[ma.b])
                        S.op("dve", lambda e, sb_=sb_, pa2=pa2, mb=mb, gn=gn: e.tensor_tensor(out=mb[:, 0:gn], in0=sb_[:, 0:gn], in1=pa2[:, 0:gn], op=ALU.mult), reads=[sb_.b, pa2.b], writes=[mb.b])
                        S.op("dve", lambda e, ma=ma, mb=mb, j=j, gn=gn: e.tensor_tensor(out=mT[:, j, 0:gn], in0=ma[:, 0:gn], in1=mb[:, 0:gn], op=ALU.add), reads=[ma.b, mb.b], writes=[mT.b], add=True)
                    for (tok0, nt) in tl:
                        c0 = tok0 - g0
                        xt = xts[rr.get("xt", 0) % 2]; rr["xt"] = rr.get("xt", 0) + 1
                        src = xp[tok0:tok0 + nt, :] if tok0 < 2048 else xs[:, :]
                        S.dma("sp", lambda e, xt=xt, nt=nt, src=src: e.dma_start(out=xt[:nt, :], in_=src), writes=[xt.b])
                        z = zt[0]; rr["zt"] = rr.get("zt", 0) + 1
                        gbc = g1p if tok0 < 2048 else g1s
                        for hf in range(2):
                            po = psr("mo", [6, 7])
                            for kc in range(8):
                                S.op("pe", lambda e, kc=kc, po=po, hf=hf, c0=c0, nt=nt: e.matmul(po[:nt, :], mT[:, kc, c0:c0 + nt], wout[:, kc, hf * 512:(hf + 1) * 512], start=(kc == 0), stop=(kc == 7)),
                                     reads=[mT.b, wout.b], writes=[po.b], add=(kc > 0))
                            S.op("dve", lambda e, po=po, z=z, hf=hf, nt=nt, gbc=gbc: e.tensor_tensor(out=z[:nt, hf * 512:(hf + 1) * 512], in0=po[:nt, :], in1=gbc[:nt, hf * 512:(hf + 1) * 512], op=ALU.mult),
                                 reads=[po.b, gbc.b], writes=[z.b], add=True)
                        S.op("dve", lambda e, z=z, xt=xt, nt=nt: e.scalar_tensor_tensor(out=z[:nt, :], in0=xt[:nt, :], scalar=ALPHA, in1=z[:nt, :], op0=ALU.mult, op1=ALU.add),
                             reads=[xt.b, z.b], writes=[z.b])
                        x1 = x1t[rr.get("x1", 0) % 2]; rr["x1"] = rr.get("x1", 0) + 1
                        if DBG.get('ydump') == 'z' and tok0 == 0:
                            S.dma("sp", lambda e, z=z: e.dma_start(out=y_p[0:128, :], in_=z[:, :]), reads=[z.b])
                            S.dma("sp", lambda e: e.dma_start(out=y_p[128:256, :], in_=g1p[:, :]), reads=[g1p.b])
                            S.dma("sp", lambda e: e.dma_start(out=y_p[256:384, :], in_=lg1[:, :]), reads=[lg1.b])
                            S.dma("sp", lambda e: e.dma_start(out=y_p[512:640, :], in_=lb1[:, :]), reads=[lb1.b])
                            S.dma("sp", lambda e, xt=xt: e.dma_start(out=y_p[640:768, :], in_=xt[:, :]), reads=[xt.b])
                        layer_norm(lnst, z, nt, lg1, lb1, x1)
                        if DBG.get('ydump') == 'z' and tok0 == 0:
                            S.dma("sp", lambda e, x1=x1: e.dma_start(out=y_p[768:896, :], in_=x1[:, :]), reads=[x1.b])
                        S.dma("sp", lambda e, x1=x1, nt=nt, tok0=tok0: e.dma_start(out=x1_scr[tok0:tok0 + nt, :], in_=x1[:nt, :]), reads=[x1.b], writes=[b_x1scr], sembuf=x1.b, add=True)
                        if DBG.get('ydump') == 'x1':
                            dstx = y_p[tok0:tok0 + nt, :] if tok0 < 2048 else y_s[:, :]
                            S.dma("sp", lambda e, x1=x1, nt=nt, dstx=dstx: e.dma_start(out=dstx, in_=x1[:nt, :]), reads=[x1.b])
                S.barrier(flush=True)
            act_es.close()
            S.barrier()

            stop_check(4)
            with ExitStack() as p5:
                lnrow = sb(p5, [1, 2, D], F32, "lnrow2")
                S.dma("sp", lambda e: e.dma_start(out=lnrow[:, 0, :], in_=ln2_g[:, :]), writes=[lnrow.b])
                S.dma("sp", lambda e: e.dma_start(out=lnrow[:, 1, :], in_=ln2_b[:, :]), writes=[lnrow.b], add=True)
                lg2 = sb(p5, [128, D], F32, "lg2"); lb2 = sb(p5, [128, D], F32, "lb2")
                g2p = sb(p5, [128, D], F32, "g2p"); g2s = sb(p5, [32, D], F32, "g2s")
                bc_rows(lg2, 128, ones_f[0:1, :], ones_f.b, lambda hf: lnrow[0:1, 0, hf * 512:(hf + 1) * 512], lnrow.b)
                bc_rows(lb2, 128, ones_f[0:1, :], ones_f.b, lambda hf: lnrow[0:1, 1, hf * 512:(hf + 1) * 512], lnrow.b)
                bc_rows(g2p, 128, selp_sb[:, :], selp_sb.b, lambda hf: modrow_g[:, 1, hf * 512:(hf + 1) * 512], modrow_g.b)
                bc_rows(g2s, 32, sels_sb[:, :], sels_sb.b, lambda hf: modrow_g[:, 1, hf * 512:(hf + 1) * 512], modrow_g.b)
                wpq = sb(p5, [128, 8, D], BF16, "wpq")
                S.dma("pool", lambda e: e.dma_start(out=wpq[:], in_=peer_wq.rearrange("(c p) n -> p c n", p=128)), writes=[wpq.b])
                iotaA = sb(p5, [128, 32, 128], BF16, "iotaA")
                S.op("pool", lambda e: e.iota(iotaA[:], [[0, 32], [1, 128]], base=0, channel_multiplier=0, allow_small_or_imprecise_dtypes=True), writes=[iotaA.b])
                x1ts = [sb(p5, [128, D], F32, "x1l") for _ in range(1)]
                h2T = sb(p5, [128, 8, 128], BF16, "h2T")
                qpT = sb(p5, [128, 8, 128], BF16, "qpT")
                Spe = sb(p5, [128, 8, 256], F32, "Spe")
                v12 = sb(p5, [128, 8, 2, 16], F32, "v12")
                i12 = sb(p5, [128, 8, 2, 16], U32, "i12")
                i12f = sb(p5, [128, 8, 2, 16], F32, "i12f")
                wk = sb(p5, [128, 256], F32, "wk")
                cand = sb(p5, [128, 8, 256], F32, "cand")
                sv = sb(p5, [128, 8, 16], F32, "sv")
                si = sb(p5, [128, 8, 16], U32, "si")
                sij = sb(p5, [128, 2, 8, 16], U32, "sij")
                sijf = sb(p5, [128, 2, 8, 16], F32, "sijf")
                eq = sb(p5, [128, 16, 16], F32, "eq")
                abw = sb(p5, [128, 3, 128], F32, "abw")
                zs = sb(p5, [128, 8], F32, "zs")
                abwT = sb(p5, [128, 3, 128], F32, "abwT")
                OA = sb(p5, [128, 32, 128], BF16, "OA"); OB = sb(p5, [128, 32, 128], BF16, "OB")
                GT = sb(p5, [128, 128, 128], BF16, "GT")
                utb = [sb(p5, [128, 4, 1024], BF16, "utl") for _ in range(2)]
                vtb = [sb(p5, [128, 4, 1024], BF16, "vtl") for _ in range(2)]
                xs_ = [sb(p5, [128, 512], F32, "gx") for _ in range(2)]
                us_ = [sb(p5, [128, 512], F32, "gu") for _ in range(2)]
                ws_ = [sb(p5, [128, 512], F32, "gw") for _ in range(2)]
                PTs = [sb(p5, [128, 512], BF16, "PT") for _ in range(2)]
                zt = [sb(p5, [128, D], F32, "zt5") for _ in range(1)]
                yt = [sb(p5, [128, D], F32, "yt") for _ in range(1)]
                lnst = dict(s1=sb(p5, [128, 1], F32, "s1"), s2=sb(p5, [128, 1], F32, "s2"), jk=sb(p5, [128, D], BF16, "jk"),
                            mu=sb(p5, [128, 1], F32, "mu"), var=sb(p5, [128, 1], F32, "var"), rstd=sb(p5, [128, 1], F32, "rstd"))
                ut_r = ut_scr.rearrange("(a4 c) p k -> a4 p c k", c=4)
                v_r = v_scr.rearrange("(a4 c p) d -> a4 p c d", c=4, p=128)
                for (tok0, nt) in LNT:
                    x1l = x1ts[0]; rr["x1l"] = rr.get("x1l", 0) + 1
                    segs = 0 if tok0 < 2048 else SAMP_SEGS
                    load_h_T([x1l], h2T, h2T.b, 0, tok0, nt, segs, 24, 32, x1_scr[tok0:tok0 + nt, :])
                    rr["xt"] -= 1
                    for hd in range(8):
                        pm = psr("pq", [0, 1])
                        for kc in range(8):
                            S.op("pe", lambda e, kc=kc, pm=pm, hd=hd, nt=nt: e.matmul(pm[:, 0:nt], wpq[:, kc, hd * 128:(hd + 1) * 128], h2T[:, kc, 0:nt], start=(kc == 0), stop=(kc == 7)),
                                 reads=[wpq.b, h2T.b], writes=[pm.b], add=(kc > 0))
                        evac(qpT[:, hd, 0:nt], qpT.b, pm[:, 0:nt], pm.b)
                    for hd in range(8):
                        pm = psr("pq", [0, 1])
                        S.op("pe", lambda e, pm=pm, hd=hd, nt=nt: e.matmul(pm[:nt, 0:256], qpT[:, hd, 0:nt], KBD[:, :], start=True, stop=True), reads=[qpT.b, KBD.b], writes=[pm.b])
                        evac(Spe[:nt, hd, :], Spe.b, pm[:nt, 0:256], pm.b)
                    for hd in range(8):
                        for hf in range(2):
                            src = Spe[:nt, hd, hf * 128:(hf + 1) * 128]
                            S.op("dve", lambda e, src=src, hd=hd, hf=hf, nt=nt: e.max(out=v12[:nt, hd, hf, 0:8], in_=src), reads=[Spe.b], writes=[v12.b], add=True)
                            S.op("dve", lambda e, src=src, hd=hd, hf=hf, nt=nt: e.match_replace(out=wk[:nt, 0:128], in_to_replace=v12[:nt, hd, hf, 0:8], in_values=src, imm_value=-1e30),
                                 reads=[Spe.b, v12.b], writes=[wk.b])
                            S.op("dve", lambda e, hd=hd, hf=hf, nt=nt: e.max(out=v12[:nt, hd, hf, 8:16], in_=wk[:nt, 0:128]), reads=[wk.b], writes=[v12.b], add=True)
                            S.op("dve", lambda e, src=src, hd=hd, hf=hf, nt=nt: e.max_index(out=i12[:nt, hd, hf, 0:8], in_max=v12[:nt, hd, hf, 0:8], in_values=src),
                                 reads=[Spe.b, v12.b], writes=[i12.b], add=True)
                            S.op("dve", lambda e, src=src, hd=hd, hf=hf, nt=nt: e.max_index(out=i12[:nt, hd, hf, 8:16], in_max=v12[:nt, hd, hf, 8:16], in_values=src),
                                 reads=[Spe.b, v12.b], writes=[i12.b], add=True)
                    S.op("dve", lambda e, nt=nt: e.tensor_copy(i12f[:nt].rearrange("p h f k -> p (h f k)"), i12[:nt].rearrange("p h f k -> p (h f k)")), reads=[i12.b], writes=[i12f.b])
                    for hd in range(8):
                        S.op("dve", lambda e, hd=hd, nt=nt: e.tensor_tensor(out=cand[:nt, hd, :].rearrange("p (i j) -> p i j", i=16),
                                                                           in0=v12[:nt, hd, 0, :].unsqueeze(2).to_broadcast([nt, 16, 16]),
                                                                           in1=v12[:nt, hd, 1, :].unsqueeze(1).to_broadcast([nt, 16, 16]), op=ALU.add),
                             reads=[v12.b], writes=[cand.b], add=True)
                    for hd in range(8):
                        src = cand[:nt, hd, :]
                        S.op("dve", lambda e, src=src, hd=hd, nt=nt: e.max(out=sv[:nt, hd, 0:8], in_=src), reads=[cand.b], writes=[sv.b], add=True)
                        S.op("dve", lambda e, src=src, hd=hd, nt=nt: e.match_replace(out=wk[:nt, :], in_to_replace=sv[:nt, hd, 0:8], in_values=src, imm_value=-1e30),
                             reads=[cand.b, sv.b], writes=[wk.b])
                        S.op("dve", lambda e, hd=hd, nt=nt: e.max(out=sv[:nt, hd, 8:16], in_=wk[:nt, :]), reads=[wk.b], writes=[sv.b], add=True)
                        S.op("dve", lambda e, src=src, hd=hd, nt=nt: e.max_index(out=si[:nt, hd, 0:8], in_max=sv[:nt, hd, 0:8], in_values=src), reads=[cand.b, sv.b], writes=[si.b], add=True)
                        S.op("dve", lambda e, src=src, hd=hd, nt=nt: e.max_index(out=si[:nt, hd, 8:16], in_max=sv[:nt, hd, 8:16], in_values=src), reads=[cand.b, sv.b], writes=[si.b], add=True)
                    wv = abw[:nt, 2, :].rearrange("p (h k) -> p h k", h=8)
                    S.op("dve", lambda e, nt=nt, wv=wv: e.tensor_tensor(out=wv, in0=sv[:nt, :, :], in1=sv[:nt, :, 0:1].to_broadcast([nt, 8, 16]), op=ALU.subtract),
                         reads=[sv.b], writes=[abw.b], add=True)
                    S.op("act", lambda e, nt=nt: e.activation(abw[:nt, 2, :], abw[:nt, 2, :], AF.Exp), reads=[abw.b], writes=[abw.b])
                    S.op("dve", lambda e, nt=nt, wv=wv: e.tensor_reduce(out=zs[:nt, :], in_=wv, axis=AX.X, op=ALU.add), reads=[abw.b], writes=[zs.b])
                    S.op("dve", lambda e, nt=nt: e.reciprocal(zs[:nt, :], zs[:nt, :]), reads=[zs.b], writes=[zs.b])
                    S.op("dve", lambda e, nt=nt, wv=wv: e.tensor_tensor(out=wv, in0=wv, in1=zs[:nt, :].unsqueeze(2).to_broadcast([nt, 8, 16]), op=ALU.mult),
                         reads=[abw.b, zs.b], writes=[abw.b])
                    S.op("dve", lambda e, nt=nt: e.tensor_single_scalar(sij[:nt, 0].rearrange("p h k -> p (h k)"), si[:nt].rearrange("p h k -> p (h k)"), 4, op=ALU.logical_shift_right),
                         reads=[si.b], writes=[sij.b], add=True)
                    S.op("dve", lambda e, nt=nt: e.tensor_single_scalar(sij[:nt, 1].rearrange("p h k -> p (h k)"), si[:nt].rearrange("p h k -> p (h k)"), 15, op=ALU.bitwise_and),
                         reads=[si.b], writes=[sij.b], add=True)
                    S.op("dve", lambda e, nt=nt: e.tensor_copy(sijf[:nt].rearrange("p t h k -> p (t h k)"), sij[:nt].rearrange("p t h k -> p (t h k)")), reads=[sij.b], writes=[sijf.b])
                    for hd in range(8):
                        for ab in range(2):
                            S.op("dve", lambda e, hd=hd, ab=ab, nt=nt: e.tensor_tensor(out=eq[:nt], in0=sijf[:nt, ab, hd, :].unsqueeze(2).to_broadcast([nt, 16, 16]),
                                                                                     in1=iota16[:nt, :].unsqueeze(1).to_broadcast([nt, 16, 16]), op=ALU.is_equal),
                                 reads=[sijf.b, iota16.b], writes=[eq.b])
                            S.op("dve", lambda e, hd=hd, ab=ab, nt=nt: e.tensor_tensor(out=eq[:nt], in0=eq[:nt], in1=i12f[:nt, hd, ab, :].unsqueeze(1).to_broadcast([nt, 16, 16]), op=ALU.mult),
                                 reads=[eq.b, i12f.b], writes=[eq.b])
                            S.op("dve", lambda e, hd=hd, ab=ab, nt=nt: e.tensor_reduce(out=abw[:nt, ab, hd * 16:(hd + 1) * 16], in_=eq[:nt], axis=AX.X, op=ALU.add),
                                 reads=[eq.b], writes=[abw.b], add=True)
                    pm = psr("pq", [0, 1])
                    for k3 in range(3):
                        S.op("pe", lambda e, pm=pm, k3=k3, nt=nt: e.transpose(pm[:, k3 * 128:k3 * 128 + nt], abw[:nt, k3, :], ident[:nt, :nt]), reads=[abw.b, ident.b], writes=[pm.b], add=(k3 > 0))
                    S.op("act", lambda e, pm=pm: e.activation(abwT[:].rearrange("p k n -> p (k n)"), pm[:, 0:384], AF.Copy), reads=[pm.b], writes=[abwT.b])
                    for n0 in range(0, nt, 32):
                        nn = min(32, nt - n0)
                        S.op("dve", lambda e, n0=n0, nn=nn: e.tensor_tensor(out=OA[:, 0:nn, :], in0=iotaA[:, 0:nn, :], in1=abwT[:, 0, n0:n0 + nn].unsqueeze(2).to_broadcast([128, nn, 128]), op=ALU.is_equal),
                             reads=[iotaA.b, abwT.b], writes=[OA.b])
                        S.op("pool", lambda e, n0=n0, nn=nn: e.tensor_tensor(out=OA[:, 0:nn, :], in0=OA[:, 0:nn, :], in1=abwT[:, 2, n0:n0 + nn].unsqueeze(2).to_broadcast([128, nn, 128]), op=ALU.mult),
                             reads=[OA.b, abwT.b], writes=[OA.b])
                        S.op("dve", lambda e, n0=n0, nn=nn: e.tensor_tensor(out=OB[:, 0:nn, :], in0=iotaA[:, 0:nn, :], in1=abwT[:, 1, n0:n0 + nn].unsqueeze(2).to_broadcast([128, nn, 128]), op=ALU.is_equal),
                             reads=[iotaA.b, abwT.b], writes=[OB.b])
                        for n4 in range(0, nn, 4):
                            pg_ = psr("pg", [2, 3])
                            for q in range(4):
                                nl = n4 + q
                                S.op("pe", lambda e, pg_=pg_, q=q, nl=nl: e.matmul(pg_[:, q * 128:(q + 1) * 128], OB[:, nl, :], OA[:, nl, :], start=True, stop=True),
                                     reads=[OA.b, OB.b], writes=[pg_.b], add=(q > 0))
                            evac(GT[:].rearrange("p a n -> p n a")[:, n0 + n4:n0 + n4 + 4, :], GT.b, pg_[:, :].rearrange("p (n a) -> p n a", n=4), pg_.b)
                    oacc = [PS[6], PS[7]]
                    def stage_ab(a4, nt=nt):
                        ub = utb[a4 % 2]; vb = vtb[a4 % 2]
                        S.dma("sp", lambda e, ub=ub, a4=a4: e.dma_start(out=ub[:], in_=ut_r[a4]), reads=[b_utscr], writes=[ub.b])
                        S.dma("sp", lambda e, vb=vb, a4=a4: e.dma_start(out=vb[:], in_=v_r[a4]), reads=[b_vscr], writes=[vb.b])
                        pa = PS[4 + a4 % 2]
                        for c in range(4):
                            for kc in range(8):
                                S.op("pe", lambda e, pa=pa, c=c, kc=kc, ub=ub, nt=nt: e.matmul(pa[:, c * 128:c * 128 + nt], ub[:, c, kc * 128:(kc + 1) * 128], h2T[:, kc, 0:nt], start=(kc == 0), stop=(kc == 7)),
                                     reads=[ub.b, h2T.b], writes=[pa.b], add=(c > 0 or kc > 0))
                        gx = xs_[a4 % 2]; gu = us_[a4 % 2]

                        def v3(t_, nt=nt):
                            return t_[:, :].rearrange("p (c n) -> p c n", c=4)[:, :, 0:nt]
                        pav = v3(pa); gxv = v3(gx); guv = v3(gu)
                        S.op("act", lambda e, gxv=gxv, pav=pav: e.activation(gxv, pav, AF.Copy), reads=[pa.b], writes=[gx.b])
                        S.op("act", lambda e, guv=guv, pav=pav: e.activation(guv, pav, AF.Square), reads=[pa.b], writes=[gu.b])
                        S.op("dve", lambda e, guv=guv: e.tensor_scalar(guv, guv, 0.044715, 1.0, op0=ALU.mult, op1=ALU.add), reads=[gu.b], writes=[gu.b])

                    def stage_cde(a4, nt=nt):
                        vb = vtb[a4 % 2]
                        gx = xs_[a4 % 2]; gu = us_[a4 % 2]; gw = ws_[a4 % 2]; PT = PTs[a4 % 2]

                        def v3(t_, nt=nt):
                            return t_[:, :].rearrange("p (c n) -> p c n", c=4)[:, :, 0:nt]
                        gxv = v3(gx); guv = v3(gu); gwv = v3(gw); PTv = v3(PT)
                        gtv = GT[:, a4 * 4:(a4 + 1) * 4, 0:nt]
                        S.op("pool", lambda e, guv=guv, gxv=gxv, gwv=gwv: e.tensor_tensor(out=gwv, in0=guv, in1=gxv, op=ALU.mult), reads=[gu.b, gx.b], writes=[gw.b])
                        S.op("act", lambda e, gwv=gwv: e.activation(gwv, gwv, AF.Sigmoid, scale=1.5957691216057308), reads=[gw.b], writes=[gw.b])
                        S.op("dve", lambda e, gwv=gwv, gxv=gxv: e.tensor_tensor(out=gxv, in0=gwv, in1=gxv, op=ALU.mult), reads=[gw.b, gx.b], writes=[gx.b])
                        S.op("dve", lambda e, gxv=gxv, PTv=PTv, gtv=gtv: e.tensor_tensor(out=PTv, in0=gxv, in1=gtv, op=ALU.mult),
                             reads=[gx.b, GT.b], writes=[PT.b])
                        for c in range(4):
                            for hf in range(2):
                                first = (a4 == 0 and c == 0)
                                last = (a4 == 31 and c == 3)
                                S.op("pe", lambda e, PT=PT, c=c, hf=hf, vb=vb, first=first, last=last, nt=nt: e.matmul(oacc[hf][:nt, :], PT[:, c * 128:c * 128 + nt], vb[:, c, hf * 512:(hf + 1) * 512], start=first, stop=last),
                                     reads=[PT.b, vb.b], writes=[oacc[hf].b], add=(not first))

                    stage_ab(0)
                    for a4 in range(32):
                        if a4 + 1 < 32:
                            stage_ab(a4 + 1)
                        stage_cde(a4)
                    if pe_dbg:
                        y = yt[0]
                        for hf in range(2):
                            S.op("act", lambda e, y=y, hf=hf, nt=nt: e.activation(y[:nt, hf * 512:(hf + 1) * 512], oacc[hf][:nt, :], AF.Copy), reads=[oacc[hf].b], writes=[y.b], add=True)
                        dstd = y_p[tok0:tok0 + nt, :] if tok0 < 2048 else y_s[:, :]
                        S.dma("sp", lambda e, y=y, nt=nt, dstd=dstd: e.dma_start(out=dstd, in_=y[:nt, :]), reads=[y.b])
                    z = zt[0]; rr["zt5"] = rr.get("zt5", 0) + 1
                    gbc = g2p if tok0 < 2048 else g2s
                    for hf in range(2):
                        S.op("dve", lambda e, z=z, hf=hf, nt=nt, gbc=gbc: e.tensor_tensor(out=z[:nt, hf * 512:(hf + 1) * 512], in0=oacc[hf][:nt, :], in1=gbc[:nt, hf * 512:(hf + 1) * 512], op=ALU.mult),
                             reads=[oacc[hf].b, gbc.b], writes=[z.b], add=True)
                    S.op("dve", lambda e, z=z, x1l=x1l, nt=nt: e.scalar_tensor_tensor(out=z[:nt, :], in0=x1l[:nt, :], scalar=ALPHA, in1=z[:nt, :], op0=ALU.mult, op1=ALU.add),
                         reads=[x1l.b, z.b], writes=[z.b])
                    y = yt[0]; rr["yt"] = rr.get("yt", 0) + 1
                    layer_norm(lnst, z, nt, lg2, lb2, y)
                    dst = y_p[tok0:tok0 + nt, :] if tok0 < 2048 else y_s[:, :]
                    if not DBG.get('ydump'):
                        S.dma("sp", lambda e, y=y, nt=nt, dst=dst: e.dma_start(out=dst, in_=y[:nt, :]), reads=[y.b])
                S.barrier(flush=True)
        except _Stop:
            for nm in ('att_es', 'act_es'):
                st_ = locals().get(nm)
                if st_ is not None:
                    st_.close()
        S.finish()
        print("instructions:", S.ninstr, "sems:", S.semid)
    return nc


_CACHE = {}


def kernel(**inp):
    f32 = np.float32
    g = lambda k: np.ascontiguousarray(np.asarray(inp[k]))
    if "nc" not in _CACHE:
        _CACHE["nc"] = build_program()
    nc = _CACHE["nc"]
    j = np.arange(383)
    bk = t5_bucket_np(j - 127)
    ohb = np.zeros((32, 383), f32); ohb[bk, j] = 1.0
    negrow = np.where(j < 127, NEG, 0.0).astype(f32)[None, :]
    sel_p = np.zeros((5, 128), f32); sel_p[0, :] = 1.0
    sel_s = np.zeros((5, 32), f32)
    for b in range(4):
        sel_s[1 + b, 8 * b:8 * b + 8] = 1.0
    xpr, xsm, cpr, csm = g("x_prompt"), g("x_sample"), g("c_prompt"), g("c_sample")
    pr = DBG['pool_rows']
    ck = g("cache_k")[0].reshape(2560 * 128, 128)[:pr]; cv = g("cache_v")[0].reshape(2560 * 128, 128)[:pr]
    cki = g("cache_kidx")[0].reshape(2560 * 128, 64)[:pr]
    sc = g("state_conv")[0]; pt = g("page_table")
    shared = dict(ck=ck, cv=cv, cki=cki, rel_bias=g("rel_bias"), w_ada=g("w_ada")[0], b_ada=g("b_ada"), w_in=g("w_in")[0],
                  conv_w=g("conv_w")[0], conv_b=g("conv_b"), w_o_attn=g("w_o_attn")[0], w_o_conv=g("w_o_conv")[0], w_out=g("w_out")[0],
                  ln1_g=g("ln1_g"), ln1_b=g("ln1_b"), ln2_g=g("ln2_g"), ln2_b=g("ln2_b"), peer_wq=g("peer_wq")[0],
                  peer_k1=g("peer_k1")[0], peer_k2=g("peer_k2")[0], peer_u=g("peer_u")[0], peer_v=g("peer_v")[0],
                  ohb=ohb, negrow=negrow, sel_p=sel_p, sel_s=sel_s)
    in_maps = []
    for c in range(8):
        m = dict(shared)
        m["xp"] = xpr[c]
        m["xs"] = np.ascontiguousarray(xsm[4 * c:4 * c + 4].reshape(32, D))
        m["cvec"] = np.ascontiguousarray(np.concatenate([cpr[c:c + 1], csm[4 * c:4 * c + 4]], axis=0))
        m["sconv"] = np.ascontiguousarray(sc[4 * c:4 * c + 4].reshape(8, 512))
        m["ptab"] = np.ascontiguousarray(pt[4 * c:4 * c + 4]).astype(np.int32)
        in_maps.append(m)
    res = run_bass_kernel_spmd(nc, in_maps, core_ids=list(range(8)))
    R = res.results
    y_prompt = np.stack([R[c]["y_p"] for c in range(8)])
    y_sample = np.concatenate([R[c]["y_s"].reshape(4, 8, D) for c in range(8)])
    k_prompt = np.stack([R[c]["k_p"].reshape(SEQ, 2, 64) for c in range(8)])[None]
    v_prompt = np.stack([R[c]["v_p"].reshape(SEQ, 2, 64) for c in range(8)])[None]
    ki_prompt = np.stack([R[c]["ki_p"] for c in range(8)])[None]
    conv_prompt = np.stack([R[c]["conv_o"][0:2] for c in range(8)])[None]
    k_sample = np.concatenate([R[c]["k_s"].reshape(4, 8, 2, 64) for c in range(8)])[None]
    v_sample = np.concatenate([R[c]["v_s"].reshape(4, 8, 2, 64) for c in range(8)])[None]
    ki_sample = np.concatenate([R[c]["ki_s"].reshape(4, 8, 64) for c in range(8)])[None]
    conv_sample = np.concatenate([R[c]["conv_o"][2:10].reshape(4, 2, 512) for c in range(8)])[None]
    outs = (y_prompt, y_sample, k_prompt, v_prompt, ki_prompt, conv_prompt, k_sample, v_sample, ki_sample, conv_sample)
    return tuple(np.ascontiguousarray(o, dtype=f32) for o in outs)
```

```python
import math
from contextlib import ExitStack
import numpy as np
import concourse.bass as bass
import concourse.mybir as mybir
from concourse.bass_utils import run_bass_kernel_spmd

F32 = mybir.dt.float32
BF16 = mybir.dt.bfloat16
U32 = mybir.dt.uint32
I32 = mybir.dt.int32
ALU = mybir.AluOpType
AF = mybir.ActivationFunctionType
AX = mybir.AxisListType

D = 1024
SEQ = 2048
NTOK = 2080
NEG = -30000.0
ATTN_SCALE = 64 ** -0.5
IDX_SCALE = 256 ** -0.5
ALPHA = 2 ** 0.25
LN_EPS = 1e-5
NBIS = 26
STOP_AFTER = [99]
DBG = dict(strict=True, pool_rows=2560 * 128, prep=128, tblocks=list(range(16)), batches=list(range(4)), npages=64)


class _Stop(Exception):
    pass


_DISCARD = [False]


def stop_check(k):
    if STOP_AFTER[0] <= k:
        _DISCARD[0] = True
EPOCH = 16000


class Buf:
    __slots__ = ("name", "w", "r", "dsem", "dcnt")

    def __init__(self, name):
        self.name = name
        self.w = []
        self.r = []
        self.dsem = None
        self.dcnt = 0


class Tok:
    __slots__ = ("sem", "val", "eng", "grp")

    def __init__(self, sem, val, eng, grp=False):
        self.sem, self.val, self.eng, self.grp = sem, val, eng, grp


class Sched:
    ENG = ("pe", "act", "dve", "pool", "sp")

    def __init__(self, nc, es):
        self.nc = nc
        self.es = es
        self.q = {e: [] for e in self.ENG}
        self.n = {e: 0 for e in self.ENG}
        self.esem = {e: None for e in self.ENG}
        self.waited = {e: {} for e in self.ENG}
        self.semid = 0
        self.dma_sems = []
        self.last = {}
        self.ninstr = 0

    def new_sem(self, tag):
        self.semid += 1
        return self.es.enter_context(self.nc.semaphore(f"s{self.semid}_{tag}"))

    def _deps(self, eng, reads, writes, add):
        strict = DBG.get("strict")
        pe_chain = (eng == "pe" and add)
        deps = []
        for b in reads:
            for t in b.w:
                if t.eng == eng and eng == "pe" and not strict:
                    continue
                deps.append(t)
        for b in writes:
            for t in b.r:
                if t.eng == eng and t.eng is not None and (not strict or pe_chain):
                    continue
                deps.append(t)
            for t in b.w:
                if add and t.grp:
                    continue
                if t.eng == eng and t.eng is not None and (not strict or pe_chain):
                    continue
                deps.append(t)
        best = {}
        for t in deps:
            k = id(t.sem)
            if k not in best or best[k].val < t.val:
                best[k] = t
        out = []
        wd = self.waited[eng]
        for k, t in best.items():
            if wd.get(k, -1) >= t.val:
                continue
            wd[k] = t.val
            out.append((t.sem, t.val))
        return out

    def _commit(self, tok, reads, writes, add):
        for b in reads:
            b.r = [t for t in b.r if t.sem is not tok.sem] + [tok]
        for b in writes:
            if add:
                b.w = [t for t in b.w if t.sem is not tok.sem] + [tok]
            else:
                b.w = [tok]
            b.r = []

    def op(self, eng, fn, reads=(), writes=(), add=False):
        if _DISCARD[0]:
            return
        waits = self._deps(eng, reads, writes, add)
        if self.esem[eng] is None or self.n[eng] >= EPOCH:
            self.esem[eng] = self.new_sem(eng)
            self.n[eng] = 0
        self.n[eng] += 1
        sem, val = self.esem[eng], self.n[eng]
        tok = Tok(sem, val, eng, add)
        self.last[eng] = tok
        self._commit(tok, reads, writes, add)

        def thunk(e, fn=fn, waits=waits, sem=sem):
            for (s, v) in waits:
                e.wait_ge(s, v)
            fn(e).then_inc(sem, 1)
        self.q[eng].append(thunk)
        self.ninstr += 1

    def dma(self, eng, fn, reads=(), writes=(), sembuf=None, add=False):
        if _DISCARD[0]:
            return
        waits = self._deps(eng, reads, writes, add)
        sb = sembuf if sembuf is not None else (writes[0] if writes else reads[0])
        kind = "sw" if eng == "pool" else "hw"
        if sb.dsem is None:
            sb.dsem = {}
        if kind not in sb.dsem:
            ent = [self.new_sem("d" + kind), 0]
            sb.dsem[kind] = ent
            self.dma_sems.append(ent)
        ent = sb.dsem[kind]
        ent[1] += 16
        tok = Tok(ent[0], ent[1], None, add)
        self._commit(tok, reads, writes, add)

        def thunk(e, fn=fn, waits=waits, sem=ent[0]):
            for (s, v) in waits:
                e.wait_ge(s, v)
            fn(e).then_inc(sem, 16)
        self.q[eng].append(thunk)
        self.ninstr += 1

    def barrier(self, force=False, flush=False):
        if not (_DISCARD[0] and not force):
            toks = [t for t in self.last.values()]
            toks += [Tok(ent[0], ent[1], None) for ent in self.dma_sems]
            for eng in self.ENG:
                waits = []
                wd = self.waited[eng]
                for t in toks:
                    if t.eng == eng:
                        continue
                    k = id(t.sem)
                    if wd.get(k, -1) >= t.val:
                        continue
                    wd[k] = t.val
                    waits.append((t.sem, t.val))

                def thunk(e, waits=waits):
                    for (s, v) in waits:
                        e.wait_ge(s, v)
                self.q[eng].append(thunk)
        if flush:
            self.flush()

    def flush(self):
        nc = self.nc
        q = self.q
        if not any(q[e] for e in self.ENG):
            return
        with nc.Block() as block:
            @block.tensor
            def _(e):
                for t in q["pe"]:
                    t(e)

            @block.scalar
            def _(e):
                for t in q["act"]:
                    t(e)

            @block.vector
            def _(e):
                for t in q["dve"]:
                    t(e)

            @block.gpsimd
            def _(e):
                for t in q["pool"]:
                    t(e)

            @block.sync
            def _(e):
                for t in q["sp"]:
                    t(e)
        self.q = {e: [] for e in self.ENG}

    def finish(self):
        self.barrier(force=True)
        self.flush()


class TL:
    def __init__(self, t, name):
        self.t = t
        self.b = Buf(name)

    def __getitem__(self, k):
        return self.t[k]


def t5_bucket_np(d):
    n = np.maximum(d, 0)
    nf = np.maximum(n, 1).astype(np.float32)
    large = 16 + (np.log(nf / np.float32(16)) / np.float32(math.log(128 / 16)) * np.float32(16)).astype(np.int32)
    large = np.minimum(large, 31)
    return np.where(n < 16, n, large)


def build_program():
    nc = bass.Bass("TRN2", target_bir_lowering=False)
    _DISCARD[0] = False

    def din(name, shape, dt=F32):
        return nc.dram_tensor(name, list(shape), dt, kind="ExternalInput").ap()

    def dout(name, shape, dt=F32):
        return nc.dram_tensor(name, list(shape), dt, kind="ExternalOutput").ap()

    xp = din("xp", [SEQ, D]); xs = din("xs", [32, D]); cvec = din("cvec", [5, D])
    ck = din("ck", [DBG["pool_rows"], 128]); cvv = din("cv", [DBG["pool_rows"], 128]); cki = din("cki", [DBG["pool_rows"], 64])
    sconv = din("sconv", [8, 512]); ptab = din("ptab", [4, 64], I32)
    rel_bias = din("rel_bias", [32, 8])
    w_ada = din("w_ada", [D, 6 * D]); b_ada = din("b_ada", [1, 6 * D])
    w_in = din("w_in", [D, 4676])
    conv_w = din("conv_w", [3, 512]); conv_b = din("conv_b", [1, 512])
    w_o_attn = din("w_o_attn", [512, D]); w_o_conv = din("w_o_conv", [512, D]); w_out = din("w_out", [D, D])
    ln1_g = din("ln1_g", [1, D]); ln1_b = din("ln1_b", [1, D]); ln2_g = din("ln2_g", [1, D]); ln2_b = din("ln2_b", [1, D])
    peer_wq = din("peer_wq", [D, D]); peer_k1 = din("peer_k1", [128, 64]); peer_k2 = din("peer_k2", [128, 64])
    peer_u = din("peer_u", [16384, D]); peer_v = din("peer_v", [16384, D])
    ohb = din("ohb", [32, 383]); negrow = din("negrow", [1, 383])
    sel_p = din("sel_p", [5, 128]); sel_s = din("sel_s", [5, 32])

    y_p = dout("y_p", [SEQ, D]); y_s = dout("y_s", [32, D])
    k_p = dout("k_p", [SEQ, 128]); v_p = dout("v_p", [SEQ, 128]); ki_p = dout("ki_p", [SEQ, 64])
    conv_o = dout("conv_o", [10, 512])
    k_s = dout("k_s", [32, 128]); v_s = dout("v_s", [32, 128]); ki_s = dout("ki_s", [32, 64])

    x1_scr = nc.dram_tensor("x1_scr", [NTOK, D], F32).ap()
    pe_dbg = DBG.get("ydump") == "pe"
    tsc = nc.dram_tensor("tsc", [8, 128, 383], F32).ap()
    ut_scr = nc.dram_tensor("ut_scr", [32, 128, 4, 1024], BF16).ap()
    v_scr = nc.dram_tensor("v_scr", [32, 128, 4, D], BF16).ap()
    b_x1scr = Buf("x1scr"); b_tsc = Buf("tsc"); b_utscr = Buf("utscr"); b_vscr = Buf("vscr")
    b_out = Buf("outputs")

    w_in_r = w_in.rearrange("(kc p) n -> p kc n", p=128)

    with ExitStack() as es:
        S = Sched(nc, es)
        cnt = [0]

        def sb(stack, shape, dt, name=None):
            cnt[0] += 1
            nm = f"{name or 't'}{cnt[0]}"
            return TL(stack.enter_context(nc.sbuf_tensor(nm, list(shape), dt)), nm)

        PS = []
        for i in range(8):
            PS.append(TL(es.enter_context(nc.psum_tensor(f"ps{i}", [128, 512], F32)), f"ps{i}"))
        rr = {}

        def psr(key, banks):
            i = rr.get(key, 0)
            rr[key] = i + 1
            return PS[banks[i % len(banks)]]

        evq = [0]

        def evac(out_ap, out_b, in_ap, in_b, engs=("act", "dve")):
            e = engs[evq[0] % len(engs)]
            evq[0] += 1
            if e == "act":
                S.op("act", lambda en: en.activation(out_ap, in_ap, AF.Copy), reads=[in_b], writes=[out_b], add=True)
            else:
                S.op("dve", lambda en: en.tensor_copy(out_ap, in_ap), reads=[in_b], writes=[out_b], add=True)

        cs = es
        io = sb(cs, [128, 128], F32, "io")
        ident = sb(cs, [128, 128], F32, "ident")
        negA = sb(cs, [128, 128], F32, "negA")
        iota16 = sb(cs, [128, 16], F32, "iota16")
        iotap = sb(cs, [128, 1], F32, "iotap")
        ones_f = sb(cs, [128, 128], F32, "ones")
        modT = sb(cs, [128, 48, 5], F32, "modT")
        modrow_g = sb(cs, [5, 2, 1024], F32, "modrowg")
        Bm = sb(cs, [128, 8, 2, 128], F32, "Bm")
        KBD = sb(cs, [128, 256], BF16, "KBD")
        selp_sb = sb(cs, [5, 128], F32, "selp")
        sels_sb = sb(cs, [5, 32], F32, "sels")
        S.op("pool", lambda e: e.iota(io[:], [[1, 128]], base=0, channel_multiplier=-1, allow_small_or_imprecise_dtypes=True), writes=[io.b])
        S.op("pool", lambda e: e.iota(iota16[:], [[1, 16]], base=0, channel_multiplier=0, allow_small_or_imprecise_dtypes=True), writes=[iota16.b])
        S.op("pool", lambda e: e.iota(iotap[:], [[0, 1]], base=0, channel_multiplier=1, allow_small_or_imprecise_dtypes=True), writes=[iotap.b])
        S.op("dve", lambda e: e.tensor_scalar(ident[:], io[:], 0.0, None, op0=ALU.is_equal), reads=[io.b], writes=[ident.b])
        S.op("dve", lambda e: e.tensor_scalar(negA[:], io[:], 0.0, NEG, op0=ALU.is_gt, op1=ALU.mult), reads=[io.b], writes=[negA.b])
        S.op("dve", lambda e: e.memset(ones_f[:], 1.0), writes=[ones_f.b])
        S.op("pool", lambda e: e.memset(KBD[:], 0.0), writes=[KBD.b])
        S.dma("sp", lambda e: e.dma_start(out=selp_sb[:], in_=sel_p[:, :]), writes=[selp_sb.b])
        S.dma("sp", lambda e: e.dma_start(out=sels_sb[:], in_=sel_s[:, :]), writes=[sels_sb.b])

        try:
            with ExitStack() as p0:
                cv_sb = sb(p0, [5, D], F32, "cv")
                cT = sb(p0, [128, 8, 5], F32, "cT")
                S.dma("sp", lambda e: e.dma_start(out=cv_sb[:], in_=cvec[:, :]), writes=[cv_sb.b])
                ps = PS[0]
                for kc in range(8):
                    S.op("pe", lambda e, kc=kc: e.transpose(ps[:, kc * 5:(kc + 1) * 5], cv_sb[:, kc * 128:(kc + 1) * 128], ident[:5, :5]),
                         reads=[cv_sb.b, ident.b], writes=[ps.b], add=True)
                S.op("dve", lambda e: e.tensor_copy(cT[:].rearrange("p k c -> p (k c)"), ps[:, 0:40]), reads=[ps.b], writes=[cT.b])
                modrow = sb(p0, [5, 6 * D], F32, "modrow")
                was = [sb(p0, [128, 8, 512], F32, "wa") for _ in range(2)]
                bas = [sb(p0, [1, 512], F32, "ba") for _ in range(2)]
                w_ada_r = w_ada.rearrange("(kc p) n -> p kc n", p=128)
                for cg in range(12):
                    wa = was[cg % 2]; ba = bas[cg % 2]
                    S.dma("sp", lambda e, wa=wa, cg=cg: e.dma_start(out=wa[:], in_=w_ada_r[:, :, cg * 512:(cg + 1) * 512]), writes=[wa.b])
                    S.dma("sp", lambda e, ba=ba, cg=cg: e.dma_start(out=ba[:], in_=b_ada[:, cg * 512:(cg + 1) * 512]), writes=[ba.b])
                    pm = psr("mod", [1, 2])
                    for kc in range(8):
                        S.op("pe", lambda e, kc=kc, wa=wa, pm=pm: e.matmul(pm[0:5, :], cT[:, kc, :], wa[:, kc, :], start=(kc == 0), stop=False),
                             reads=[cT.b, wa.b], writes=[pm.b], add=(kc > 0))
                    S.op("pe", lambda e, ba=ba, pm=pm: e.matmul(pm[0:5, :], ones_f[0:1, 0:5], ba[0:1, :], start=False, stop=True),
                         reads=[ones_f.b, ba.b], writes=[pm.b], add=True)
                    S.op("act", lambda e, pm=pm, cg=cg: e.activation(modrow[:, cg * 512:(cg + 1) * 512], pm[0:5, :], AF.Copy),
                         reads=[pm.b], writes=[modrow.b], add=True)
                S.op("dve", lambda e: e.tensor_copy(modrow_g[:, 0, :], modrow[:, 2 * D:3 * D]), reads=[modrow.b], writes=[modrow_g.b])
                S.op("dve", lambda e: e.tensor_copy(modrow_g[:, 1, :], modrow[:, 5 * D:6 * D]), reads=[modrow.b], writes=[modrow_g.b], add=True)
                for jg in range(2):
                    pm = psr("mod", [1, 2])
                    for jj in range(24):
                        j = jg * 24 + jj
                        S.op("pe", lambda e, j=j, jj=jj, pm=pm: e.transpose(pm[:, jj * 5:(jj + 1) * 5], modrow[:, j * 128:(j + 1) * 128], ident[:5, :5]),
                             reads=[modrow.b, ident.b], writes=[pm.b], add=(jj > 0))
                    S.op("dve", lambda e, jg=jg, pm=pm: e.tensor_copy(modT[:, jg * 24:(jg + 1) * 24, :].rearrange("p j c -> p (j c)"), pm[:, 0:120]),
                         reads=[pm.b], writes=[modT.b], add=True)
                for j0 in (8, 32):
                    S.op("dve", lambda e, j0=j0: e.tensor_scalar(modT[:, j0:j0 + 8, :], modT[:, j0:j0 + 8, :], 1.0, None, op0=ALU.add),
                         reads=[modT.b], writes=[modT.b])
                rb_sb = sb(p0, [32, 8], F32, "rb")
                rbrep = sb(p0, [32, 8, 128], F32, "rbrep")
                ohb_sb = sb(p0, [32, 383], F32, "ohb")
                neg_sb = sb(p0, [1, 383], F32, "negr")
                tv = sb(p0, [128, 8, 383], F32, "tv")
                b31 = sb(p0, [128, 8], F32, "b31")
                S.dma("sp", lambda e: e.dma_start(out=rb_sb[:], in_=rel_bias[:, :]), writes=[rb_sb.b])
                S.dma("sp", lambda e: e.dma_start(out=ohb_sb[:], in_=ohb[:, :]), writes=[ohb_sb.b])
                S.dma("sp", lambda e: e.dma_start(out=neg_sb[:], in_=negrow[:, :]), writes=[neg_sb.b])
                for h in range(8):
                    S.op("dve", lambda e, h=h: e.tensor_copy(rbrep[:, h, :], rb_sb[:, h:h + 1].to_broadcast([32, 128])),
                         reads=[rb_sb.b], writes=[rbrep.b], add=True)
                for h in range(8):
                    pm = psr("mod", [1, 2])
                    S.op("pe", lambda e, h=h, pm=pm: e.matmul(pm[:, 0:383], rbrep[:, h, :], ohb_sb[:, :], start=True, stop=False),
                         reads=[rbrep.b, ohb_sb.b], writes=[pm.b])
                    S.op("pe", lambda e, pm=pm: e.matmul(pm[:, 0:383], ones_f[0:1, :], neg_sb[0:1, :], start=False, stop=True),
                         reads=[ones_f.b, neg_sb.b], writes=[pm.b], add=True)
                    S.op("act", lambda e, h=h, pm=pm: e.activation(b31[:, h:h + 1], pm[:, 382:383], AF.Copy), reads=[pm.b], writes=[b31.b], add=True)
                    S.op("dve", lambda e, h=h, pm=pm: e.tensor_scalar(tv[:, h, :], pm[:, 0:383], b31[:, h:h + 1], None, op0=ALU.subtract),
                         reads=[pm.b, b31.b], writes=[tv.b], add=True)
                S.dma("sp", lambda e: e.dma_start(out=tsc.rearrange("h p j -> p h j"), in_=tv[:]), reads=[tv.b], writes=[b_tsc])
                for h in range(8):
                    for ci, c in enumerate((0, 128)):
                        src = bass.AP(tsc.tensor, h * 128 * 383 + 127 + c, [[382, 128], [1, 128]])
                        S.dma("sp", lambda e, h=h, ci=ci, src=src: e.dma_start(out=Bm[:, h, ci, :], in_=src), reads=[b_tsc], writes=[Bm.b], add=True)
                k12 = sb(p0, [128, 128], F32, "k12")
                S.dma("sp", lambda e: e.dma_start(out=k12[:, 0:64], in_=peer_k1[:, :]), writes=[k12.b])
                S.dma("sp", lambda e: e.dma_start(out=k12[:, 64:128], in_=peer_k2[:, :]), writes=[k12.b], add=True)
                pm = psr("mod", [1, 2])
                S.op("pe", lambda e, pm=pm: e.transpose(pm[:, 0:128], k12[:], ident[:]), reads=[k12.b, ident.b], writes=[pm.b])
                S.op("dve", lambda e, pm=pm: e.tensor_copy(KBD[0:64, 0:128], pm[0:64, 0:128]), reads=[pm.b], writes=[KBD.b], add=True)
                S.op("dve", lambda e, pm=pm: e.tensor_copy(KBD[64:128, 128:256], pm[64:128, 0:128]), reads=[pm.b], writes=[KBD.b], add=True)

                ust = [sb(p0, [128, D], F32, "ust") for _ in range(2)]
                utb = [sb(p0, [128, 8, 128], BF16, "utb") for _ in range(2)]
                vst = [sb(p0, [128, D], BF16, "vst") for _ in range(2)]
                for a in range(DBG['prep']):
                    us = ust[a % 2]; ub = utb[a % 2]; vs_ = vst[a % 2]
                    S.dma("sp", lambda e, us=us, a=a: e.dma_start(out=us[:], in_=peer_u[a * 128:(a + 1) * 128, :]), writes=[us.b])
                    for hb in range(2):
                        pm = psr("prep", [3, 4, 5, 6])
                        for k4 in range(4):
                            kc = hb * 4 + k4
                            S.op("pe", lambda e, kc=kc, k4=k4, pm=pm, us=us: e.transpose(pm[:, k4 * 128:(k4 + 1) * 128], us[:, kc * 128:(kc + 1) * 128], ident[:]),
                                 reads=[us.b, ident.b], writes=[pm.b], add=(k4 > 0))
                        evac(ub[:, hb * 4:(hb + 1) * 4, :].rearrange("p k e -> p (k e)"), ub.b, pm[:, :], pm.b)
                    S.dma("sp", lambda e, ub=ub, a=a: e.dma_start(out=ut_scr[a // 4, :, a % 4, :], in_=ub[:].rearrange("p k e -> p (k e)")), reads=[ub.b], writes=[b_utscr], sembuf=ub.b, add=True)
                    S.dma("pool", lambda e, vs_=vs_, a=a: e.dma_start(out=vs_[:], in_=peer_v[a * 128:(a + 1) * 128, :]), writes=[vs_.b])
                    S.dma("sp", lambda e, vs_=vs_, a=a: e.dma_start(out=v_scr[a // 4, :, a % 4, :], in_=vs_[:]), reads=[vs_.b], writes=[b_vscr], sembuf=vs_.b, add=True)
                S.barrier(flush=True)
            S.barrier()

            stop_check(0)
            TILES = [(128 * t, 128, 0) for t in range(16)] + [(2048 + 8 * b, 8, 1 + b) for b in range(4)]

            def load_h_T(stack_tiles, dstT, dst_b, col0, tok0, nt, cidx, shj, scj, src_ap):
                xt = stack_tiles[rr.get("xt", 0) % len(stack_tiles)]
                rr["xt"] = rr.get("xt", 0) + 1
                S.dma("sp", lambda e: e.dma_start(out=xt[:nt, :], in_=src_ap), reads=[b_x1scr], writes=[xt.b])
                for hb in range(2):
                    pm = psr("tr", [0, 1])
                    for k4 in range(4):
                        kc = hb * 4 + k4
                        S.op("pe", lambda e, kc=kc, k4=k4, pm=pm: e.transpose(pm[:, k4 * 128:k4 * 128 + nt], xt[:nt, kc * 128:(kc + 1) * 128], ident[:nt, :nt]),
                             reads=[xt.b, ident.b], writes=[pm.b], add=(k4 > 0))
                    for k4 in range(4):
                        kc = hb * 4 + k4
                        if isinstance(cidx, int):
                            segs = [(0, nt, cidx)]
                        else:
                            segs = cidx
                        for (c0, cn, ci) in segs:
                            S.op("act", lambda e, kc=kc, k4=k4, pm=pm, c0=c0, cn=cn, ci=ci: e.activation(
                                dstT[:, kc, col0 + c0:col0 + c0 + cn], pm[:, k4 * 128 + c0:k4 * 128 + c0 + cn], AF.Identity,
                                bias=modT[:, shj + kc, ci:ci + 1], scale=modT[:, scj + kc, ci:ci + 1]),
                                reads=[pm.b, modT.b], writes=[dst_b], add=True)

            act_es = ExitStack()
            o_convT = sb(act_es, [128, 4, NTOK], BF16, "oconvT")
            o_attnT = sb(act_es, [64, 8, NTOK], BF16, "oattnT")
            att_es = ExitStack()
            qT = sb(att_es, [128, 4, NTOK], BF16, "qT")
            kTd = sb(att_es, [128, 2, NTOK], BF16, "kTd")
            qiT = sb(att_es, [128, 2, NTOK], BF16, "qiT")
            kiTd = sb(att_es, [128, NTOK], BF16, "kiTd")
            Vx = sb(att_es, [128, 20, 2, 65], BF16, "Vx")
            wi_sb = sb(att_es, [128, 20, 4], F32, "wi")
            S.op("pool", lambda e: e.memset(Vx[:], 1.0), writes=[Vx.b])

            GROUPS = [(0, 512), (512, 512), (1024, 512), (1536, 512), (2048, 32)]

            with ExitStack() as p1:
                h1T = sb(p1, [128, 8, NTOK], BF16, "h1T")
                xts = [sb(p1, [128, D], F32, "xt") for _ in range(1)]
                for (tok0, nt, ci) in TILES:
                    src = xp[tok0:tok0 + nt, :] if tok0 < 2048 else xs[tok0 - 2048:tok0 - 2048 + nt, :]
                    load_h_T(xts, h1T, h1T.b, tok0, tok0, nt, ci, 0, 8, src)
                wq = sb(p1, [128, 8, 512], BF16, "wq")
                wkd = sb(p1, [128, 8, 2, 128], BF16, "wkd")
                wqi = sb(p1, [128, 8, 256], BF16, "wqi")
                wkid = sb(p1, [128, 8, 128], BF16, "wkid")
                wtm = sb(p1, [128, 8, 324], BF16, "wtm")
                S.dma("pool", lambda e: e.dma_start(out=wq[:], in_=w_in_r[:, :, 0:512]), writes=[wq.b])
                for g in range(2):
                    for hf in range(2):
                        S.dma("pool", lambda e, g=g, hf=hf: e.dma_start(out=wkd[:, :, g, hf * 64:(hf + 1) * 64], in_=w_in_r[:, :, 512 + 64 * g:576 + 64 * g]),
                              writes=[wkd.b], add=True)
                S.dma("pool", lambda e: e.dma_start(out=wqi[:], in_=w_in_r[:, :, 768:1024]), writes=[wqi.b])
                for hf in range(2):
                    S.dma("pool", lambda e, hf=hf: e.dma_start(out=wkid[:, :, hf * 64:(hf + 1) * 64], in_=w_in_r[:, :, 1028:1092]), writes=[wkid.b], add=True)
                S.dma("pool", lambda e: e.dma_start(out=wtm[:, :, 0:256], in_=w_in_r[:, :, 512:768]), writes=[wtm.b], add=True)
                S.dma("pool", lambda e: e.dma_start(out=wtm[:, :, 256:320], in_=w_in_r[:, :, 1028:1092]), writes=[wtm.b], add=True)
                S.dma("pool", lambda e: e.dma_start(out=wtm[:, :, 320:324], in_=w_in_r[:, :, 1024:1028]), writes=[wtm.b], add=True)
                fm_sets = []
                for j in range(4):
                    fm_sets.append((lambda kc, j=j: wq[:, kc, j * 128:(j + 1) * 128], wq.b, lambda c0, n, j=j: qT[:, j, c0:c0 + n], qT.b))
                for g in range(2):
                    fm_sets.append((lambda kc, g=g: wkd[:, kc, g, :], wkd.b, lambda c0, n, g=g: kTd[:, g, c0:c0 + n], kTd.b))
                for j in range(2):
                    fm_sets.append((lambda kc, j=j: wqi[:, kc, j * 128:(j + 1) * 128], wqi.b, lambda c0, n, j=j: qiT[:, j, c0:c0 + n], qiT.b))
                fm_sets.append((lambda kc: wkid[:, kc, :], wkid.b, lambda c0, n: kiTd[:, c0:c0 + n], kiTd.b))
                for (wf, wb, df, db) in fm_sets:
                    for (g0, gn) in GROUPS:
                        pm = psr("fm", [2, 3, 4, 5])
                        for kc in range(8):
                            S.op("pe", lambda e, kc=kc, pm=pm, wf=wf, g0=g0, gn=gn: e.matmul(pm[:, 0:gn], wf(kc), h1T[:, kc, g0:g0 + gn], start=(kc == 0), stop=(kc == 7)),
                                 reads=[wb, h1T.b], writes=[pm.b], add=(kc > 0))
                        evac(df(g0, gn), db, pm[:, 0:gn], pm.b)
                kvst = [sb(p1, [128, 324], F32, "kvst") for _ in range(2)]
                for ti, (tok0, nt, ci) in enumerate(TILES):
                    pm = psr("fm", [2, 3, 4, 5])
                    kv = kvst[ti % 2]
                    for kc in range(8):
                        S.op("pe", lambda e, kc=kc, pm=pm, tok0=tok0, nt=nt: e.matmul(pm[:nt, 0:324], h1T[:, kc, tok0:tok0 + nt], wtm[:, kc, :], start=(kc == 0), stop=(kc == 7)),
                             reads=[wtm.b, h1T.b], writes=[pm.b], add=(kc > 0))
                    S.op("act", lambda e, pm=pm, kv=kv, nt=nt: e.activation(kv[:nt, :], pm[:nt, 0:324], AF.Copy), reads=[pm.b], writes=[kv.b])
                    if tok0 < 2048:
                        ko, vo, kio, r0 = k_p, v_p, ki_p, tok0
                    else:
                        ko, vo, kio, r0 = k_s, v_s, ki_s, tok0 - 2048
                    S.dma("sp", lambda e, kv=kv, nt=nt, ko=ko, r0=r0: e.dma_start(out=ko[r0:r0 + nt, :], in_=kv[:nt, 0:128]), reads=[kv.b])
                    S.dma("sp", lambda e, kv=kv, nt=nt, vo=vo, r0=r0: e.dma_start(out=vo[r0:r0 + nt, :], in_=kv[:nt, 128:256]), reads=[kv.b])
                    S.dma("sp", lambda e, kv=kv, nt=nt, kio=kio, r0=r0: e.dma_start(out=kio[r0:r0 + nt, :], in_=kv[:nt, 256:320]), reads=[kv.b])
                    S.op("dve", lambda e, kv=kv, nt=nt, ti=ti: e.tensor_copy(Vx[:nt, ti, :, 0:64], kv[:nt, 128:256].rearrange("p (g d) -> p g d", g=2)),
                         reads=[kv.b], writes=[Vx.b], add=True)
                    S.op("dve", lambda e, kv=kv, nt=nt, ti=ti: e.tensor_copy(wi_sb[:nt, ti, :], kv[:nt, 320:324]), reads=[kv.b], writes=[wi_sb.b], add=True)

                cw = sb(p1, [128, 4, 3], F32, "cw"); cb = sb(p1, [128, 4], F32, "cb")
                for cc in range(4):
                    S.dma("sp", lambda e, cc=cc: e.dma_start(out=cw[:, cc, :], in_=conv_w[:, cc * 128:(cc + 1) * 128].rearrange("j p -> p j"), allow_slow_non_contiguous=True), writes=[cw.b], add=True)
                    S.dma("sp", lambda e, cc=cc: e.dma_start(out=cb[:, cc:cc + 1], in_=conv_b[:, cc * 128:(cc + 1) * 128].rearrange("o p -> p o"), allow_slow_non_contiguous=True), writes=[cb.b], add=True)
                Up = sb(p1, [128, 2050], F32, "Up"); Us = sb(p1, [128, 4, 10], F32, "Us")
                Useq = lambda sq: (Up[:, :] if sq == 0 else Us[:, sq - 1, :])
                Ub = lambda sq: (Up.b if sq == 0 else Us.b)
                bgT = sb(p1, [128, NTOK], BF16, "bgT")
                ycv = sb(p1, [128, 2048], F32, "ycv")
                cgs = [sb(p1, [128, 512], F32, "cgs") for _ in range(1)]
                lastU = sb(p1, [128, 4, 10], F32, "lastU")
                wcv = [sb(p1, [128, 8, 3, 128], BF16, "wcv") for _ in range(1)]
                for cc in range(4):
                    wc = wcv[0]
                    for k3 in range(3):
                        S.dma("pool", lambda e, wc=wc, k3=k3, cc=cc: e.dma_start(out=wc[:, :, k3, :], in_=w_in_r[:, :, 1092 + 512 * k3 + 128 * cc:1092 + 512 * k3 + 128 * (cc + 1)]),
                              writes=[wc.b], add=(k3 > 0))
                    S.op("pool", lambda e: e.memset(Up[:, 0:2], 0.0), writes=[Up.b], add=True)
                    for b in range(4):
                        S.dma("sp", lambda e, cc=cc, b=b: e.dma_start(out=Us[:, b, 0:2], in_=sconv[2 * b:2 * b + 2, cc * 128:(cc + 1) * 128].rearrange("t p -> p t"), allow_slow_non_contiguous=True),
                              writes=[Us.b], add=True)
                    for (g0, gn) in GROUPS:
                        pb = psr("fm", [2, 3, 4, 5]); pc = psr("fm", [2, 3, 4, 5]); px = psr("fm", [2, 3, 4, 5])
                        for k3, pp in enumerate((pb, pc, px)):
                            for kc in range(8):
                                S.op("pe", lambda e, kc=kc, pp=pp, k3=k3, wc=wc, g0=g0, gn=gn: e.matmul(pp[:, 0:gn], wc[:, kc, k3, :], h1T[:, kc, g0:g0 + gn], start=(kc == 0), stop=(kc == 7)),
                                     reads=[wc.b, h1T.b], writes=[pp.b], add=(kc > 0))
                        cgx = cgs[0]; rr["cg"] = rr.get("cg", 0) + 1
                        S.op("act", lambda e, cgx=cgx, pc=pc, gn=gn: e.activation(cgx[:, 0:gn], pc[:, 0:gn], AF.Copy), reads=[pc.b], writes=[cgx.b])
                        S.op("act", lambda e, pb=pb, g0=g0, gn=gn: e.activation(bgT[:, g0:g0 + gn], pb[:, 0:gn], AF.Copy), reads=[pb.b], writes=[bgT.b], add=True)
                        if g0 < 2048:
                            S.op("dve", lambda e, cgx=cgx, px=px, g0=g0, gn=gn: e.tensor_tensor(out=Up[:, 2 + g0:2 + g0 + gn], in0=cgx[:, 0:gn], in1=px[:, 0:gn], op=ALU.mult),
                                 reads=[cgx.b, px.b], writes=[Up.b], add=True)
                        else:
                            S.op("dve", lambda e, cgx=cgx, px=px: e.tensor_tensor(out=Us[:, :, 2:10], in0=cgx[:, 0:32].rearrange("p (b t) -> p b t", b=4),
                                                                                 in1=px[:, 0:32].rearrange("p (b t) -> p b t", b=4), op=ALU.mult),
                                 reads=[cgx.b, px.b], writes=[Us.b], add=True)
                    for (sq, T_, c0) in [(0, 2048, 0)] + [(1 + b, 8, 2048 + 8 * b) for b in range(4)]:
                        S.op("dve", lambda e, sq=sq, T_=T_, cc=cc: e.tensor_scalar(ycv[:, 0:T_], Useq(sq)[:, 0:T_], cw[:, cc, 0:1], cb[:, cc:cc + 1], op0=ALU.mult, op1=ALU.add),
                             reads=[Ub(sq), cw.b, cb.b], writes=[ycv.b])
                        for j in (1, 2):
                            S.op("dve", lambda e, sq=sq, T_=T_, cc=cc, j=j: e.scalar_tensor_tensor(out=ycv[:, 0:T_], in0=Useq(sq)[:, j:j + T_], scalar=cw[:, cc, j:j + 1], in1=ycv[:, 0:T_], op0=ALU.mult, op1=ALU.add),
                                 reads=[Ub(sq), cw.b, ycv.b], writes=[ycv.b])
                        S.op("dve", lambda e, T_=T_, cc=cc, c0=c0: e.tensor_tensor(out=o_convT[:, cc, c0:c0 + T_], in0=ycv[:, 0:T_], in1=bgT[:, c0:c0 + T_], op=ALU.mult),
                             reads=[ycv.b, bgT.b], writes=[o_convT.b], add=True)
                        S.op("act", lambda e, sq=sq, T_=T_, cc=cc: e.activation(lastU[:, cc, 2 * sq:2 * sq + 2], Useq(sq)[:, T_:T_ + 2], AF.Copy), reads=[Ub(sq)], writes=[lastU.b], add=True)
                pm = psr("fm", [2, 3, 4, 5])
                for cc in range(4):
                    S.op("pe", lambda e, cc=cc, pm=pm: e.transpose(pm[0:10, cc * 128:(cc + 1) * 128], lastU[:, cc, :], ident[:, :]),
                         reads=[lastU.b, ident.b], writes=[pm.b], add=(cc > 0))
                cst = sb(p1, [10, 512], F32, "cst")
                S.op("act", lambda e, pm=pm: e.activation(cst[:], pm[0:10, :], AF.Copy), reads=[pm.b], writes=[cst.b])
                S.dma("sp", lambda e: e.dma_start(out=conv_o[:, :], in_=cst[:]), reads=[cst.b])
                S.barrier(flush=True)
            S.barrier()

            stop_check(2)
            def attention(st, blocks, nq, qtok0, wi_ap, L, need_topk, S_sc, kiT_ap_fn):
                rt, cntt, lo, W, mid, tt, mn, mx_ = st["rt"], st["cnt"], st["lo"], st["W"], st["mid"], st["tt"], st["mn"], st["mx"]
                junk = st["junk"]
                k0 = 0
                while k0 < L:
                    n = min(512, L - k0)
                    kap, kb = kiT_ap_fn(k0, n)
                    for h in range(4):
                        pm = PS[h % 2]
                        hp = (h % 2) * 64
                        S.op("pe", lambda e, pm=pm, h=h, hp=hp, kap=kap, n=n: e.matmul(pm[:nq, 0:n], qiT[hp:hp + 64, h // 2, qtok0:qtok0 + nq], kap[hp:hp + 64, :], start=True, stop=True),
                             reads=[qiT.b, kb], writes=[pm.b])
                        r = rt[rr.get("rt", 0) % 2]; rr["rt"] = rr.get("rt", 0) + 1
                        S.op("act", lambda e, pm=pm, r=r, n=n: e.activation(r[:nq, 0:n], pm[:nq, 0:n], AF.Relu, scale=IDX_SCALE), reads=[pm.b], writes=[r.b])
                        if h == 0:
                            S.op("dve", lambda e, r=r, n=n, k0=k0: e.tensor_scalar(S_sc[:nq, k0:k0 + n], r[:nq, 0:n], wi_ap[:, 0:1], None, op0=ALU.mult),
                                 reads=[r.b, wi_sb.b], writes=[S_sc.b], add=True)
                        else:
                            S.op("dve", lambda e, r=r, n=n, k0=k0, h=h: e.scalar_tensor_tensor(out=S_sc[:nq, k0:k0 + n], in0=r[:nq, 0:n], scalar=wi_ap[:, h:h + 1], in1=S_sc[:nq, k0:k0 + n], op0=ALU.mult, op1=ALU.add),
                                 reads=[r.b, wi_sb.b, S_sc.b], writes=[S_sc.b])
                    k0 += n
                stop_check(2.1)
                nk0 = blocks[-1]["nk"]
                if need_topk:
                    S.op("dve", lambda e: e.tensor_reduce(out=mx_[:nq, :], in_=S_sc[:nq, 0:L], axis=AX.X, op=ALU.max), reads=[S_sc.b], writes=[mx_.b])
                    S.op("dve", lambda e: e.tensor_reduce(out=mn[:nq, :], in_=S_sc[:nq, 0:L], axis=AX.X, op=ALU.min), reads=[S_sc.b], writes=[mn.b])
                    S.op("dve", lambda e: e.tensor_scalar(lo[:nq, :], mn[:nq, :], -1.0, None, op0=ALU.add), reads=[mn.b], writes=[lo.b])
                    S.op("dve", lambda e: e.scalar_tensor_tensor(out=W[:nq, :], in0=mx_[:nq, :], scalar=2.0, in1=mn[:nq, :], op0=ALU.add, op1=ALU.subtract),
                         reads=[mx_.b, mn.b], writes=[W.b])
                S.op("dve", lambda e: e.tensor_tensor(out=S_sc[:nq, L - nk0:L], in0=S_sc[:nq, L - nk0:L], in1=negA[:nq, :nk0], op=ALU.add),
                     reads=[S_sc.b, negA.b], writes=[S_sc.b])
                if need_topk:
                    S.op("dve", lambda e: e.memset(cntt[:], 0.0), writes=[cntt.b])
                    for it in range(NBIS):
                        ck_ = 2.0 ** -(it + 1)
                        S.op("dve", lambda e, ck_=ck_: e.scalar_tensor_tensor(out=mid[:nq, :], in0=W[:nq, :], scalar=ck_, in1=lo[:nq, :], op0=ALU.mult, op1=ALU.add),
                             reads=[W.b, lo.b], writes=[mid.b])
                        S.op("dve", lambda e, it=it: e.tensor_scalar(junk[:nq, 0:L], S_sc[:nq, 0:L], mid[:nq, 0:1], 0.0, op0=ALU.is_gt, op1=ALU.add, accum_out=cntt[:nq, it:it + 1]),
                             reads=[S_sc.b, mid.b, cntt.b], writes=[junk.b, cntt.b])
                        S.op("dve", lambda e, it=it, ck_=ck_: e.tensor_scalar(tt[:nq, :], cntt[:nq, it:it + 1], 256.0, ck_, op0=ALU.is_ge, op1=ALU.mult),
                             reads=[cntt.b], writes=[tt.b])
                        S.op("dve", lambda e: e.scalar_tensor_tensor(out=lo[:nq, :], in0=tt[:nq, :], scalar=W[:nq, 0:1], in1=lo[:nq, :], op0=ALU.mult, op1=ALU.add),
                             reads=[tt.b, W.b, lo.b], writes=[lo.b])
                    S.op("dve", lambda e: e.tensor_scalar(S_sc[:nq, 0:L], S_sc[:nq, 0:L], lo[:nq, 0:1], None, op0=ALU.is_gt), reads=[S_sc.b, lo.b], writes=[S_sc.b])
                else:
                    S.op("dve", lambda e: e.tensor_scalar(S_sc[:nq, 0:L], S_sc[:nq, 0:L], -10000.0, None, op0=ALU.is_gt), reads=[S_sc.b], writes=[S_sc.b])
                stop_check(2.2)
                oacc = [PS[6], PS[7]]
                kpos = 0
                nb = len(blocks)
                for bi, blk in enumerate(blocks):
                    nk = blk["nk"]
                    if blk.get("prep"):
                        blk["prep"]()
                    pmk = psr("mk", [2, 3])
                    S.op("pe", lambda e, pmk=pmk, kpos=kpos, nk=nk: e.transpose(pmk[:nk, 0:nq], S_sc[:nq, kpos:kpos + nk], ident[:nq, :nq]),
                         reads=[S_sc.b, ident.b], writes=[pmk.b])
                    stop_check(2.22)
                    addm = st["addm"][bi % 2]
                    S.op("dve", lambda e, pmk=pmk, addm=addm, nk=nk: e.tensor_scalar(addm[:nk, 0:nq], pmk[:nk, 0:nq], -1.0, -NEG, op0=ALU.add, op1=ALU.mult),
                         reads=[pmk.b], writes=[addm.b])
                    stop_check(2.25)
                    pqE = PS[4]; pqO = PS[5]
                    for h in (0, 2, 4, 6, 1, 3, 5, 7):
                        g = h // 4
                        hp = (h % 2) * 64
                        pq = pqE if hp == 0 else pqO
                        col = (h // 2) * nq
                        kap, kb = blk["KT"](g)
                        S.op("pe", lambda e, pq=pq, col=col, h=h, hp=hp, kap=kap, nk=nk: e.matmul(pq[:nk, col:col + nq], kap[hp:hp + 64, :], qT[hp:hp + 64, h // 2, qtok0:qtok0 + nq], start=True, stop=True),
                             reads=[kb, qT.b], writes=[pq.b], add=True)
                    stop_check(2.3)
                    for g in range(2):
                        hs = (4 * g, 4 * g + 2, 4 * g + 1, 4 * g + 3)
                        lg = st["lg"][rr.get("lg", 0) % 2]; rr["lg"] = rr.get("lg", 0) + 1
                        lgv = lg[:nk, 0:4 * nq].rearrange("p (s q) -> p s q", s=4)
                        for half, pq in enumerate((pqE, pqO)):
                            S.op("dve", lambda e, pq=pq, lgv=lgv, half=half, g=g, addm=addm, nk=nk: e.scalar_tensor_tensor(
                                out=lgv[:, 2 * half:2 * half + 2, :], in0=pq[:nk, 2 * g * nq:(2 * g + 2) * nq].rearrange("p (s q) -> p s q", s=2),
                                scalar=ATTN_SCALE, in1=addm[:nk, 0:nq].unsqueeze(1).to_broadcast([nk, 2, nq]), op0=ALU.mult, op1=ALU.add),
                                reads=[pq.b, addm.b], writes=[lg.b], add=True)
                        stop_check(2.32)
                        if blk["kind"] != "far":
                            ci = 0 if blk["kind"] == "near0" else 1
                            for sl in range(4):
                                S.op("pool", lambda e, lg=lg, sl=sl, hh_=hs[sl], ci=ci, nk=nk: e.tensor_tensor(out=lg[:nk, sl * nq:(sl + 1) * nq], in0=lg[:nk, sl * nq:(sl + 1) * nq], in1=Bm[:nk, hh_, ci, 0:nq], op=ALU.add),
                                     reads=[lg.b, Bm.b], writes=[lg.b])
                        stop_check(2.34)
                        pT = st["pT"][rr.get("pT", 0) % 2]; rr["pT"] = rr.get("pT", 0) + 1
                        S.op("act", lambda e, lg=lg, pT=pT, nk=nk: e.activation(pT[:nk, 0:4 * nq], lg[:nk, 0:4 * nq], AF.Exp), reads=[lg.b], writes=[pT.b])
                        stop_check(2.36)
                        vap, vb = blk["V"](g)
                        S.op("pe", lambda e, g=g, pT=pT, vap=vap, nk=nk, bi=bi: e.matmul(oacc[g][0:65, 0:4 * nq], vap, pT[:nk, 0:4 * nq], start=(bi == 0), stop=(bi == nb - 1)),
                             reads=[vb, pT.b], writes=[oacc[g].b], add=(bi > 0))
                    kpos += nk
                stop_check(2.4)
                for g in range(2):
                    den = st["den"]
                    S.op("act", lambda e, g=g: e.activation(den[64:65, 0:4 * nq], oacc[g][64:65, 0:4 * nq], AF.Copy), reads=[oacc[g].b], writes=[den.b])
                    pb_ = psr("mk", [2, 3])
                    S.op("pe", lambda e, pb_=pb_: e.matmul(pb_[0:64, 0:4 * nq], ones_f[64:65, 0:64], den[64:65, 0:4 * nq], start=True, stop=True),
                         reads=[ones_f.b, den.b], writes=[pb_.b])
                    rec = st["rec"]
                    S.op("dve", lambda e, pb_=pb_: e.reciprocal(rec[0:64, 0:4 * nq], pb_[0:64, 0:4 * nq]), reads=[pb_.b], writes=[rec.b])
                    for half in range(2):
                        S.op("dve", lambda e, g=g, half=half: e.tensor_tensor(out=o_attnT[:, 4 * g + half:4 * g + 4:2, qtok0:qtok0 + nq],
                                                                              in0=oacc[g][0:64, 0:4 * nq].rearrange("p (h q) -> p h q", h=4)[:, 2 * half:2 * half + 2, :],
                                                                              in1=rec[0:64, 0:4 * nq].rearrange("p (h q) -> p h q", h=4)[:, 2 * half:2 * half + 2, :], op=ALU.mult),
                             reads=[oacc[g].b, rec.b], writes=[o_attnT.b], add=True)

            with ExitStack() as p3:
                st = dict(
                    rt=[sb(p3, [128, 512], F32, "rt") for _ in range(2)],
                    cnt=sb(p3, [128, NBIS], F32, "cnt"), lo=sb(p3, [128, 1], F32, "lo"), W=sb(p3, [128, 1], F32, "W"),
                    mid=sb(p3, [128, 1], F32, "mid"), tt=sb(p3, [128, 1], F32, "tt"), mn=sb(p3, [128, 1], F32, "mn"), mx=sb(p3, [128, 1], F32, "mx"),
                    addm=[sb(p3, [128, 128], F32, "addm") for _ in range(2)],
                    lg=[sb(p3, [128, 512], F32, "lg") for _ in range(2)],
                    pT=[sb(p3, [128, 512], BF16, "pT") for _ in range(2)],
                    den=sb(p3, [65, 512], F32, "den"), rec=sb(p3, [64, 512], F32, "rec"),
                )
                with ExitStack() as p3a:
                    S_p = sb(p3a, [128, 2048], F32, "S_p")
                    st["junk"] = sb(p3a, [128, 2048], BF16, "junkp")
                    for t in DBG['tblocks']:
                        blocks = []
                        for j in range(t + 1):
                            kind = "near0" if j == t else ("near1" if j == t - 1 else "far")
                            blocks.append(dict(nk=128, kind=kind,
                                               KT=lambda g, j=j: (kTd[:, g, j * 128:(j + 1) * 128], kTd.b),
                                               V=lambda g, j=j: (Vx[:, j, g, :], Vx.b)))
                        attention(st, blocks, 128, 128 * t, wi_sb[:, t, :], 128 * (t + 1), t >= 2, S_p,
                                  lambda k0, n: (kiTd[:, k0:k0 + n], kiTd.b))
                    S.barrier(flush=True)
                S.barrier()
                stop_check(2.5)
                with ExitStack() as p3b:
                    S_s = sb(p3b, [8, 8208], F32, "S_s")
                    st["junk"] = sb(p3b, [8, 8208], BF16, "junks")
                    kiT_s = sb(p3b, [128, 8200], BF16, "kiT_s")
                    ptb = sb(p3b, [128, 64], I32, "ptb")
                    idx = sb(p3b, [128, 64], I32, "idx")
                    kst = [sb(p3b, [128, 64], F32, "kst") for _ in range(3)]
                    kstd = [sb(p3b, [128, 2, 64], F32, "kstd") for _ in range(3)]
                    Kdd = [sb(p3b, [128, 2, 2, 64], F32, "Kdd") for _ in range(3)]
                    Kst = [sb(p3b, [128, 128], F32, "Kst") for _ in range(3)]
                    Vst = [sb(p3b, [128, 128], F32, "Vst") for _ in range(3)]
                    KTb = [sb(p3b, [128, 2, 128], BF16, "KTb") for _ in range(3)]
                    Vxb = [sb(p3b, [128, 2, 65], BF16, "Vxb") for _ in range(3)]
                    for vx in Vxb:
                        S.op("pool", lambda e, vx=vx: e.memset(vx[:], 1.0), writes=[vx.b])
                    for b in DBG['batches']:
                        src = bass.AP(ptab.tensor, b * 64, [[0, 128], [1, 64]])
                        S.dma("sp", lambda e, src=src: e.dma_start(out=ptb[:], in_=src), writes=[ptb.b])
                        S.op("dve", lambda e: e.tensor_scalar(idx[:], ptb[:], 128.0, iotap[:, 0:1], op0=ALU.mult, op1=ALU.add), reads=[ptb.b, iotap.b], writes=[idx.b])
                        for pg in range(64):
                            ks = kst[pg % 3]
                            S.dma("pool", lambda e, ks=ks, pg=pg: e.indirect_dma_start(out=ks[:], out_offset=None, in_=cki[:, :], in_offset=bass.IndirectOffsetOnAxis(ap=idx[:, pg:pg + 1], axis=0)),
                                  reads=[idx.b], writes=[ks.b])
                            ksd = kstd[pg % 3]
                            S.op("dve", lambda e, ks=ks, ksd=ksd: e.tensor_copy(ksd[:, :, :], ks[:].unsqueeze(1).to_broadcast([128, 2, 64])), reads=[ks.b], writes=[ksd.b])
                            pm = psr("ix", [0, 1])
                            S.op("pe", lambda e, pm=pm, ksd=ksd: e.transpose(pm[:, 0:128], ksd[:].rearrange("p r d -> p (r d)"), ident[:]),
                                 reads=[ksd.b, ident.b], writes=[pm.b])
                            evac(kiT_s[:, pg * 128:(pg + 1) * 128], kiT_s.b, pm[:, 0:128], pm.b)
                        S.op("dve", lambda e, b=b: e.tensor_copy(kiT_s[:, 8192:8200], kiTd[:, 2048 + 8 * b:2056 + 8 * b]), reads=[kiTd.b], writes=[kiT_s.b], add=True)
                        blocks = []
                        for pg in range(64):
                            def prep(pg=pg):
                                Ks = Kst[pg % 3]; Vs = Vst[pg % 3]; KT_ = KTb[pg % 3]; Vb = Vxb[pg % 3]
                                S.dma("pool", lambda e: e.indirect_dma_start(out=Ks[:], out_offset=None, in_=ck[:, :], in_offset=bass.IndirectOffsetOnAxis(ap=idx[:, pg:pg + 1], axis=0)),
                                      reads=[idx.b], writes=[Ks.b])
                                S.dma("pool", lambda e: e.indirect_dma_start(out=Vs[:], out_offset=None, in_=cvv[:, :], in_offset=bass.IndirectOffsetOnAxis(ap=idx[:, pg:pg + 1], axis=0)),
                                      reads=[idx.b], writes=[Vs.b])
                                Kd = Kdd[pg % 3]
                                S.op("dve", lambda e: e.tensor_copy(Kd[:, :, :, :], Ks[:].rearrange("p (g d) -> p g d", g=2).unsqueeze(2).to_broadcast([128, 2, 2, 64])), reads=[Ks.b], writes=[Kd.b])
                                for g in range(2):
                                    pm = psr("ix", [0, 1])
                                    S.op("pe", lambda e, pm=pm, g=g: e.transpose(pm[:, 0:128], Kd[:, g, :, :].rearrange("p r d -> p (r d)"), ident[:]),
                                         reads=[Kd.b, ident.b], writes=[pm.b])
                                    evac(KT_[:, g, :], KT_.b, pm[:, 0:128], pm.b)
                                S.op("dve", lambda e: e.tensor_copy(Vb[:, :, 0:64], Vs[:].rearrange("p (g d) -> p g d", g=2)), reads=[Vs.b], writes=[Vb.b], add=True)
                            blocks.append(dict(nk=128, kind=("near1" if pg == 63 else "far"), prep=prep,
                                               KT=lambda g, pg=pg: (KTb[pg % 3][:, g, :], KTb[pg % 3].b),
                                               V=lambda g, pg=pg: (Vxb[pg % 3][:, g, :], Vxb[pg % 3].b)))
                        blocks.append(dict(nk=8, kind="near0",
                                           KT=lambda g, b=b: (kTd[:, g, 2048 + 8 * b:2056 + 8 * b], kTd.b),
                                           V=lambda g, b=b: (Vx[0:8, 16 + b, g, :], Vx.b)))
                        attention(st, blocks, 8, 2048 + 8 * b, wi_sb[0:8, 16 + b, :], 8200, True, S_s,
                                  lambda k0, n: (kiT_s[:, k0:k0 + n], kiT_s.b))
                    S.barrier(flush=True)
            att_es.close()
            S.barrier()

            stop_check(3)
            if DBG.get('ydump') == 'oc':
                for cc in range(4):
                    dsto = bass.AP(y_p.tensor, cc * 128 * 2048, [[2048, 128], [1, 2048]])
                    S.dma("pool", lambda e, cc=cc, dsto=dsto: e.dma_start(out=dsto, in_=o_convT[:, cc, 0:2048]), reads=[o_convT.b])
                for h in range(8):
                    dsto = bass.AP(y_p.tensor, (512 + h * 64) * 2048, [[2048, 64], [1, 2048]])
                    S.dma("pool", lambda e, h=h, dsto=dsto: e.dma_start(out=dsto, in_=o_attnT[:, h, 0:2048]), reads=[o_attnT.b])
                S.barrier()
            stop_check(3.5)
            def bc_rows(dst, nrows, lhsT_ap, lhs_b, rhs_fn, rhs_b):
                for hf in range(2):
                    pm = psr("bc", [0, 1])
                    S.op("pe", lambda e, pm=pm, hf=hf: e.matmul(pm[:nrows, :], lhsT_ap, rhs_fn(hf), start=True, stop=True), reads=[lhs_b, rhs_b], writes=[pm.b])
                    S.op("act", lambda e, pm=pm, hf=hf: e.activation(dst[:nrows, hf * 512:(hf + 1) * 512], pm[:nrows, :], AF.Copy), reads=[pm.b], writes=[dst.b], add=True)

            def layer_norm(stk, z, nt, lg_bc, lb_bc, out_t):
                s1 = stk["s1"]; s2 = stk["s2"]; jk = stk["jk"]
                S.op("dve", lambda e: e.memset(s1[:], 0.0), writes=[s1.b])
                S.op("dve", lambda e: e.memset(s2[:], 0.0), writes=[s2.b])
                S.op("act", lambda e: e.activation(jk[:nt, :], z[:nt, :], AF.Identity, accum_out=s1[:nt, 0:1]), reads=[z.b, s1.b], writes=[jk.b, s1.b])
                S.op("act", lambda e: e.activation(jk[:nt, :], z[:nt, :], AF.Square, accum_out=s2[:nt, 0:1]), reads=[z.b, s2.b], writes=[jk.b, s2.b])
                mu = stk["mu"]; var = stk["var"]; rstd = stk["rstd"]
                S.op("dve", lambda e: e.tensor_scalar(mu[:nt, :], s1[:nt, :], 1.0 / D, None, op0=ALU.mult), reads=[s1.b], writes=[mu.b])
                S.op("dve", lambda e: e.tensor_tensor(out=var[:nt, :], in0=mu[:nt, :], in1=mu[:nt, :], op=ALU.mult), reads=[mu.b], writes=[var.b])
                S.op("dve", lambda e: e.scalar_tensor_tensor(out=var[:nt, :], in0=s2[:nt, :], scalar=1.0 / D, in1=var[:nt, :], op0=ALU.mult, op1=ALU.subtract),
                     reads=[s2.b, var.b], writes=[var.b])
                S.op("dve", lambda e: e.tensor_scalar(var[:nt, :], var[:nt, :], LN_EPS, None, op0=ALU.add), reads=[var.b], writes=[var.b])
                S.op("act", lambda e: e.activation(rstd[:nt, :], var[:nt, :], AF.Sqrt), reads=[var.b], writes=[rstd.b])
                S.op("dve", lambda e: e.reciprocal(rstd[:nt, :], rstd[:nt, :]), reads=[rstd.b], writes=[rstd.b])
                S.op("dve", lambda e: e.tensor_scalar(out_t[:nt, :], z[:nt, :], mu[:nt, 0:1], rstd[:nt, 0:1], op0=ALU.subtract, op1=ALU.mult),
                     reads=[z.b, mu.b, rstd.b], writes=[out_t.b])
                S.op("dve", lambda e: e.tensor_tensor(out=out_t[:nt, :], in0=out_t[:nt, :], in1=lg_bc[:nt, :], op=ALU.mult), reads=[out_t.b, lg_bc.b], writes=[out_t.b])
                S.op("dve", lambda e: e.tensor_tensor(out=out_t[:nt, :], in0=out_t[:nt, :], in1=lb_bc[:nt, :], op=ALU.add), reads=[out_t.b, lb_bc.b], writes=[out_t.b])

            LNT = [(128 * t, 128) for t in range(16)] + [(2048, 32)]
            SAMP_SEGS = [(8 * b, 8, 1 + b) for b in range(4)]

            with ExitStack() as p4:
                lnrow = sb(p4, [1, 2, D], F32, "lnrow")
                S.dma("sp", lambda e: e.dma_start(out=lnrow[:, 0, :], in_=ln1_g[:, :]), writes=[lnrow.b])
                S.dma("sp", lambda e: e.dma_start(out=lnrow[:, 1, :], in_=ln1_b[:, :]), writes=[lnrow.b], add=True)
                lg1 = sb(p4, [128, D], F32, "lg1"); lb1 = sb(p4, [128, D], F32, "lb1")
                g1p = sb(p4, [128, D], F32, "g1p"); g1s = sb(p4, [32, D], F32, "g1s")
                bc_rows(lg1, 128, ones_f[0:1, :], ones_f.b, lambda hf: lnrow[0:1, 0, hf * 512:(hf + 1) * 512], lnrow.b)
                bc_rows(lb1, 128, ones_f[0:1, :], ones_f.b, lambda hf: lnrow[0:1, 1, hf * 512:(hf + 1) * 512], lnrow.b)
                bc_rows(g1p, 128, selp_sb[:, :], selp_sb.b, lambda hf: modrow_g[:, 0, hf * 512:(hf + 1) * 512], modrow_g.b)
                bc_rows(g1s, 32, sels_sb[:, :], sels_sb.b, lambda hf: modrow_g[:, 0, hf * 512:(hf + 1) * 512], modrow_g.b)
                woa = sb(p4, [64, 8, D], BF16, "woa"); woc = sb(p4, [128, 4, D], BF16, "woc"); wout = sb(p4, [128, 8, D], BF16, "wout")
                S.dma("pool", lambda e: e.dma_start(out=woa[:], in_=w_o_attn.rearrange("(h p) n -> p h n", p=64)), writes=[woa.b])
                S.dma("pool", lambda e: e.dma_start(out=woc[:], in_=w_o_conv.rearrange("(c p) n -> p c n", p=128)), writes=[woc.b])
                S.dma("pool", lambda e: e.dma_start(out=wout[:], in_=w_out.rearrange("(c p) n -> p c n", p=128)), writes=[wout.b])
                wgs = [sb(p4, [128, 8, 2, 128], BF16, "wg") for _ in range(2)]
                h1g = sb(p4, [128, 8, 512], BF16, "h1g")
                mT = sb(p4, [128, 8, 512], BF16, "mT")
                xts = [sb(p4, [128, D], F32, "xt4") for _ in range(2)]
                sg = [sb(p4, [128, 512], F32, "sg") for _ in range(2)]
                m1 = [sb(p4, [128, 512], F32, "m1") for _ in range(2)]
                zt = [sb(p4, [128, D], F32, "zt") for _ in range(1)]
                x1t = [sb(p4, [128, D], F32, "x1t") for _ in range(2)]
                lnst = dict(s1=sb(p4, [128, 1], F32, "s1"), s2=sb(p4, [128, 1], F32, "s2"), jk=sb(p4, [128, D], BF16, "jk"),
                            mu=sb(p4, [128, 1], F32, "mu"), var=sb(p4, [128, 1], F32, "var"), rstd=sb(p4, [128, 1], F32, "rstd"))
                for (g0, gn) in GROUPS:
                    if g0 < 2048:
                        tl = [(g0 + 128 * i, 128) for i in range(4)]
                        for (tok0, nt) in tl:
                            load_h_T(xts, h1g, h1g.b, tok0 - g0, tok0, nt, 0, 0, 8, xp[tok0:tok0 + nt, :])
                    else:
                        tl = [(2048, 32)]
                        load_h_T(xts, h1g, h1g.b, 0, 2048, 32, SAMP_SEGS, 0, 8, xs[:, :])
                    for j in range(8):
                        wg = wgs[j % 2]
                        for k2 in range(2):
                            S.dma("pool", lambda e, wg=wg, k2=k2, j=j: e.dma_start(out=wg[:, :, k2, :], in_=w_in_r[:, :, 2628 + 1024 * k2 + 128 * j:2628 + 1024 * k2 + 128 * (j + 1)]),
                                  writes=[wg.b], add=(k2 > 0))
                        pa1 = psr("mg", [2, 3, 4, 5]); pa2 = psr("mg", [2, 3, 4, 5]); pga = psr("mg", [2, 3, 4, 5]); pgb = psr("mg", [2, 3, 4, 5])
                        for h in range(8):
                            S.op("pe", lambda e, h=h, j=j, pa1=pa1, g0=g0, gn=gn: e.matmul(pa1[:, 0:gn], woa[:, h, j * 128:(j + 1) * 128], o_attnT[:, h, g0:g0 + gn], start=(h == 0), stop=(h == 7)),
                                 reads=[woa.b, o_attnT.b], writes=[pa1.b], add=(h > 0))
                        for cc in range(4):
                            S.op("pe", lambda e, cc=cc, j=j, pa2=pa2, g0=g0, gn=gn: e.matmul(pa2[:, 0:gn], woc[:, cc, j * 128:(j + 1) * 128], o_convT[:, cc, g0:g0 + gn], start=(cc == 0), stop=(cc == 3)),
                                 reads=[woc.b, o_convT.b], writes=[pa2.b], add=(cc > 0))
                        for k2, pg_ in enumerate((pga, pgb)):
                            for kc in range(8):
                                S.op("pe", lambda e, kc=kc, k2=k2, pg_=pg_, wg=wg, gn=gn: e.matmul(pg_[:, 0:gn], wg[:, kc, k2, :], h1g[:, kc, 0:gn], start=(kc == 0), stop=(kc == 7)),
                                     reads=[wg.b, h1g.b], writes=[pg_.b], add=(kc > 0))
                        sa = sg[0]; sb_ = sg[1]; ma = m1[0]; mb = m1[1]
                        S.op("act", lambda e, pga=pga, sa=sa, gn=gn: e.activation(sa[:, 0:gn], pga[:, 0:gn], AF.Sigmoid), reads=[pga.b], writes=[sa.b])
                        S.op("act", lambda e, pgb=pgb, sb_=sb_, gn=gn: e.activation(sb_[:, 0:gn], pgb[:, 0:gn], AF.Sigmoid), reads=[pgb.b], writes=[sb_.b])
                        S.op("dve", lambda e, sa=sa, pa1=pa1, ma=ma, gn=gn: e.tensor_tensor(out=ma[:, 0:gn], in0=sa[:, 0:gn], in1=pa1[:, 0:gn], op=ALU.mult), reads=[sa.b, pa1.b], writes=[ma.b])
                        S.op("dve", lambda e, sb_=sb_, pa2=pa2, mb=mb, gn=gn: e.tensor_tensor(out=mb[:, 0:gn], in0=sb_[:, 0:gn], in1=pa2[:, 0:gn], op=ALU.mult), reads=[sb_.b, pa2.b], writes=[mb.b])
                        S.op("dve", lambda e, ma=ma, mb=mb, j=j, gn=gn: e.tensor_tensor(out=mT[:, j, 0:gn], in0=ma[:, 0:gn], in1=mb[:, 0:gn], op=ALU.add), reads=[ma.b, mb.b], writes=[mT.b], add=True)
                    for (tok0, nt) in tl:
                        c0 = tok0 - g0
                        xt = xts[rr.get("xt", 0) % 2]; rr["xt"] = rr.get("xt", 0) + 1
                        src = xp[tok0:tok0 + nt, :] if tok0 < 2048 else xs[:, :]
                        S.dma("sp", lambda e, xt=xt, nt=nt, src=src: e.dma_start(out=xt[:nt, :], in_=src), writes=[xt.b])
                        z = zt[0]; rr["zt"] = rr.get("zt", 0) + 1
                        gbc = g1p if tok0 < 2048 else g1s
                        for hf in range(2):
                            po = psr("mo", [6, 7])
                            for kc in range(8):
                                S.op("pe", lambda e, kc=kc, po=po, hf=hf, c0=c0, nt=nt: e.matmul(po[:nt, :], mT[:, kc, c0:c0 + nt], wout[:, kc, hf * 512:(hf + 1) * 512], start=(kc == 0), stop=(kc == 7)),
                                     reads=[mT.b, wout.b], writes=[po.b], add=(kc > 0))
                            S.op("dve", lambda e, po=po, z=z, hf=hf, nt=nt, gbc=gbc: e.tensor_tensor(out=z[:nt, hf * 512:(hf + 1) * 512], in0=po[:nt, :], in1=gbc[:nt, hf * 512:(hf + 1) * 512], op=ALU.mult),
                                 reads=[po.b, gbc.b], writes=[z.b], add=True)
                        S.op("dve", lambda e, z=z, xt=xt, nt=nt: e.scalar_tensor_tensor(out=z[:nt, :], in0=xt[:nt, :], scalar=ALPHA, in1=z[:nt, :], op0=ALU.mult, op1=ALU.add),
                             reads=[xt.b, z.b], writes=[z.b])
                        x1 = x1t[rr.get("x1", 0) % 2]; rr["x1"] = rr.get("x1", 0) + 1
                        if DBG.get('ydump') == 'z' and tok0 == 0:
                            S.dma("sp", lambda e, z=z: e.dma_start(out=y_p[0:128, :], in_=z[:, :]), reads=[z.b])
                            S.dma("sp", lambda e: e.dma_start(out=y_p[128:256, :], in_=g1p[:, :]), reads=[g1p.b])
                            S.dma("sp", lambda e: e.dma_start(out=y_p[256:384, :], in_=lg1[:, :]), reads=[lg1.b])
                            S.dma("sp", lambda e: e.dma_start(out=y_p[512:640, :], in_=lb1[:, :]), reads=[lb1.b])
                            S.dma("sp", lambda e, xt=xt: e.dma_start(out=y_p[640:768, :], in_=xt[:, :]), reads=[xt.b])
                        layer_norm(lnst, z, nt, lg1, lb1, x1)
                        if DBG.get('ydump') == 'z' and tok0 == 0:
                            S.dma("sp", lambda e, x1=x1: e.dma_start(out=y_p[768:896, :], in_=x1[:, :]), reads=[x1.b])
                        S.dma("sp", lambda e, x1=x1, nt=nt, tok0=tok0: e.dma_start(out=x1_scr[tok0:tok0 + nt, :], in_=x1[:nt, :]), reads=[x1.b], writes=[b_x1scr], sembuf=x1.b, add=True)
                        if DBG.get('ydump') == 'x1':
                            dstx = y_p[tok0:tok0 + nt, :] if tok0 < 2048 else y_s[:, :]
                            S.dma("sp", lambda e, x1=x1, nt=nt, dstx=dstx: e.dma_start(out=dstx, in_=x1[:nt, :]), reads=[x1.b])
                S.barrier(flush=True)
            act_es.close()
            S.barrier()

            stop_check(4)
            with ExitStack() as p5:
                lnrow = sb(p5, [1, 2, D], F32, "lnrow2")
                S.dma("sp", lambda e: e.dma_start(out=lnrow[:, 0, :], in_=ln2_g[:, :]), writes=[lnrow.b])
                S.dma("sp", lambda e: e.dma_start(out=lnrow[:, 1, :], in_=ln2_b[:, :]), writes=[lnrow.b], add=True)
                lg2 = sb(p5, [128, D], F32, "lg2"); lb2 = sb(p5, [128, D], F32, "lb2")
                g2p = sb(p5, [128, D], F32, "g2p"); g2s = sb(p5, [32, D], F32, "g2s")
                bc_rows(lg2, 128, ones_f[0:1, :], ones_f.b, lambda hf: lnrow[0:1, 0, hf * 512:(hf + 1) * 512], lnrow.b)
                bc_rows(lb2, 128, ones_f[0:1, :], ones_f.b, lambda hf: lnrow[0:1, 1, hf * 512:(hf + 1) * 512], lnrow.b)
                bc_rows(g2p, 128, selp_sb[:, :], selp_sb.b, lambda hf: modrow_g[:, 1, hf * 512:(hf + 1) * 512], modrow_g.b)
                bc_rows(g2s, 32, sels_sb[:, :], sels_sb.b, lambda hf: modrow_g[:, 1, hf * 512:(hf + 1) * 512], modrow_g.b)
                wpq = sb(p5, [128, 8, D], BF16, "wpq")
                S.dma("pool", lambda e: e.dma_start(out=wpq[:], in_=peer_wq.rearrange("(c p) n -> p c n", p=128)), writes=[wpq.b])
                iotaA = sb(p5, [128, 32, 128], BF16, "iotaA")
                S.op("pool", lambda e: e.iota(iotaA[:], [[0, 32], [1, 128]], base=0, channel_multiplier=0, allow_small_or_imprecise_dtypes=True), writes=[iotaA.b])
                x1ts = [sb(p5, [128, D], F32, "x1l") for _ in range(1)]
                h2T = sb(p5, [128, 8, 128], BF16, "h2T")
                qpT = sb(p5, [128, 8, 128], BF16, "qpT")
                Spe = sb(p5, [128, 8, 256], F32, "Spe")
                v12 = sb(p5, [128, 8, 2, 16], F32, "v12")
                i12 = sb(p5, [128, 8, 2, 16], U32, "i12")
                i12f = sb(p5, [128, 8, 2, 16], F32, "i12f")
                wk = sb(p5, [128, 256], F32, "wk")
                cand = sb(p5, [128, 8, 256], F32, "cand")
                sv = sb(p5, [128, 8, 16], F32, "sv")
                si = sb(p5, [128, 8, 16], U32, "si")
                sij = sb(p5, [128, 2, 8, 16], U32, "sij")
                sijf = sb(p5, [128, 2, 8, 16], F32, "sijf")
                eq = sb(p5, [128, 16, 16], F32, "eq")
                abw = sb(p5, [128, 3, 128], F32, "abw")
                zs = sb(p5, [128, 8], F32, "zs")
                abwT = sb(p5, [128, 3, 128], F32, "abwT")
                OA = sb(p5, [128, 32, 128], BF16, "OA"); OB = sb(p5, [128, 32, 128], BF16, "OB")
                GT = sb(p5, [128, 128, 128], BF16, "GT")
                utb = [sb(p5, [128, 4, 1024], BF16, "utl") for _ in range(2)]
                vtb = [sb(p5, [128, 4, 1024], BF16, "vtl") for _ in range(2)]
                xs_ = [sb(p5, [128, 512], F32, "gx") for _ in range(2)]
                us_ = [sb(p5, [128, 512], F32, "gu") for _ in range(2)]
                ws_ = [sb(p5, [128, 512], F32, "gw") for _ in range(2)]
                PTs = [sb(p5, [128, 512], BF16, "PT") for _ in range(2)]
                zt = [sb(p5, [128, D], F32, "zt5") for _ in range(1)]
                yt = [sb(p5, [128, D], F32, "yt") for _ in range(1)]
                lnst = dict(s1=sb(p5, [128, 1], F32, "s1"), s2=sb(p5, [128, 1], F32, "s2"), jk=sb(p5, [128, D], BF16, "jk"),
                            mu=sb(p5, [128, 1], F32, "mu"), var=sb(p5, [128, 1], F32, "var"), rstd=sb(p5, [128, 1], F32, "rstd"))
                ut_r = ut_scr
                v_r = v_scr
                for (tok0, nt) in LNT:
                    x1l = x1ts[0]; rr["x1l"] = rr.get("x1l", 0) + 1
                    segs = 0 if tok0 < 2048 else SAMP_SEGS
                    load_h_T([x1l], h2T, h2T.b, 0, tok0, nt, segs, 24, 32, x1_scr[tok0:tok0 + nt, :])
                    rr["xt"] -= 1
                    for hd in range(8):
                        pm = psr("pq", [0, 1])
                        for kc in range(8):
                            S.op("pe", lambda e, kc=kc, pm=pm, hd=hd, nt=nt: e.matmul(pm[:, 0:nt], wpq[:, kc, hd * 128:(hd + 1) * 128], h2T[:, kc, 0:nt], start=(kc == 0), stop=(kc == 7)),
                                 reads=[wpq.b, h2T.b], writes=[pm.b], add=(kc > 0))
                        evac(qpT[:, hd, 0:nt], qpT.b, pm[:, 0:nt], pm.b)
                    for hd in range(8):
                        pm = psr("pq", [0, 1])
                        S.op("pe", lambda e, pm=pm, hd=hd, nt=nt: e.matmul(pm[:nt, 0:256], qpT[:, hd, 0:nt], KBD[:, :], start=True, stop=True), reads=[qpT.b, KBD.b], writes=[pm.b])
                        evac(Spe[:nt, hd, :], Spe.b, pm[:nt, 0:256], pm.b)
                    for hd in range(8):
                        for hf in range(2):
                            src = Spe[:nt, hd, hf * 128:(hf + 1) * 128]
                            S.op("dve", lambda e, src=src, hd=hd, hf=hf, nt=nt: e.max(out=v12[:nt, hd, hf, 0:8], in_=src), reads=[Spe.b], writes=[v12.b], add=True)
                            S.op("dve", lambda e, src=src, hd=hd, hf=hf, nt=nt: e.match_replace(out=wk[:nt, 0:128], in_to_replace=v12[:nt, hd, hf, 0:8], in_values=src, imm_value=-1e30),
                                 reads=[Spe.b, v12.b], writes=[wk.b])
                            S.op("dve", lambda e, hd=hd, hf=hf, nt=nt: e.max(out=v12[:nt, hd, hf, 8:16], in_=wk[:nt, 0:128]), reads=[wk.b], writes=[v12.b], add=True)
                            S.op("dve", lambda e, src=src, hd=hd, hf=hf, nt=nt: e.max_index(out=i12[:nt, hd, hf, 0:8], in_max=v12[:nt, hd, hf, 0:8], in_values=src),
                                 reads=[Spe.b, v12.b], writes=[i12.b], add=True)
                            S.op("dve", lambda e, src=src, hd=hd, hf=hf, nt=nt: e.max_index(out=i12[:nt, hd, hf, 8:16], in_max=v12[:nt, hd, hf, 8:16], in_values=src),
                                 reads=[Spe.b, v12.b], writes=[i12.b], add=True)
                    S.op("dve", lambda e, nt=nt: e.tensor_copy(i12f[:nt].rearrange("p h f k -> p (h f k)"), i12[:nt].rearrange("p h f k -> p (h f k)")), reads=[i12.b], writes=[i12f.b])
                    for hd in range(8):
                        S.op("dve", lambda e, hd=hd, nt=nt: e.tensor_tensor(out=cand[:nt, hd, :].rearrange("p (i j) -> p i j", i=16),
                                                                           in0=v12[:nt, hd, 0, :].unsqueeze(2).to_broadcast([nt, 16, 16]),
                                                                           in1=v12[:nt, hd, 1, :].unsqueeze(1).to_broadcast([nt, 16, 16]), op=ALU.add),
                             reads=[v12.b], writes=[cand.b], add=True)
                    for hd in range(8):
                        src = cand[:nt, hd, :]
                        S.op("dve", lambda e, src=src, hd=hd, nt=nt: e.max(out=sv[:nt, hd, 0:8], in_=src), reads=[cand.b], writes=[sv.b], add=True)
                        S.op("dve", lambda e, src=src, hd=hd, nt=nt: e.match_replace(out=wk[:nt, :], in_to_replace=sv[:nt, hd, 0:8], in_values=src, imm_value=-1e30),
                             reads=[cand.b, sv.b], writes=[wk.b])
                        S.op("dve", lambda e, hd=hd, nt=nt: e.max(out=sv[:nt, hd, 8:16], in_=wk[:nt, :]), reads=[wk.b], writes=[sv.b], add=True)
                        S.op("dve", lambda e, src=src, hd=hd, nt=nt: e.max_index(out=si[:nt, hd, 0:8], in_max=sv[:nt, hd, 0:8], in_values=src), reads=[cand.b, sv.b], writes=[si.b], add=True)
                        S.op("dve", lambda e, src=src, hd=hd, nt=nt: e.max_index(out=si[:nt, hd, 8:16], in_max=sv[:nt, hd, 8:16], in_values=src), reads=[cand.b, sv.b], writes=[si.b], add=True)
                    wv = abw[:nt, 2, :].rearrange("p (h k) -> p h k", h=8)
                    S.op("dve", lambda e, nt=nt, wv=wv: e.tensor_tensor(out=wv, in0=sv[:nt, :, :], in1=sv[:nt, :, 0:1].to_broadcast([nt, 8, 16]), op=ALU.subtract),
                         reads=[sv.b], writes=[abw.b], add=True)
                    S.op("act", lambda e, nt=nt: e.activation(abw[:nt, 2, :], abw[:nt, 2, :], AF.Exp), reads=[abw.b], writes=[abw.b])
                    S.op("dve", lambda e, nt=nt, wv=wv: e.tensor_reduce(out=zs[:nt, :], in_=wv, axis=AX.X, op=ALU.add), reads=[abw.b], writes=[zs.b])
                    S.op("dve", lambda e, nt=nt: e.reciprocal(zs[:nt, :], zs[:nt, :]), reads=[zs.b], writes=[zs.b])
                    S.op("dve", lambda e, nt=nt, wv=wv: e.tensor_tensor(out=wv, in0=wv, in1=zs[:nt, :].unsqueeze(2).to_broadcast([nt, 8, 16]), op=ALU.mult),
                         reads=[abw.b, zs.b], writes=[abw.b])
                    S.op("dve", lambda e, nt=nt: e.tensor_single_scalar(sij[:nt, 0].rearrange("p h k -> p (h k)"), si[:nt].rearrange("p h k -> p (h k)"), 4, op=ALU.logical_shift_right),
                         reads=[si.b], writes=[sij.b], add=True)
                    S.op("dve", lambda e, nt=nt: e.tensor_single_scalar(sij[:nt, 1].rearrange("p h k -> p (h k)"), si[:nt].rearrange("p h k -> p (h k)"), 15, op=ALU.bitwise_and),
                         reads=[si.b], writes=[sij.b], add=True)
                    S.op("dve", lambda e, nt=nt: e.tensor_copy(sijf[:nt].rearrange("p t h k -> p (t h k)"), sij[:nt].rearrange("p t h k -> p (t h k)")), reads=[sij.b], writes=[sijf.b])
                    for hd in range(8):
                        for ab in range(2):
                            S.op("dve", lambda e, hd=hd, ab=ab, nt=nt: e.tensor_tensor(out=eq[:nt], in0=sijf[:nt, ab, hd, :].unsqueeze(2).to_broadcast([nt, 16, 16]),
                                                                                     in1=iota16[:nt, :].unsqueeze(1).to_broadcast([nt, 16, 16]), op=ALU.is_equal),
                                 reads=[sijf.b, iota16.b], writes=[eq.b])
                            S.op("dve", lambda e, hd=hd, ab=ab, nt=nt: e.tensor_tensor(out=eq[:nt], in0=eq[:nt], in1=i12f[:nt, hd, ab, :].unsqueeze(1).to_broadcast([nt, 16, 16]), op=ALU.mult),
                                 reads=[eq.b, i12f.b], writes=[eq.b])
                            S.op("dve", lambda e, hd=hd, ab=ab, nt=nt: e.tensor_reduce(out=abw[:nt, ab, hd * 16:(hd + 1) * 16], in_=eq[:nt], axis=AX.X, op=ALU.add),
                                 reads=[eq.b], writes=[abw.b], add=True)
                    pm = psr("pq", [0, 1])
                    for k3 in range(3):
                        S.op("pe", lambda e, pm=pm, k3=k3, nt=nt: e.transpose(pm[:, k3 * 128:k3 * 128 + nt], abw[:nt, k3, :], ident[:nt, :nt]), reads=[abw.b, ident.b], writes=[pm.b], add=(k3 > 0))
                    S.op("act", lambda e, pm=pm: e.activation(abwT[:].rearrange("p k n -> p (k n)"), pm[:, 0:384], AF.Copy), reads=[pm.b], writes=[abwT.b])
                    for n0 in range(0, nt, 32):
                        nn = min(32, nt - n0)
                        S.op("dve", lambda e, n0=n0, nn=nn: e.tensor_tensor(out=OA[:, 0:nn, :], in0=iotaA[:, 0:nn, :], in1=abwT[:, 0, n0:n0 + nn].unsqueeze(2).to_broadcast([128, nn, 128]), op=ALU.is_equal),
                             reads=[iotaA.b, abwT.b], writes=[OA.b])
                        S.op("pool", lambda e, n0=n0, nn=nn: e.tensor_tensor(out=OA[:, 0:nn, :], in0=OA[:, 0:nn, :], in1=abwT[:, 2, n0:n0 + nn].unsqueeze(2).to_broadcast([128, nn, 128]), op=ALU.mult),
                             reads=[OA.b, abwT.b], writes=[OA.b])
                        S.op("dve", lambda e, n0=n0, nn=nn: e.tensor_tensor(out=OB[:, 0:nn, :], in0=iotaA[:, 0:nn, :], in1=abwT[:, 1, n0:n0 + nn].unsqueeze(2).to_broadcast([128, nn, 128]), op=ALU.is_equal),
                             reads=[iotaA.b, abwT.b], writes=[OB.b])
                        for n4 in range(0, nn, 4):
                            pg_ = psr("pg", [2, 3])
                            for q in range(4):
                                nl = n4 + q
                                S.op("pe", lambda e, pg_=pg_, q=q, nl=nl: e.matmul(pg_[:, q * 128:(q + 1) * 128], OB[:, nl, :], OA[:, nl, :], start=True, stop=True),
                                     reads=[OA.b, OB.b], writes=[pg_.b], add=(q > 0))
                            evac(GT[:].rearrange("p a n -> p n a")[:, n0 + n4:n0 + n4 + 4, :], GT.b, pg_[:, :].rearrange("p (n a) -> p n a", n=4), pg_.b)
                    oacc = [PS[6], PS[7]]
                    def stage_ab(a4, nt=nt):
                        ub = utb[a4 % 2]; vb = vtb[a4 % 2]
                        S.dma("sp", lambda e, ub=ub, a4=a4: e.dma_start(out=ub[:], in_=ut_r[a4]), reads=[b_utscr], writes=[ub.b])
                        S.dma("sp", lambda e, vb=vb, a4=a4: e.dma_start(out=vb[:], in_=v_r[a4]), reads=[b_vscr], writes=[vb.b])
                        pa = PS[4 + a4 % 2]
                        for c in range(4):
                            for kc in range(8):
                                S.op("pe", lambda e, pa=pa, c=c, kc=kc, ub=ub, nt=nt: e.matmul(pa[:, c * 128:c * 128 + nt], ub[:, c, kc * 128:(kc + 1) * 128], h2T[:, kc, 0:nt], start=(kc == 0), stop=(kc == 7)),
                                     reads=[ub.b, h2T.b], writes=[pa.b], add=(c > 0 or kc > 0))
                        gx = xs_[a4 % 2]; gu = us_[a4 % 2]

                        def v3(t_, nt=nt):
                            return t_[:, :].rearrange("p (c n) -> p c n", c=4)[:, :, 0:nt]
                        pav = v3(pa); gxv = v3(gx); guv = v3(gu)
                        S.op("act", lambda e, gxv=gxv, pav=pav: e.activation(gxv, pav, AF.Copy), reads=[pa.b], writes=[gx.b])
                        S.op("act", lambda e, guv=guv, pav=pav: e.activation(guv, pav, AF.Square), reads=[pa.b], writes=[gu.b])
                        S.op("dve", lambda e, guv=guv: e.tensor_scalar(guv, guv, 0.044715, 1.0, op0=ALU.mult, op1=ALU.add), reads=[gu.b], writes=[gu.b])

                    def stage_cde(a4, nt=nt):
                        vb = vtb[a4 % 2]
                        gx = xs_[a4 % 2]; gu = us_[a4 % 2]; gw = ws_[a4 % 2]; PT = PTs[a4 % 2]

                        def v3(t_, nt=nt):
                            return t_[:, :].rearrange("p (c n) -> p c n", c=4)[:, :, 0:nt]
                        gxv = v3(gx); guv = v3(gu); gwv = v3(gw); PTv = v3(PT)
                        gtv = GT[:, a4 * 4:(a4 + 1) * 4, 0:nt]
                        S.op("pool", lambda e, guv=guv, gxv=gxv, gwv=gwv: e.tensor_tensor(out=gwv, in0=guv, in1=gxv, op=ALU.mult), reads=[gu.b, gx.b], writes=[gw.b])
                        S.op("act", lambda e, gwv=gwv: e.activation(gwv, gwv, AF.Sigmoid, scale=1.5957691216057308), reads=[gw.b], writes=[gw.b])
                        S.op("dve", lambda e, gwv=gwv, gxv=gxv: e.tensor_tensor(out=gxv, in0=gwv, in1=gxv, op=ALU.mult), reads=[gw.b, gx.b], writes=[gx.b])
                        S.op("dve", lambda e, gxv=gxv, PTv=PTv, gtv=gtv: e.tensor_tensor(out=PTv, in0=gxv, in1=gtv, op=ALU.mult),
                             reads=[gx.b, GT.b], writes=[PT.b])
                        for c in range(4):
                            for hf in range(2):
                                first = (a4 == 0 and c == 0)
                                last = (a4 == 31 and c == 3)
                                S.op("pe", lambda e, PT=PT, c=c, hf=hf, vb=vb, first=first, last=last, nt=nt: e.matmul(oacc[hf][:nt, :], PT[:, c * 128:c * 128 + nt], vb[:, c, hf * 512:(hf + 1) * 512], start=first, stop=last),
                                     reads=[PT.b, vb.b], writes=[oacc[hf].b], add=(not first))

                    stage_ab(0)
                    for a4 in range(32):
                        if a4 + 1 < 32:
                            stage_ab(a4 + 1)
                        stage_cde(a4)
                    if pe_dbg:
                        y = yt[0]
                        for hf in range(2):
                            S.op("act", lambda e, y=y, hf=hf, nt=nt: e.activation(y[:nt, hf * 512:(hf + 1) * 512], oacc[hf][:nt, :], AF.Copy), reads=[oacc[hf].b], writes=[y.b], add=True)
                        dstd = y_p[tok0:tok0 + nt, :] if tok0 < 2048 else y_s[:, :]
                        S.dma("sp", lambda e, y=y, nt=nt, dstd=dstd: e.dma_start(out=dstd, in_=y[:nt, :]), reads=[y.b])
                    z = zt[0]; rr["zt5"] = rr.get("zt5", 0) + 1
                    gbc = g2p if tok0 < 2048 else g2s
                    for hf in range(2):
                        S.op("dve", lambda e, z=z, hf=hf, nt=nt, gbc=gbc: e.tensor_tensor(out=z[:nt, hf * 512:(hf + 1) * 512], in0=oacc[hf][:nt, :], in1=gbc[:nt, hf * 512:(hf + 1) * 512], op=ALU.mult),
                             reads=[oacc[hf].b, gbc.b], writes=[z.b], add=True)
                    S.op("dve", lambda e, z=z, x1l=x1l, nt=nt: e.scalar_tensor_tensor(out=z[:nt, :], in0=x1l[:nt, :], scalar=ALPHA, in1=z[:nt, :], op0=ALU.mult, op1=ALU.add),
                         reads=[x1l.b, z.b], writes=[z.b])
                    y = yt[0]; rr["yt"] = rr.get("yt", 0) + 1
                    layer_norm(lnst, z, nt, lg2, lb2, y)
                    dst = y_p[tok0:tok0 + nt, :] if tok0 < 2048 else y_s[:, :]
                    if not DBG.get('ydump'):
                        S.dma("sp", lambda e, y=y, nt=nt, dst=dst: e.dma_start(out=dst, in_=y[:nt, :]), reads=[y.b])
                S.barrier(flush=True)
        except _Stop:
            for nm in ('att_es', 'act_es'):
                st_ = locals().get(nm)
                if st_ is not None:
                    st_.close()
        S.finish()
        print("instructions:", S.ninstr, "sems:", S.semid)
    return nc


_CACHE = {}


def kernel(**inp):
    f32 = np.float32
    g = lambda k: np.ascontiguousarray(np.asarray(inp[k]))
    if "nc" not in _CACHE:
        _CACHE["nc"] = build_program()
    nc = _CACHE["nc"]
    j = np.arange(383)
    bk = t5_bucket_np(j - 127)
    ohb = np.zeros((32, 383), f32); ohb[bk, j] = 1.0
    negrow = np.where(j < 127, NEG, 0.0).astype(f32)[None, :]
    sel_p = np.zeros((5, 128), f32); sel_p[0, :] = 1.0
    sel_s = np.zeros((5, 32), f32)
    for b in range(4):
        sel_s[1 + b, 8 * b:8 * b + 8] = 1.0
    xpr, xsm, cpr, csm = g("x_prompt"), g("x_sample"), g("c_prompt"), g("c_sample")
    pr = DBG['pool_rows']
    ck = g("cache_k")[0].reshape(2560 * 128, 128)[:pr]; cv = g("cache_v")[0].reshape(2560 * 128, 128)[:pr]
    cki = g("cache_kidx")[0].reshape(2560 * 128, 64)[:pr]
    sc = g("state_conv")[0]; pt = g("page_table")
    shared = dict(ck=ck, cv=cv, cki=cki, rel_bias=g("rel_bias"), w_ada=g("w_ada")[0], b_ada=g("b_ada"), w_in=g("w_in")[0],
                  conv_w=g("conv_w")[0], conv_b=g("conv_b"), w_o_attn=g("w_o_attn")[0], w_o_conv=g("w_o_conv")[0], w_out=g("w_out")[0],
                  ln1_g=g("ln1_g"), ln1_b=g("ln1_b"), ln2_g=g("ln2_g"), ln2_b=g("ln2_b"), peer_wq=g("peer_wq")[0],
                  peer_k1=g("peer_k1")[0], peer_k2=g("peer_k2")[0], peer_u=g("peer_u")[0], peer_v=g("peer_v")[0],
                  ohb=ohb, negrow=negrow, sel_p=sel_p, sel_s=sel_s)
    in_maps = []
    for c in range(8):
        m = dict(shared)
        m["xp"] = xpr[c]
        m["xs"] = np.ascontiguousarray(xsm[4 * c:4 * c + 4].reshape(32, D))
        m["cvec"] = np.ascontiguousarray(np.concatenate([cpr[c:c + 1], csm[4 * c:4 * c + 4]], axis=0))
        m["sconv"] = np.ascontiguousarray(sc[4 * c:4 * c + 4].reshape(8, 512))
        m["ptab"] = np.ascontiguousarray(pt[4 * c:4 * c + 4]).astype(np.int32)
        in_maps.append(m)
    res = run_bass_kernel_spmd(nc, in_maps, core_ids=list(range(8)))
    R = res.results
    y_prompt = np.stack([R[c]["y_p"] for c in range(8)])
    y_sample = np.concatenate([R[c]["y_s"].reshape(4, 8, D) for c in range(8)])
    k_prompt = np.stack([R[c]["k_p"].reshape(SEQ, 2, 64) for c in range(8)])[None]
    v_prompt = np.stack([R[c]["v_p"].reshape(SEQ, 2, 64) for c in range(8)])[None]
    ki_prompt = np.stack([R[c]["ki_p"] for c in range(8)])[None]
    conv_prompt = np.stack([R[c]["conv_o"][0:2] for c in range(8)])[None]
    k_sample = np.concatenate([R[c]["k_s"].reshape(4, 8, 2, 64) for c in range(8)])[None]
    v_sample = np.concatenate([R[c]["v_s"].reshape(4, 8, 2, 64) for c in range(8)])[None]
    ki_sample = np.concatenate([R[c]["ki_s"].reshape(4, 8, 64) for c in range(8)])[None]
    conv_sample = np.concatenate([R[c]["conv_o"][2:10].reshape(4, 2, 512) for c in range(8)])[None]
    outs = (y_prompt, y_sample, k_prompt, v_prompt, ki_prompt, conv_prompt, k_sample, v_sample, ki_sample, conv_sample)
    return tuple(np.ascontiguousarray(o, dtype=f32) for o in outs)
```

```python
import math
from contextlib import ExitStack
import numpy as np
import concourse.bass as bass
import concourse.mybir as mybir
from concourse.bass_utils import run_bass_kernel_spmd

F32 = mybir.dt.float32
BF16 = mybir.dt.bfloat16
U32 = mybir.dt.uint32
I32 = mybir.dt.int32
ALU = mybir.AluOpType
AF = mybir.ActivationFunctionType
AX = mybir.AxisListType

D = 1024
SEQ = 2048
NTOK = 2080
NEG = -30000.0
ATTN_SCALE = 64 ** -0.5
IDX_SCALE = 256 ** -0.5
ALPHA = 2 ** 0.25
LN_EPS = 1e-5
NBIS = 26
STOP_AFTER = [99]
DBG = dict(strict=True, pool_rows=2560 * 128, prep=128, tblocks=list(range(16)), batches=list(range(4)), npages=64)


class _Stop(Exception):
    pass


_DISCARD = [False]


def stop_check(k):
    if STOP_AFTER[0] <= k:
        _DISCARD[0] = True
EPOCH = 16000


class Buf:
    __slots__ = ("name", "w", "r", "dsem", "dcnt")

    def __init__(self, name):
        self.name = name
        self.w = []
        self.r = []
        self.dsem = None
        self.dcnt = 0


class Tok:
    __slots__ = ("sem", "val", "eng", "grp")

    def __init__(self, sem, val, eng, grp=False):
        self.sem, self.val, self.eng, self.grp = sem, val, eng, grp


class Sched:
    ENG = ("pe", "act", "dve", "pool", "sp")

    def __init__(self, nc, es):
        self.nc = nc
        self.es = es
        self.q = {e: [] for e in self.ENG}
        self.n = {e: 0 for e in self.ENG}
        self.esem = {e: None for e in self.ENG}
        self.waited = {e: {} for e in self.ENG}
        self.semid = 0
        self.dma_sems = []
        self.last = {}
        self.ninstr = 0

    def new_sem(self, tag):
        self.semid += 1
        return self.es.enter_context(self.nc.semaphore(f"s{self.semid}_{tag}"))

    def _deps(self, eng, reads, writes, add):
        strict = DBG.get("strict")
        pe_chain = (eng == "pe" and add)
        deps = []
        for b in reads:
            for t in b.w:
                if t.eng == eng and eng == "pe" and not strict:
                    continue
                deps.append(t)
        for b in writes:
            for t in b.r:
                if t.eng == eng and t.eng is not None and (not strict or pe_chain):
                    continue
                deps.append(t)
            for t in b.w:
                if add and t.grp:
                    continue
                if t.eng == eng and t.eng is not None and (not strict or pe_chain):
                    continue
                deps.append(t)
        best = {}
        for t in deps:
            k = id(t.sem)
            if k not in best or best[k].val < t.val:
                best[k] = t
        out = []
        wd = self.waited[eng]
        for k, t in best.items():
            if wd.get(k, -1) >= t.val:
                continue
            wd[k] = t.val
            out.append((t.sem, t.val))
        return out

    def _commit(self, tok, reads, writes, add):
        for b in reads:
            b.r = [t for t in b.r if t.sem is not tok.sem] + [tok]
        for b in writes:
            if add:
                b.w = [t for t in b.w if t.sem is not tok.sem] + [tok]
            else:
                b.w = [tok]
            b.r = []

    def op(self, eng, fn, reads=(), writes=(), add=False):
        if _DISCARD[0]:
            return
        waits = self._deps(eng, reads, writes, add)
        if self.esem[eng] is None or self.n[eng] >= EPOCH:
            self.esem[eng] = self.new_sem(eng)
            self.n[eng] = 0
        self.n[eng] += 1
        sem, val = self.esem[eng], self.n[eng]
        tok = Tok(sem, val, eng, add)
        self.last[eng] = tok
        self._commit(tok, reads, writes, add)

        def thunk(e, fn=fn, waits=waits, sem=sem):
            for (s, v) in waits:
                e.wait_ge(s, v)
            fn(e).then_inc(sem, 1)
        self.q[eng].append(thunk)
        self.ninstr += 1

    def dma(self, eng, fn, reads=(), writes=(), sembuf=None, add=False):
        if _DISCARD[0]:
            return
        waits = self._deps(eng, reads, writes, add)
        sb = sembuf if sembuf is not None else (writes[0] if writes else reads[0])
        kind = "sw" if eng == "pool" else "hw"
        if sb.dsem is None:
            sb.dsem = {}
        if kind not in sb.dsem:
            ent = [self.new_sem("d" + kind), 0]
            sb.dsem[kind] = ent
            self.dma_sems.append(ent)
        ent = sb.dsem[kind]
        ent[1] += 16
        tok = Tok(ent[0], ent[1], None, add)
        self._commit(tok, reads, writes, add)

        def thunk(e, fn=fn, waits=waits, sem=ent[0]):
            for (s, v) in waits:
                e.wait_ge(s, v)
            fn(e).then_inc(sem, 16)
        self.q[eng].append(thunk)
        self.ninstr += 1

    def barrier(self, force=False, flush=False):
        if not (_DISCARD[0] and not force):
            toks = [t for t in self.last.values()]
            toks += [Tok(ent[0], ent[1], None) for ent in self.dma_sems]
            for eng in self.ENG:
                waits = []
                wd = self.waited[eng]
                for t in toks:
                    if t.eng == eng:
                        continue
                    k = id(t.sem)
                    if wd.get(k, -1) >= t.val:
                        continue
                    wd[k] = t.val
                    waits.append((t.sem, t.val))

                def thunk(e, waits=waits):
                    for (s, v) in waits:
                        e.wait_ge(s, v)
                self.q[eng].append(thunk)
        if flush:
            self.flush()

    def flush(self):
        nc = self.nc
        q = self.q
        if not any(q[e] for e in self.ENG):
            return
        with nc.Block() as block:
            @block.tensor
            def _(e):
                for t in q["pe"]:
                    t(e)

            @block.scalar
            def _(e):
                for t in q["act"]:
                    t(e)

            @block.vector
            def _(e):
                for t in q["dve"]:
                    t(e)

            @block.gpsimd
            def _(e):
                for t in q["pool"]:
                    t(e)

            @block.sync
            def _(e):
                for t in q["sp"]:
                    t(e)
        self.q = {e: [] for e in self.ENG}

    def finish(self):
        self.barrier(force=True)
        self.flush()


class TL:
    def __init__(self, t, name):
        self.t = t
        self.b = Buf(name)

    def __getitem__(self, k):
        return self.t[k]


def t5_bucket_np(d):
    n = np.maximum(d, 0)
    nf = np.maximum(n, 1).astype(np.float32)
    large = 16 + (np.log(nf / np.float32(16)) / np.float32(math.log(128 / 16)) * np.float32(16)).astype(np.int32)
    large = np.minimum(large, 31)
    return np.where(n < 16, n, large)


def build_program():
    nc = bass.Bass("TRN2", target_bir_lowering=False)
    _DISCARD[0] = False

    def din(name, shape, dt=F32):
        return nc.dram_tensor(name, list(shape), dt, kind="ExternalInput").ap()

    def dout(name, shape, dt=F32):
        return nc.dram_tensor(name, list(shape), dt, kind="ExternalOutput").ap()

    xp = din("xp", [SEQ, D]); xs = din("xs", [32, D]); cvec = din("cvec", [5, D])
    ck = din("ck", [DBG["pool_rows"], 128]); cvv = din("cv", [DBG["pool_rows"], 128]); cki = din("cki", [DBG["pool_rows"], 64])
    sconv = din("sconv", [8, 512]); ptab = din("ptab", [4, 64], I32)
    rel_bias = din("rel_bias", [32, 8])
    w_ada = din("w_ada", [D, 6 * D]); b_ada = din("b_ada", [1, 6 * D])
    w_in = din("w_in", [D, 4676])
    conv_w = din("conv_w", [3, 512]); conv_b = din("conv_b", [1, 512])
    w_o_attn = din("w_o_attn", [512, D]); w_o_conv = din("w_o_conv", [512, D]); w_out = din("w_out", [D, D])
    ln1_g = din("ln1_g", [1, D]); ln1_b = din("ln1_b", [1, D]); ln2_g = din("ln2_g", [1, D]); ln2_b = din("ln2_b", [1, D])
    peer_wq = din("peer_wq", [D, D]); peer_k1 = din("peer_k1", [128, 64]); peer_k2 = din("peer_k2", [128, 64])
    peer_u = din("peer_u", [16384, D]); peer_v = din("peer_v", [16384, D])
    ohb = din("ohb", [32, 383]); negrow = din("negrow", [1, 383])
    sel_p = din("sel_p", [5, 128]); sel_s = din("sel_s", [5, 32])

    y_p = dout("y_p", [SEQ, D]); y_s = dout("y_s", [32, D])
    k_p = dout("k_p", [SEQ, 128]); v_p = dout("v_p", [SEQ, 128]); ki_p = dout("ki_p", [SEQ, 64])
    conv_o = dout("conv_o", [10, 512])
    k_s = dout("k_s", [32, 128]); v_s = dout("v_s", [32, 128]); ki_s = dout("ki_s", [32, 64])

    x1_scr = nc.dram_tensor("x1_scr", [NTOK, D], F32).ap()
    pe_dbg = DBG.get("ydump") == "pe"
    tsc = nc.dram_tensor("tsc", [8, 128, 383], F32).ap()
    ut_scr = nc.dram_tensor("ut_scr", [128, 128, 1024], BF16).ap()
    v_scr = nc.dram_tensor("v_scr", [16384, D], BF16).ap()
    b_x1scr = Buf("x1scr"); b_tsc = Buf("tsc"); b_utscr = Buf("utscr"); b_vscr = Buf("vscr")
    b_out = Buf("outputs")

    w_in_r = w_in.rearrange("(kc p) n -> p kc n", p=128)

    with ExitStack() as es:
        S = Sched(nc, es)
        cnt = [0]

        def sb(stack, shape, dt, name=None):
            cnt[0] += 1
            nm = f"{name or 't'}{cnt[0]}"
            return TL(stack.enter_context(nc.sbuf_tensor(nm, list(shape), dt)), nm)

        PS = []
        for i in range(8):
            PS.append(TL(es.enter_context(nc.psum_tensor(f"ps{i}", [128, 512], F32)), f"ps{i}"))
        rr = {}

        def psr(key, banks):
            i = rr.get(key, 0)
            rr[key] = i + 1
            return PS[banks[i % len(banks)]]

        evq = [0]

        def evac(out_ap, out_b, in_ap, in_b, engs=("act", "dve")):
            e = engs[evq[0] % len(engs)]
            evq[0] += 1
            if e == "act":
                S.op("act", lambda en: en.activation(out_ap, in_ap, AF.Copy), reads=[in_b], writes=[out_b], add=True)
            else:
                S.op("dve", lambda en: en.tensor_copy(out_ap, in_ap), reads=[in_b], writes=[out_b], add=True)

        cs = es
        io = sb(cs, [128, 128], F32, "io")
        ident = sb(cs, [128, 128], F32, "ident")
        negA = sb(cs, [128, 128], F32, "negA")
        iota16 = sb(cs, [128, 16], F32, "iota16")
        iotap = sb(cs, [128, 1], F32, "iotap")
        ones_f = sb(cs, [128, 128], F32, "ones")
        modT = sb(cs, [128, 48, 5], F32, "modT")
        modrow_g = sb(cs, [5, 2, 1024], F32, "modrowg")
        Bm = sb(cs, [128, 8, 2, 128], F32, "Bm")
        KBD = sb(cs, [128, 256], BF16, "KBD")
        selp_sb = sb(cs, [5, 128], F32, "selp")
        sels_sb = sb(cs, [5, 32], F32, "sels")
        S.op("pool", lambda e: e.iota(io[:], [[1, 128]], base=0, channel_multiplier=-1, allow_small_or_imprecise_dtypes=True), writes=[io.b])
        S.op("pool", lambda e: e.iota(iota16[:], [[1, 16]], base=0, channel_multiplier=0, allow_small_or_imprecise_dtypes=True), writes=[iota16.b])
        S.op("pool", lambda e: e.iota(iotap[:], [[0, 1]], base=0, channel_multiplier=1, allow_small_or_imprecise_dtypes=True), writes=[iotap.b])
        S.op("dve", lambda e: e.tensor_scalar(ident[:], io[:], 0.0, None, op0=ALU.is_equal), reads=[io.b], writes=[ident.b])
        S.op("dve", lambda e: e.tensor_scalar(negA[:], io[:], 0.0, NEG, op0=ALU.is_gt, op1=ALU.mult), reads=[io.b], writes=[negA.b])
        S.op("dve", lambda e: e.memset(ones_f[:], 1.0), writes=[ones_f.b])
        S.op("pool", lambda e: e.memset(KBD[:], 0.0), writes=[KBD.b])
        S.dma("sp", lambda e: e.dma_start(out=selp_sb[:], in_=sel_p[:, :]), writes=[selp_sb.b])
        S.dma("sp", lambda e: e.dma_start(out=sels_sb[:], in_=sel_s[:, :]), writes=[sels_sb.b])

        try:
            with ExitStack() as p0:
                cv_sb = sb(p0, [5, D], F32, "cv")
                cT = sb(p0, [128, 8, 5], F32, "cT")
                S.dma("sp", lambda e: e.dma_start(out=cv_sb[:], in_=cvec[:, :]), writes=[cv_sb.b])
                ps = PS[0]
                for kc in range(8):
                    S.op("pe", lambda e, kc=kc: e.transpose(ps[:, kc * 5:(kc + 1) * 5], cv_sb[:, kc * 128:(kc + 1) * 128], ident[:5, :5]),
                         reads=[cv_sb.b, ident.b], writes=[ps.b], add=True)
                S.op("dve", lambda e: e.tensor_copy(cT[:].rearrange("p k c -> p (k c)"), ps[:, 0:40]), reads=[ps.b], writes=[cT.b])
                modrow = sb(p0, [5, 6 * D], F32, "modrow")
                was = [sb(p0, [128, 8, 512], F32, "wa") for _ in range(2)]
                bas = [sb(p0, [1, 512], F32, "ba") for _ in range(2)]
                w_ada_r = w_ada.rearrange("(kc p) n -> p kc n", p=128)
                for cg in range(12):
                    wa = was[cg % 2]; ba = bas[cg % 2]
                    S.dma("sp", lambda e, wa=wa, cg=cg: e.dma_start(out=wa[:], in_=w_ada_r[:, :, cg * 512:(cg + 1) * 512]), writes=[wa.b])
                    S.dma("sp", lambda e, ba=ba, cg=cg: e.dma_start(out=ba[:], in_=b_ada[:, cg * 512:(cg + 1) * 512]), writes=[ba.b])
                    pm = psr("mod", [1, 2])
                    for kc in range(8):
                        S.op("pe", lambda e, kc=kc, wa=wa, pm=pm: e.matmul(pm[0:5, :], cT[:, kc, :], wa[:, kc, :], start=(kc == 0), stop=False),
                             reads=[cT.b, wa.b], writes=[pm.b], add=(kc > 0))
                    S.op("pe", lambda e, ba=ba, pm=pm: e.matmul(pm[0:5, :], ones_f[0:1, 0:5], ba[0:1, :], start=False, stop=True),
                         reads=[ones_f.b, ba.b], writes=[pm.b], add=True)
                    S.op("act", lambda e, pm=pm, cg=cg: e.activation(modrow[:, cg * 512:(cg + 1) * 512], pm[0:5, :], AF.Copy),
                         reads=[pm.b], writes=[modrow.b], add=True)
                S.op("dve", lambda e: e.tensor_copy(modrow_g[:, 0, :], modrow[:, 2 * D:3 * D]), reads=[modrow.b], writes=[modrow_g.b])
                S.op("dve", lambda e: e.tensor_copy(modrow_g[:, 1, :], modrow[:, 5 * D:6 * D]), reads=[modrow.b], writes=[modrow_g.b], add=True)
                for jg in range(2):
                    pm = psr("mod", [1, 2])
                    for jj in range(24):
                        j = jg * 24 + jj
                        S.op("pe", lambda e, j=j, jj=jj, pm=pm: e.transpose(pm[:, jj * 5:(jj + 1) * 5], modrow[:, j * 128:(j + 1) * 128], ident[:5, :5]),
                             reads=[modrow.b, ident.b], writes=[pm.b], add=(jj > 0))
                    S.op("dve", lambda e, jg=jg, pm=pm: e.tensor_copy(modT[:, jg * 24:(jg + 1) * 24, :].rearrange("p j c -> p (j c)"), pm[:, 0:120]),
                         reads=[pm.b], writes=[modT.b], add=True)
                for j0 in (8, 32):
                    S.op("dve", lambda e, j0=j0: e.tensor_scalar(modT[:, j0:j0 + 8, :], modT[:, j0:j0 + 8, :], 1.0, None, op0=ALU.add),
                         reads=[modT.b], writes=[modT.b])
                rb_sb = sb(p0, [32, 8], F32, "rb")
                rbrep = sb(p0, [32, 8, 128], F32, "rbrep")
                ohb_sb = sb(p0, [32, 383], F32, "ohb")
                neg_sb = sb(p0, [1, 383], F32, "negr")
                tv = sb(p0, [128, 8, 383], F32, "tv")
                b31 = sb(p0, [128, 8], F32, "b31")
                S.dma("sp", lambda e: e.dma_start(out=rb_sb[:], in_=rel_bias[:, :]), writes=[rb_sb.b])
                S.dma("sp", lambda e: e.dma_start(out=ohb_sb[:], in_=ohb[:, :]), writes=[ohb_sb.b])
                S.dma("sp", lambda e: e.dma_start(out=neg_sb[:], in_=negrow[:, :]), writes=[neg_sb.b])
                for h in range(8):
                    S.op("dve", lambda e, h=h: e.tensor_copy(rbrep[:, h, :], rb_sb[:, h:h + 1].to_broadcast([32, 128])),
                         reads=[rb_sb.b], writes=[rbrep.b], add=True)
                for h in range(8):
                    pm = psr("mod", [1, 2])
                    S.op("pe", lambda e, h=h, pm=pm: e.matmul(pm[:, 0:383], rbrep[:, h, :], ohb_sb[:, :], start=True, stop=False),
                         reads=[rbrep.b, ohb_sb.b], writes=[pm.b])
                    S.op("pe", lambda e, pm=pm: e.matmul(pm[:, 0:383], ones_f[0:1, :], neg_sb[0:1, :], start=False, stop=True),
                         reads=[ones_f.b, neg_sb.b], writes=[pm.b], add=True)
                    S.op("act", lambda e, h=h, pm=pm: e.activation(b31[:, h:h + 1], pm[:, 382:383], AF.Copy), reads=[pm.b], writes=[b31.b], add=True)
                    S.op("dve", lambda e, h=h, pm=pm: e.tensor_scalar(tv[:, h, :], pm[:, 0:383], b31[:, h:h + 1], None, op0=ALU.subtract),
                         reads=[pm.b, b31.b], writes=[tv.b], add=True)
                S.dma("sp", lambda e: e.dma_start(out=tsc.rearrange("h p j -> p h j"), in_=tv[:]), reads=[tv.b], writes=[b_tsc])
                for h in range(8):
                    for ci, c in enumerate((0, 128)):
                        src = bass.AP(tsc.tensor, h * 128 * 383 + 127 + c, [[382, 128], [1, 128]])
                        S.dma("sp", lambda e, h=h, ci=ci, src=src: e.dma_start(out=Bm[:, h, ci, :], in_=src), reads=[b_tsc], writes=[Bm.b], add=True)
                k12 = sb(p0, [128, 128], F32, "k12")
                S.dma("sp", lambda e: e.dma_start(out=k12[:, 0:64], in_=peer_k1[:, :]), writes=[k12.b])
                S.dma("sp", lambda e: e.dma_start(out=k12[:, 64:128], in_=peer_k2[:, :]), writes=[k12.b], add=True)
                pm = psr("mod", [1, 2])
                S.op("pe", lambda e, pm=pm: e.transpose(pm[:, 0:128], k12[:], ident[:]), reads=[k12.b, ident.b], writes=[pm.b])
                S.op("dve", lambda e, pm=pm: e.tensor_copy(KBD[0:64, 0:128], pm[0:64, 0:128]), reads=[pm.b], writes=[KBD.b], add=True)
                S.op("dve", lambda e, pm=pm: e.tensor_copy(KBD[64:128, 128:256], pm[64:128, 0:128]), reads=[pm.b], writes=[KBD.b], add=True)

                ust = [sb(p0, [128, D], F32, "ust") for _ in range(2)]
                utb = [sb(p0, [128, 8, 128], BF16, "utb") for _ in range(2)]
                vst = [sb(p0, [128, D], BF16, "vst") for _ in range(2)]
                for a in range(DBG['prep']):
                    us = ust[a % 2]; ub = utb[a % 2]; vs_ = vst[a % 2]
                    S.dma("sp", lambda e, us=us, a=a: e.dma_start(out=us[:], in_=peer_u[a * 128:(a + 1) * 128, :]), writes=[us.b])
                    for hb in range(2):
                        pm = psr("prep", [3, 4, 5, 6])
                        for k4 in range(4):
                            kc = hb * 4 + k4
                            S.op("pe", lambda e, kc=kc, k4=k4, pm=pm, us=us: e.transpose(pm[:, k4 * 128:(k4 + 1) * 128], us[:, kc * 128:(kc + 1) * 128], ident[:]),
                                 reads=[us.b, ident.b], writes=[pm.b], add=(k4 > 0))
                        evac(ub[:, hb * 4:(hb + 1) * 4, :].rearrange("p k e -> p (k e)"), ub.b, pm[:, :], pm.b)
                    S.dma("sp", lambda e, ub=ub, a=a: e.dma_start(out=ut_scr[a], in_=ub[:].rearrange("p k e -> p (k e)")), reads=[ub.b], writes=[b_utscr], sembuf=ub.b, add=True)
                    S.dma("pool", lambda e, vs_=vs_, a=a: e.dma_start(out=vs_[:], in_=peer_v[a * 128:(a + 1) * 128, :]), writes=[vs_.b])
                    S.dma("sp", lambda e, vs_=vs_, a=a: e.dma_start(out=v_scr[a * 128:(a + 1) * 128, :], in_=vs_[:]), reads=[vs_.b], writes=[b_vscr], sembuf=vs_.b, add=True)
                S.barrier(flush=True)
            S.barrier()

            stop_check(0)
            TILES = [(128 * t, 128, 0) for t in range(16)] + [(2048 + 8 * b, 8, 1 + b) for b in range(4)]

            def load_h_T(stack_tiles, dstT, dst_b, col0, tok0, nt, cidx, shj, scj, src_ap):
                xt = stack_tiles[rr.get("xt", 0) % len(stack_tiles)]
                rr["xt"] = rr.get("xt", 0) + 1
                S.dma("sp", lambda e: e.dma_start(out=xt[:nt, :], in_=src_ap), reads=[b_x1scr], writes=[xt.b])
                for hb in range(2):
                    pm = psr("tr", [0, 1])
                    for k4 in range(4):
                        kc = hb * 4 + k4
                        S.op("pe", lambda e, kc=kc, k4=k4, pm=pm: e.transpose(pm[:, k4 * 128:k4 * 128 + nt], xt[:nt, kc * 128:(kc + 1) * 128], ident[:nt, :nt]),
                             reads=[xt.b, ident.b], writes=[pm.b], add=(k4 > 0))
                    for k4 in range(4):
                        kc = hb * 4 + k4
                        if isinstance(cidx, int):
                            segs = [(0, nt, cidx)]
                        else:
                            segs = cidx
                        for (c0, cn, ci) in segs:
                            S.op("act", lambda e, kc=kc, k4=k4, pm=pm, c0=c0, cn=cn, ci=ci: e.activation(
                                dstT[:, kc, col0 + c0:col0 + c0 + cn], pm[:, k4 * 128 + c0:k4 * 128 + c0 + cn], AF.Identity,
                                bias=modT[:, shj + kc, ci:ci + 1], scale=modT[:, scj + kc, ci:ci + 1]),
                                reads=[pm.b, modT.b], writes=[dst_b], add=True)

            act_es = ExitStack()
            o_convT = sb(act_es, [128, 4, NTOK], BF16, "oconvT")
            o_attnT = sb(act_es, [64, 8, NTOK], BF16, "oattnT")
            att_es = ExitStack()
            qT = sb(att_es, [128, 4, NTOK], BF16, "qT")
            kTd = sb(att_es, [128, 2, NTOK], BF16, "kTd")
            qiT = sb(att_es, [128, 2, NTOK], BF16, "qiT")
            kiTd = sb(att_es, [128, NTOK], BF16, "kiTd")
            Vx = sb(att_es, [128, 20, 2, 65], BF16, "Vx")
            wi_sb = sb(att_es, [128, 20, 4], F32, "wi")
            S.op("pool", lambda e: e.memset(Vx[:], 1.0), writes=[Vx.b])

            GROUPS = [(0, 512), (512, 512), (1024, 512), (1536, 512), (2048, 32)]

            with ExitStack() as p1:
                h1T = sb(p1, [128, 8, NTOK], BF16, "h1T")
                xts = [sb(p1, [128, D], F32, "xt") for _ in range(1)]
                for (tok0, nt, ci) in TILES:
                    src = xp[tok0:tok0 + nt, :] if tok0 < 2048 else xs[tok0 - 2048:tok0 - 2048 + nt, :]
                    load_h_T(xts, h1T, h1T.b, tok0, tok0, nt, ci, 0, 8, src)
                wq = sb(p1, [128, 8, 512], BF16, "wq")
                wkd = sb(p1, [128, 8, 2, 128], BF16, "wkd")
                wqi = sb(p1, [128, 8, 256], BF16, "wqi")
                wkid = sb(p1, [128, 8, 128], BF16, "wkid")
                wtm = sb(p1, [128, 8, 324], BF16, "wtm")
                S.dma("pool", lambda e: e.dma_start(out=wq[:], in_=w_in_r[:, :, 0:512]), writes=[wq.b])
                for g in range(2):
                    for hf in range(2):
                        S.dma("pool", lambda e, g=g, hf=hf: e.dma_start(out=wkd[:, :, g, hf * 64:(hf + 1) * 64], in_=w_in_r[:, :, 512 + 64 * g:576 + 64 * g]),
                              writes=[wkd.b], add=True)
                S.dma("pool", lambda e: e.dma_start(out=wqi[:], in_=w_in_r[:, :, 768:1024]), writes=[wqi.b])
                for hf in range(2):
                    S.dma("pool", lambda e, hf=hf: e.dma_start(out=wkid[:, :, hf * 64:(hf + 1) * 64], in_=w_in_r[:, :, 1028:1092]), writes=[wkid.b], add=True)
                S.dma("pool", lambda e: e.dma_start(out=wtm[:, :, 0:256], in_=w_in_r[:, :, 512:768]), writes=[wtm.b], add=True)
                S.dma("pool", lambda e: e.dma_start(out=wtm[:, :, 256:320], in_=w_in_r[:, :, 1028:1092]), writes=[wtm.b], add=True)
                S.dma("pool", lambda e: e.dma_start(out=wtm[:, :, 320:324], in_=w_in_r[:, :, 1024:1028]), writes=[wtm.b], add=True)
                fm_sets = []
                for j in range(4):
                    fm_sets.append((lambda kc, j=j: wq[:, kc, j * 128:(j + 1) * 128], wq.b, lambda c0, n, j=j: qT[:, j, c0:c0 + n], qT.b))
                for g in range(2):
                    fm_sets.append((lambda kc, g=g: wkd[:, kc, g, :], wkd.b, lambda c0, n, g=g: kTd[:, g, c0:c0 + n], kTd.b))
                for j in range(2):
                    fm_sets.append((lambda kc, j=j: wqi[:, kc, j * 128:(j + 1) * 128], wqi.b, lambda c0, n, j=j: qiT[:, j, c0:c0 + n], qiT.b))
                fm_sets.append((lambda kc: wkid[:, kc, :], wkid.b, lambda c0, n: kiTd[:, c0:c0 + n], kiTd.b))
                for (wf, wb, df, db) in fm_sets:
                    for (g0, gn) in GROUPS:
                        pm = psr("fm", [2, 3, 4, 5])
                        for kc in range(8):
                            S.op("pe", lambda e, kc=kc, pm=pm, wf=wf, g0=g0, gn=gn: e.matmul(pm[:, 0:gn], wf(kc), h1T[:, kc, g0:g0 + gn], start=(kc == 0), stop=(kc == 7)),
                                 reads=[wb, h1T.b], writes=[pm.b], add=(kc > 0))
                        evac(df(g0, gn), db, pm[:, 0:gn], pm.b)
                kvst = [sb(p1, [128, 324], F32, "kvst") for _ in range(2)]
                for ti, (tok0, nt, ci) in enumerate(TILES):
                    pm = psr("fm", [2, 3, 4, 5])
                    kv = kvst[ti % 2]
                    for kc in range(8):
                        S.op("pe", lambda e, kc=kc, pm=pm, tok0=tok0, nt=nt: e.matmul(pm[:nt, 0:324], h1T[:, kc, tok0:tok0 + nt], wtm[:, kc, :], start=(kc == 0), stop=(kc == 7)),
                             reads=[wtm.b, h1T.b], writes=[pm.b], add=(kc > 0))
                    S.op("act", lambda e, pm=pm, kv=kv, nt=nt: e.activation(kv[:nt, :], pm[:nt, 0:324], AF.Copy), reads=[pm.b], writes=[kv.b])
                    if tok0 < 2048:
                        ko, vo, kio, r0 = k_p, v_p, ki_p, tok0
                    else:
                        ko, vo, kio, r0 = k_s, v_s, ki_s, tok0 - 2048
                    S.dma("sp", lambda e, kv=kv, nt=nt, ko=ko, r0=r0: e.dma_start(out=ko[r0:r0 + nt, :], in_=kv[:nt, 0:128]), reads=[kv.b])
                    S.dma("sp", lambda e, kv=kv, nt=nt, vo=vo, r0=r0: e.dma_start(out=vo[r0:r0 + nt, :], in_=kv[:nt, 128:256]), reads=[kv.b])
                    S.dma("sp", lambda e, kv=kv, nt=nt, kio=kio, r0=r0: e.dma_start(out=kio[r0:r0 + nt, :], in_=kv[:nt, 256:320]), reads=[kv.b])
                    S.op("dve", lambda e, kv=kv, nt=nt, ti=ti: e.tensor_copy(Vx[:nt, ti, :, 0:64], kv[:nt, 128:256].rearrange("p (g d) -> p g d", g=2)),
                         reads=[kv.b], writes=[Vx.b], add=True)
                    S.op("dve", lambda e, kv=kv, nt=nt, ti=ti: e.tensor_copy(wi_sb[:nt, ti, :], kv[:nt, 320:324]), reads=[kv.b], writes=[wi_sb.b], add=True)

                cw = sb(p1, [128, 4, 3], F32, "cw"); cb = sb(p1, [128, 4], F32, "cb")
                for cc in range(4):
                    S.dma("sp", lambda e, cc=cc: e.dma_start(out=cw[:, cc, :], in_=conv_w[:, cc * 128:(cc + 1) * 128].rearrange("j p -> p j"), allow_slow_non_contiguous=True), writes=[cw.b], add=True)
                    S.dma("sp", lambda e, cc=cc: e.dma_start(out=cb[:, cc:cc + 1], in_=conv_b[:, cc * 128:(cc + 1) * 128].rearrange("o p -> p o"), allow_slow_non_contiguous=True), writes=[cb.b], add=True)
                Up = sb(p1, [128, 2050], F32, "Up"); Us = sb(p1, [128, 4, 10], F32, "Us")
                Useq = lambda sq: (Up[:, :] if sq == 0 else Us[:, sq - 1, :])
                Ub = lambda sq: (Up.b if sq == 0 else Us.b)
                bgT = sb(p1, [128, NTOK], BF16, "bgT")
                ycv = sb(p1, [128, 2048], F32, "ycv")
                cgs = [sb(p1, [128, 512], F32, "cgs") for _ in range(1)]
                lastU = sb(p1, [128, 4, 10], F32, "lastU")
                wcv = [sb(p1, [128, 8, 3, 128], BF16, "wcv") for _ in range(1)]
                for cc in range(4):
                    wc = wcv[0]
                    for k3 in range(3):
                        S.dma("pool", lambda e, wc=wc, k3=k3, cc=cc: e.dma_start(out=wc[:, :, k3, :], in_=w_in_r[:, :, 1092 + 512 * k3 + 128 * cc:1092 + 512 * k3 + 128 * (cc + 1)]),
                              writes=[wc.b], add=(k3 > 0))
                    S.op("pool", lambda e: e.memset(Up[:, 0:2], 0.0), writes=[Up.b], add=True)
                    for b in range(4):
                        S.dma("sp", lambda e, cc=cc, b=b: e.dma_start(out=Us[:, b, 0:2], in_=sconv[2 * b:2 * b + 2, cc * 128:(cc + 1) * 128].rearrange("t p -> p t"), allow_slow_non_contiguous=True),
                              writes=[Us.b], add=True)
                    for (g0, gn) in GROUPS:
                        pb = psr("fm", [2, 3, 4, 5]); pc = psr("fm", [2, 3, 4, 5]); px = psr("fm", [2, 3, 4, 5])
                        for k3, pp in enumerate((pb, pc, px)):
                            for kc in range(8):
                                S.op("pe", lambda e, kc=kc, pp=pp, k3=k3, wc=wc, g0=g0, gn=gn: e.matmul(pp[:, 0:gn], wc[:, kc, k3, :], h1T[:, kc, g0:g0 + gn], start=(kc == 0), stop=(kc == 7)),
                                     reads=[wc.b, h1T.b], writes=[pp.b], add=(kc > 0))
                        cgx = cgs[0]; rr["cg"] = rr.get("cg", 0) + 1
                        S.op("act", lambda e, cgx=cgx, pc=pc, gn=gn: e.activation(cgx[:, 0:gn], pc[:, 0:gn], AF.Copy), reads=[pc.b], writes=[cgx.b])
                        S.op("act", lambda e, pb=pb, g0=g0, gn=gn: e.activation(bgT[:, g0:g0 + gn], pb[:, 0:gn], AF.Copy), reads=[pb.b], writes=[bgT.b], add=True)
                        if g0 < 2048:
                            S.op("dve", lambda e, cgx=cgx, px=px, g0=g0, gn=gn: e.tensor_tensor(out=Up[:, 2 + g0:2 + g0 + gn], in0=cgx[:, 0:gn], in1=px[:, 0:gn], op=ALU.mult),
                                 reads=[cgx.b, px.b], writes=[Up.b], add=True)
                        else:
                            S.op("dve", lambda e, cgx=cgx, px=px: e.tensor_tensor(out=Us[:, :, 2:10], in0=cgx[:, 0:32].rearrange("p (b t) -> p b t", b=4),
                                                                                 in1=px[:, 0:32].rearrange("p (b t) -> p b t", b=4), op=ALU.mult),
                                 reads=[cgx.b, px.b], writes=[Us.b], add=True)
                    for (sq, T_, c0) in [(0, 2048, 0)] + [(1 + b, 8, 2048 + 8 * b) for b in range(4)]:
                        S.op("dve", lambda e, sq=sq, T_=T_, cc=cc: e.tensor_scalar(ycv[:, 0:T_], Useq(sq)[:, 0:T_], cw[:, cc, 0:1], cb[:, cc:cc + 1], op0=ALU.mult, op1=ALU.add),
                             reads=[Ub(sq), cw.b, cb.b], writes=[ycv.b])
                        for j in (1, 2):
                            S.op("dve", lambda e, sq=sq, T_=T_, cc=cc, j=j: e.scalar_tensor_tensor(out=ycv[:, 0:T_], in0=Useq(sq)[:, j:j + T_], scalar=cw[:, cc, j:j + 1], in1=ycv[:, 0:T_], op0=ALU.mult, op1=ALU.add),
                                 reads=[Ub(sq), cw.b, ycv.b], writes=[ycv.b])
                        S.op("dve", lambda e, T_=T_, cc=cc, c0=c0: e.tensor_tensor(out=o_convT[:, cc, c0:c0 + T_], in0=ycv[:, 0:T_], in1=bgT[:, c0:c0 + T_], op=ALU.mult),
                             reads=[ycv.b, bgT.b], writes=[o_convT.b], add=True)
                        S.op("act", lambda e, sq=sq, T_=T_, cc=cc: e.activation(lastU[:, cc, 2 * sq:2 * sq + 2], Useq(sq)[:, T_:T_ + 2], AF.Copy), reads=[Ub(sq)], writes=[lastU.b], add=True)
                pm = psr("fm", [2, 3, 4, 5])
                for cc in range(4):
                    S.op("pe", lambda e, cc=cc, pm=pm: e.transpose(pm[0:10, cc * 128:(cc + 1) * 128], lastU[:, cc, :], ident[:, :]),
                         reads=[lastU.b, ident.b], writes=[pm.b], add=(cc > 0))
                cst = sb(p1, [10, 512], F32, "cst")
                S.op("act", lambda e, pm=pm: e.activation(cst[:], pm[0:10, :], AF.Copy), reads=[pm.b], writes=[cst.b])
                S.dma("sp", lambda e: e.dma_start(out=conv_o[:, :], in_=cst[:]), reads=[cst.b])
                S.barrier(flush=True)
            S.barrier()

            stop_check(2)
            def attention(st, blocks, nq, qtok0, wi_ap, L, need_topk, S_sc, kiT_ap_fn):
                rt, cntt, lo, W, mid, tt, mn, mx_ = st["rt"], st["cnt"], st["lo"], st["W"], st["mid"], st["tt"], st["mn"], st["mx"]
                junk = st["junk"]
                k0 = 0
                while k0 < L:
                    n = min(512, L - k0)
                    kap, kb = kiT_ap_fn(k0, n)
                    for h in range(4):
                        pm = PS[h % 2]
                        hp = (h % 2) * 64
                        S.op("pe", lambda e, pm=pm, h=h, hp=hp, kap=kap, n=n: e.matmul(pm[:nq, 0:n], qiT[hp:hp + 64, h // 2, qtok0:qtok0 + nq], kap[hp:hp + 64, :], start=True, stop=True),
                             reads=[qiT.b, kb], writes=[pm.b])
                        r = rt[rr.get("rt", 0) % 2]; rr["rt"] = rr.get("rt", 0) + 1
                        S.op("act", lambda e, pm=pm, r=r, n=n: e.activation(r[:nq, 0:n], pm[:nq, 0:n], AF.Relu, scale=IDX_SCALE), reads=[pm.b], writes=[r.b])
                        if h == 0:
                            S.op("dve", lambda e, r=r, n=n, k0=k0: e.tensor_scalar(S_sc[:nq, k0:k0 + n], r[:nq, 0:n], wi_ap[:, 0:1], None, op0=ALU.mult),
                                 reads=[r.b, wi_sb.b], writes=[S_sc.b], add=True)
                        else:
                            S.op("dve", lambda e, r=r, n=n, k0=k0, h=h: e.scalar_tensor_tensor(out=S_sc[:nq, k0:k0 + n], in0=r[:nq, 0:n], scalar=wi_ap[:, h:h + 1], in1=S_sc[:nq, k0:k0 + n], op0=ALU.mult, op1=ALU.add),
                                 reads=[r.b, wi_sb.b, S_sc.b], writes=[S_sc.b])
                    k0 += n
                stop_check(2.1)
                nk0 = blocks[-1]["nk"]
                if need_topk:
                    S.op("dve", lambda e: e.tensor_reduce(out=mx_[:nq, :], in_=S_sc[:nq, 0:L], axis=AX.X, op=ALU.max), reads=[S_sc.b], writes=[mx_.b])
                    S.op("dve", lambda e: e.tensor_reduce(out=mn[:nq, :], in_=S_sc[:nq, 0:L], axis=AX.X, op=ALU.min), reads=[S_sc.b], writes=[mn.b])
                    S.op("dve", lambda e: e.tensor_scalar(lo[:nq, :], mn[:nq, :], -1.0, None, op0=ALU.add), reads=[mn.b], writes=[lo.b])
                    S.op("dve", lambda e: e.scalar_tensor_tensor(out=W[:nq, :], in0=mx_[:nq, :], scalar=2.0, in1=mn[:nq, :], op0=ALU.add, op1=ALU.subtract),
                         reads=[mx_.b, mn.b], writes=[W.b])
                S.op("dve", lambda e: e.tensor_tensor(out=S_sc[:nq, L - nk0:L], in0=S_sc[:nq, L - nk0:L], in1=negA[:nq, :nk0], op=ALU.add),
                     reads=[S_sc.b, negA.b], writes=[S_sc.b])
                if need_topk:
                    S.op("dve", lambda e: e.memset(cntt[:], 0.0), writes=[cntt.b])
                    for it in range(NBIS):
                        ck_ = 2.0 ** -(it + 1)
                        S.op("dve", lambda e, ck_=ck_: e.scalar_tensor_tensor(out=mid[:nq, :], in0=W[:nq, :], scalar=ck_, in1=lo[:nq, :], op0=ALU.mult, op1=ALU.add),
                             reads=[W.b, lo.b], writes=[mid.b])
                        S.op("dve", lambda e, it=it: e.tensor_scalar(junk[:nq, 0:L], S_sc[:nq, 0:L], mid[:nq, 0:1], 0.0, op0=ALU.is_gt, op1=ALU.add, accum_out=cntt[:nq, it:it + 1]),
                             reads=[S_sc.b, mid.b, cntt.b], writes=[junk.b, cntt.b])
                        S.op("dve", lambda e, it=it, ck_=ck_: e.tensor_scalar(tt[:nq, :], cntt[:nq, it:it + 1], 256.0, ck_, op0=ALU.is_ge, op1=ALU.mult),
                             reads=[cntt.b], writes=[tt.b])
                        S.op("dve", lambda e: e.scalar_tensor_tensor(out=lo[:nq, :], in0=tt[:nq, :], scalar=W[:nq, 0:1], in1=lo[:nq, :], op0=ALU.mult, op1=ALU.add),
                             reads=[tt.b, W.b, lo.b], writes=[lo.b])
                    S.op("dve", lambda e: e.tensor_scalar(S_sc[:nq, 0:L], S_sc[:nq, 0:L], lo[:nq, 0:1], None, op0=ALU.is_gt), reads=[S_sc.b, lo.b], writes=[S_sc.b])
                else:
                    S.op("dve", lambda e: e.tensor_scalar(S_sc[:nq, 0:L], S_sc[:nq, 0:L], -10000.0, None, op0=ALU.is_gt), reads=[S_sc.b], writes=[S_sc.b])
                stop_check(2.2)
                oacc = [PS[6], PS[7]]
                kpos = 0
                nb = len(blocks)
                for bi, blk in enumerate(blocks):
                    nk = blk["nk"]
                    if blk.get("prep"):
                        blk["prep"]()
                    pmk = psr("mk", [2, 3])
                    S.op("pe", lambda e, pmk=pmk, kpos=kpos, nk=nk: e.transpose(pmk[:nk, 0:nq], S_sc[:nq, kpos:kpos + nk], ident[:nq, :nq]),
                         reads=[S_sc.b, ident.b], writes=[pmk.b])
                    stop_check(2.22)
                    addm = st["addm"][bi % 2]
                    S.op("dve", lambda e, pmk=pmk, addm=addm, nk=nk: e.tensor_scalar(addm[:nk, 0:nq], pmk[:nk, 0:nq], -1.0, -NEG, op0=ALU.add, op1=ALU.mult),
                         reads=[pmk.b], writes=[addm.b])
                    stop_check(2.25)
                    pqE = PS[4]; pqO = PS[5]
                    for h in (0, 2, 4, 6, 1, 3, 5, 7):
                        g = h // 4
                        hp = (h % 2) * 64
                        pq = pqE if hp == 0 else pqO
                        col = (h // 2) * nq
                        kap, kb = blk["KT"](g)
                        S.op("pe", lambda e, pq=pq, col=col, h=h, hp=hp, kap=kap, nk=nk: e.matmul(pq[:nk, col:col + nq], kap[hp:hp + 64, :], qT[hp:hp + 64, h // 2, qtok0:qtok0 + nq], start=True, stop=True),
                             reads=[kb, qT.b], writes=[pq.b], add=True)
                    stop_check(2.3)
                    for g in range(2):
                        hs = (4 * g, 4 * g + 2, 4 * g + 1, 4 * g + 3)
                        lg = st["lg"][rr.get("lg", 0) % 2]; rr["lg"] = rr.get("lg", 0) + 1
                        lgv = lg[:nk, 0:4 * nq].rearrange("p (s q) -> p s q", s=4)
                        for half, pq in enumerate((pqE, pqO)):
                            S.op("dve", lambda e, pq=pq, lgv=lgv, half=half, g=g, addm=addm, nk=nk: e.scalar_tensor_tensor(
                                out=lgv[:, 2 * half:2 * half + 2, :], in0=pq[:nk, 2 * g * nq:(2 * g + 2) * nq].rearrange("p (s q) -> p s q", s=2),
                                scalar=ATTN_SCALE, in1=addm[:nk, 0:nq].unsqueeze(1).to_broadcast([nk, 2, nq]), op0=ALU.mult, op1=ALU.add),
                                reads=[pq.b, addm.b], writes=[lg.b], add=True)
                        stop_check(2.32)
                        if blk["kind"] != "far":
                            ci = 0 if blk["kind"] == "near0" else 1
                            for sl in range(4):
                                S.op("pool", lambda e, lg=lg, sl=sl, hh_=hs[sl], ci=ci, nk=nk: e.tensor_tensor(out=lg[:nk, sl * nq:(sl + 1) * nq], in0=lg[:nk, sl * nq:(sl + 1) * nq], in1=Bm[:nk, hh_, ci, 0:nq], op=ALU.add),
                                     reads=[lg.b, Bm.b], writes=[lg.b])
                        stop_check(2.34)
                        pT = st["pT"][rr.get("pT", 0) % 2]; rr["pT"] = rr.get("pT", 0) + 1
                        S.op("act", lambda e, lg=lg, pT=pT, nk=nk: e.activation(pT[:nk, 0:4 * nq], lg[:nk, 0:4 * nq], AF.Exp), reads=[lg.b], writes=[pT.b])
                        stop_check(2.36)
                        vap, vb = blk["V"](g)
                        S.op("pe", lambda e, g=g, pT=pT, vap=vap, nk=nk, bi=bi: e.matmul(oacc[g][0:65, 0:4 * nq], vap, pT[:nk, 0:4 * nq], start=(bi == 0), stop=(bi == nb - 1)),
                             reads=[vb, pT.b], writes=[oacc[g].b], add=(bi > 0))
                    kpos += nk
                stop_check(2.4)
                for g in range(2):
                    den = st["den"]
                    S.op("act", lambda e, g=g: e.activation(den[64:65, 0:4 * nq], oacc[g][64:65, 0:4 * nq], AF.Copy), reads=[oacc[g].b], writes=[den.b])
                    pb_ = psr("mk", [2, 3])
                    S.op("pe", lambda e, pb_=pb_: e.matmul(pb_[0:64, 0:4 * nq], ones_f[64:65, 0:64], den[64:65, 0:4 * nq], start=True, stop=True),
                         reads=[ones_f.b, den.b], writes=[pb_.b])
                    rec = st["rec"]
                    S.op("dve", lambda e, pb_=pb_: e.reciprocal(rec[0:64, 0:4 * nq], pb_[0:64, 0:4 * nq]), reads=[pb_.b], writes=[rec.b])
                    for half in range(2):
                        S.op("dve", lambda e, g=g, half=half: e.tensor_tensor(out=o_attnT[:, 4 * g + half:4 * g + 4:2, qtok0:qtok0 + nq],
                                                                              in0=oacc[g][0:64, 0:4 * nq].rearrange("p (h q) -> p h q", h=4)[:, 2 * half:2 * half + 2, :],
                                                                              in1=rec[0:64, 0:4 * nq].rearrange("p (h q) -> p h q", h=4)[:, 2 * half:2 * half + 2, :], op=ALU.mult),
                             reads=[oacc[g].b, rec.b], writes=[o_attnT.b], add=True)

            with ExitStack() as p3:
                st = dict(
                    rt=[sb(p3, [128, 512], F32, "rt") for _ in range(2)],
                    cnt=sb(p3, [128, NBIS], F32, "cnt"), lo=sb(p3, [128, 1], F32, "lo"), W=sb(p3, [128, 1], F32, "W"),
                    mid=sb(p3, [128, 1], F32, "mid"), tt=sb(p3, [128, 1], F32, "tt"), mn=sb(p3, [128, 1], F32, "mn"), mx=sb(p3, [128, 1], F32, "mx"),
                    addm=[sb(p3, [128, 128], F32, "addm") for _ in range(2)],
                    lg=[sb(p3, [128, 512], F32, "lg") for _ in range(2)],
                    pT=[sb(p3, [128, 512], BF16, "pT") for _ in range(2)],
                    den=sb(p3, [65, 512], F32, "den"), rec=sb(p3, [64, 512], F32, "rec"),
                )
                with ExitStack() as p3a:
                    S_p = sb(p3a, [128, 2048], F32, "S_p")
                    st["junk"] = sb(p3a, [128, 2048], BF16, "junkp")
                    for t in DBG['tblocks']:
                        blocks = []
                        for j in range(t + 1):
                            kind = "near0" if j == t else ("near1" if j == t - 1 else "far")
                            blocks.append(dict(nk=128, kind=kind,
                                               KT=lambda g, j=j: (kTd[:, g, j * 128:(j + 1) * 128], kTd.b),
                                               V=lambda g, j=j: (Vx[:, j, g, :], Vx.b)))
                        attention(st, blocks, 128, 128 * t, wi_sb[:, t, :], 128 * (t + 1), t >= 2, S_p,
                                  lambda k0, n: (kiTd[:, k0:k0 + n], kiTd.b))
                    S.barrier(flush=True)
                S.barrier()
                stop_check(2.5)
                with ExitStack() as p3b:
                    S_s = sb(p3b, [8, 8208], F32, "S_s")
                    st["junk"] = sb(p3b, [8, 8208], BF16, "junks")
                    kiT_s = sb(p3b, [128, 8200], BF16, "kiT_s")
                    ptb = sb(p3b, [128, 64], I32, "ptb")
                    idx = sb(p3b, [128, 64], I32, "idx")
                    kst = [sb(p3b, [128, 64], F32, "kst") for _ in range(3)]
                    kstd = [sb(p3b, [128, 2, 64], F32, "kstd") for _ in range(3)]
                    Kdd = [sb(p3b, [128, 2, 2, 64], F32, "Kdd") for _ in range(3)]
                    Kst = [sb(p3b, [128, 128], F32, "Kst") for _ in range(3)]
                    Vst = [sb(p3b, [128, 128], F32, "Vst") for _ in range(3)]
                    KTb = [sb(p3b, [128, 2, 128], BF16, "KTb") for _ in range(3)]
                    Vxb = [sb(p3b, [128, 2, 65], BF16, "Vxb") for _ in range(3)]
                    for vx in Vxb:
                        S.op("pool", lambda e, vx=vx: e.memset(vx[:], 1.0), writes=[vx.b])
                    for b in DBG['batches']:
                        src = bass.AP(ptab.tensor, b * 64, [[0, 128], [1, 64]])
                        S.dma("sp", lambda e, src=src: e.dma_start(out=ptb[:], in_=src), writes=[ptb.b])
                        S.op("dve", lambda e: e.tensor_scalar(idx[:], ptb[:], 128.0, iotap[:, 0:1], op0=ALU.mult, op1=ALU.add), reads=[ptb.b, iotap.b], writes=[idx.b])
                        for pg in range(64):
                            ks = kst[pg % 3]
                            S.dma("pool", lambda e, ks=ks, pg=pg: e.indirect_dma_start(out=ks[:], out_offset=None, in_=cki[:, :], in_offset=bass.IndirectOffsetOnAxis(ap=idx[:, pg:pg + 1], axis=0)),
                                  reads=[idx.b], writes=[ks.b])
                            ksd = kstd[pg % 3]
                            S.op("dve", lambda e, ks=ks, ksd=ksd: e.tensor_copy(ksd[:, :, :], ks[:].unsqueeze(1).to_broadcast([128, 2, 64])), reads=[ks.b], writes=[ksd.b])
                            pm = psr("ix", [0, 1])
                            S.op("pe", lambda e, pm=pm, ksd=ksd: e.transpose(pm[:, 0:128], ksd[:].rearrange("p r d -> p (r d)"), ident[:]),
                                 reads=[ksd.b, ident.b], writes=[pm.b])
                            evac(kiT_s[:, pg * 128:(pg + 1) * 128], kiT_s.b, pm[:, 0:128], pm.b)
                        S.op("dve", lambda e, b=b: e.tensor_copy(kiT_s[:, 8192:8200], kiTd[:, 2048 + 8 * b:2056 + 8 * b]), reads=[kiTd.b], writes=[kiT_s.b], add=True)
                        blocks = []
                        for pg in range(64):
                            def prep(pg=pg):
                                Ks = Kst[pg % 3]; Vs = Vst[pg % 3]; KT_ = KTb[pg % 3]; Vb = Vxb[pg % 3]
                                S.dma("pool", lambda e: e.indirect_dma_start(out=Ks[:], out_offset=None, in_=ck[:, :], in_offset=bass.IndirectOffsetOnAxis(ap=idx[:, pg:pg + 1], axis=0)),
                                      reads=[idx.b], writes=[Ks.b])
                                S.dma("pool", lambda e: e.indirect_dma_start(out=Vs[:], out_offset=None, in_=cvv[:, :], in_offset=bass.IndirectOffsetOnAxis(ap=idx[:, pg:pg + 1], axis=0)),
                                      reads=[idx.b], writes=[Vs.b])
                                Kd = Kdd[pg % 3]
                                S.op("dve", lambda e: e.tensor_copy(Kd[:, :, :, :], Ks[:].rearrange("p (g d) -> p g d", g=2).unsqueeze(2).to_broadcast([128, 2, 2, 64])), reads=[Ks.b], writes=[Kd.b])
                                for g in range(2):
                                    pm = psr("ix", [0, 1])
                                    S.op("pe", lambda e, pm=pm, g=g: e.transpose(pm[:, 0:128], Kd[:, g, :, :].rearrange("p r d -> p (r d)"), ident[:]),
                                         reads=[Kd.b, ident.b], writes=[pm.b])
                                    evac(KT_[:, g, :], KT_.b, pm[:, 0:128], pm.b)
                                S.op("dve", lambda e: e.tensor_copy(Vb[:, :, 0:64], Vs[:].rearrange("p (g d) -> p g d", g=2)), reads=[Vs.b], writes=[Vb.b], add=True)
                            blocks.append(dict(nk=128, kind=("near1" if pg == 63 else "far"), prep=prep,
                                               KT=lambda g, pg=pg: (KTb[pg % 3][:, g, :], KTb[pg % 3].b),
                                               V=lambda g, pg=pg: (Vxb[pg % 3][:, g, :], Vxb[pg % 3].b)))
                        blocks.append(dict(nk=8, kind="near0",
                                           KT=lambda g, b=b: (kTd[:, g, 2048 + 8 * b:2056 + 8 * b], kTd.b),
                                           V=lambda g, b=b: (Vx[0:8, 16 + b, g, :], Vx.b)))
                        attention(st, blocks, 8, 2048 + 8 * b, wi_sb[0:8, 16 + b, :], 8200, True, S_s,
                                  lambda k0, n: (kiT_s[:, k0:k0 + n], kiT_s.b))
                    S.barrier(flush=True)
            att_es.close()
            S.barrier()

            stop_check(3)
            if DBG.get('ydump') == 'oc':
                for cc in range(4):
                    dsto = bass.AP(y_p.tensor, cc * 128 * 2048, [[2048, 128], [1, 2048]])
                    S.dma("pool", lambda e, cc=cc, dsto=dsto: e.dma_start(out=dsto, in_=o_convT[:, cc, 0:2048]), reads=[o_convT.b])
                for h in range(8):
                    dsto = bass.AP(y_p.tensor, (512 + h * 64) * 2048, [[2048, 64], [1, 2048]])
                    S.dma("pool", lambda e, h=h, dsto=dsto: e.dma_start(out=dsto, in_=o_attnT[:, h, 0:2048]), reads=[o_attnT.b])
                S.barrier()
            stop_check(3.5)
            def bc_rows(dst, nrows, lhsT_ap, lhs_b, rhs_fn, rhs_b):
                for hf in range(2):
                    pm = psr("bc", [0, 1])
                    S.op("pe", lambda e, pm=pm, hf=hf: e.matmul(pm[:nrows, :], lhsT_ap, rhs_fn(hf), start=True, stop=True), reads=[lhs_b, rhs_b], writes=[pm.b])
                    S.op("act", lambda e, pm=pm, hf=hf: e.activation(dst[:nrows, hf * 512:(hf + 1) * 512], pm[:nrows, :], AF.Copy), reads=[pm.b], writes=[dst.b], add=True)

            def layer_norm(stk, z, nt, lg_bc, lb_bc, out_t):
                s1 = stk["s1"]; s2 = stk["s2"]; jk = stk["jk"]
                S.op("dve", lambda e: e.memset(s1[:], 0.0), writes=[s1.b])
                S.op("dve", lambda e: e.memset(s2[:], 0.0), writes=[s2.b])
                S.op("act", lambda e: e.activation(jk[:nt, :], z[:nt, :], AF.Identity, accum_out=s1[:nt, 0:1]), reads=[z.b, s1.b], writes=[jk.b, s1.b])
                S.op("act", lambda e: e.activation(jk[:nt, :], z[:nt, :], AF.Square, accum_out=s2[:nt, 0:1]), reads=[z.b, s2.b], writes=[jk.b, s2.b])
                mu = stk["mu"]; var = stk["var"]; rstd = stk["rstd"]
                S.op("dve", lambda e: e.tensor_scalar(mu[:nt, :], s1[:nt, :], 1.0 / D, None, op0=ALU.mult), reads=[s1.b], writes=[mu.b])
                S.op("dve", lambda e: e.tensor_tensor(out=var[:nt, :], in0=mu[:nt, :], in1=mu[:nt, :], op=ALU.mult), reads=[mu.b], writes=[var.b])
                S.op("dve", lambda e: e.scalar_tensor_tensor(out=var[:nt, :], in0=s2[:nt, :], scalar=1.0 / D, in1=var[:nt, :], op0=ALU.mult, op1=ALU.subtract),
                     reads=[s2.b, var.b], writes=[var.b])
                S.op("dve", lambda e: e.tensor_scalar(var[:nt, :], var[:nt, :], LN_EPS, None, op0=ALU.add), reads=[var.b], writes=[var.b])
                S.op("act", lambda e: e.activation(rstd[:nt, :], var[:nt, :], AF.Sqrt), reads=[var.b], writes=[rstd.b])
                S.op("dve", lambda e: e.reciprocal(rstd[:nt, :], rstd[:nt, :]), reads=[rstd.b], writes=[rstd.b])
                S.op("dve", lambda e: e.tensor_scalar(out_t[:nt, :], z[:nt, :], mu[:nt, 0:1], rstd[:nt, 0:1], op0=ALU.subtract, op1=ALU.mult),
                     reads=[z.b, mu.b, rstd.b], writes=[out_t.b])
                S.op("dve", lambda e: e.tensor_tensor(out=out_t[:nt, :], in0=out_t[:nt, :], in1=lg_bc[:nt, :], op=ALU.mult), reads=[out_t.b, lg_bc.b], writes=[out_t.b])
                S.op("dve", lambda e: e.tensor_tensor(out=out_t[:nt, :], in0=out_t[:nt, :], in1=lb_bc[:nt, :], op=ALU.add), reads=[out_t.b, lb_bc.b], writes=[out_t.b])

            LNT = [(128 * t, 128) for t in range(16)] + [(2048, 32)]
            SAMP_SEGS = [(8 * b, 8, 1 + b) for b in range(4)]

            with ExitStack() as p4:
                lnrow = sb(p4, [1, 2, D], F32, "lnrow")
                S.dma("sp", lambda e: e.dma_start(out=lnrow[:, 0, :], in_=ln1_g[:, :]), writes=[lnrow.b])
                S.dma("sp", lambda e: e.dma_start(out=lnrow[:, 1, :], in_=ln1_b[:, :]), writes=[lnrow.b], add=True)
                lg1 = sb(p4, [128, D], F32, "lg1"); lb1 = sb(p4, [128, D], F32, "lb1")
                g1p = sb(p4, [128, D], F32, "g1p"); g1s = sb(p4, [32, D], F32, "g1s")
                bc_rows(lg1, 128, ones_f[0:1, :], ones_f.b, lambda hf: lnrow[0:1, 0, hf * 512:(hf + 1) * 512], lnrow.b)
                bc_rows(lb1, 128, ones_f[0:1, :], ones_f.b, lambda hf: lnrow[0:1, 1, hf * 512:(hf + 1) * 512], lnrow.b)
                bc_rows(g1p, 128, selp_sb[:, :], selp_sb.b, lambda hf: modrow_g[:, 0, hf * 512:(hf + 1) * 512], modrow_g.b)
                bc_rows(g1s, 32, sels_sb[:, :], sels_sb.b, lambda hf: modrow_g[:, 0, hf * 512:(hf + 1) * 512], modrow_g.b)
                woa = sb(p4, [64, 8, D], BF16, "woa"); woc = sb(p4, [128, 4, D], BF16, "woc"); wout = sb(p4, [128, 8, D], BF16, "wout")
                S.dma("pool", lambda e: e.dma_start(out=woa[:], in_=w_o_attn.rearrange("(h p) n -> p h n", p=64)), writes=[woa.b])
                S.dma("pool", lambda e: e.dma_start(out=woc[:], in_=w_o_conv.rearrange("(c p) n -> p c n", p=128)), writes=[woc.b])
                S.dma("pool", lambda e: e.dma_start(out=wout[:], in_=w_out.rearrange("(c p) n -> p c n", p=128)), writes=[wout.b])
                wgs = [sb(p4, [128, 8, 2, 128], BF16, "wg") for _ in range(2)]
                h1g = sb(p4, [128, 8, 512], BF16, "h1g")
                mT = sb(p4, [128, 8, 512], BF16, "mT")
                xts = [sb(p4, [128, D], F32, "xt4") for _ in range(2)]
                sg = [sb(p4, [128, 512], F32, "sg") for _ in range(2)]
                m1 = [sb(p4, [128, 512], F32, "m1") for _ in range(2)]
                zt = [sb(p4, [128, D], F32, "zt") for _ in range(1)]
                x1t = [sb(p4, [128, D], F32, "x1t") for _ in range(2)]
                lnst = dict(s1=sb(p4, [128, 1], F32, "s1"), s2=sb(p4, [128, 1], F32, "s2"), jk=sb(p4, [128, D], BF16, "jk"),
                            mu=sb(p4, [128, 1], F32, "mu"), var=sb(p4, [128, 1], F32, "var"), rstd=sb(p4, [128, 1], F32, "rstd"))
                for (g0, gn) in GROUPS:
                    if g0 < 2048:
                        tl = [(g0 + 128 * i, 128) for i in range(4)]
                        for (tok0, nt) in tl:
                            load_h_T(xts, h1g, h1g.b, tok0 - g0, tok0, nt, 0, 0, 8, xp[tok0:tok0 + nt, :])
                    else:
                        tl = [(2048, 32)]
                        load_h_T(xts, h1g, h1g.b, 0, 2048, 32, SAMP_SEGS, 0, 8, xs[:, :])
                    for j in range(8):
                        wg = wgs[j % 2]
                        for k2 in range(2):
                            S.dma("pool", lambda e, wg=wg, k2=k2, j=j: e.dma_start(out=wg[:, :, k2, :], in_=w_in_r[:, :, 2628 + 1024 * k2 + 128 * j:2628 + 1024 * k2 + 128 * (j + 1)]),
                                  writes=[wg.b], add=(k2 > 0))
                        pa1 = psr("mg", [2, 3, 4, 5]); pa2 = psr("mg", [2, 3, 4, 5]); pga = psr("mg", [2, 3, 4, 5]); pgb = psr("mg", [2, 3, 4, 5])
                        for h in range(8):
                            S.op("pe", lambda e, h=h, j=j, pa1=pa1, g0=g0, gn=gn: e.matmul(pa1[:, 0:gn], woa[:, h, j * 128:(j + 1) * 128], o_attnT[:, h, g0:g0 + gn], start=(h == 0), stop=(h == 7)),
                                 reads=[woa.b, o_attnT.b], writes=[pa1.b], add=(h > 0))
                        for cc in range(4):
                            S.op("pe", lambda e, cc=cc, j=j, pa2=pa2, g0=g0, gn=gn: e.matmul(pa2[:, 0:gn], woc[:, cc, j * 128:(j + 1) * 128], o_convT[:, cc, g0:g0 + gn], start=(cc == 0), stop=(cc == 3)),
                                 reads=[woc.b, o_convT.b], writes=[pa2.b], add=(cc > 0))
                        for k2, pg_ in enumerate((pga, pgb)):
                            for kc in range(8):
                                S.op("pe", lambda e, kc=kc, k2=k2, pg_=pg_, wg=wg, gn=gn: e.matmul(pg_[:, 0:gn], wg[:, kc, k2, :], h1g[:, kc, 0:gn], start=(kc == 0), stop=(kc == 7)),
                                     reads=[wg.b, h1g.b], writes=[pg_.b], add=(kc > 0))
                        sa = sg[0]; sb_ = sg[1]; ma = m1[0]; mb = m1[1]
                        S.op("act", lambda e, pga=pga, sa=sa, gn=gn: e.activation(sa[:, 0:gn], pga[:, 0:gn], AF.Sigmoid), reads=[pga.b], writes=[sa.b])
                        S.op("act", lambda e, pgb=pgb, sb_=sb_, gn=gn: e.activation(sb_[:, 0:gn], pgb[:, 0:gn], AF.Sigmoid), reads=[pgb.b], writes=[sb_.b])
                        S.op("dve", lambda e, sa=sa, pa1=pa1, ma=ma, gn=gn: e.tensor_tensor(out=ma[:, 0:gn], in0=sa[:, 0:gn], in1=pa1[:, 0:gn], op=ALU.mult), reads=[sa.b, pa1.b], writes=[ma.b])
                        S.op("dve", lambda e, sb_=sb_, pa2=pa2, mb=mb, gn=gn: e.tensor_tensor(out=mb[:, 0:gn], in0=sb_[:, 0:gn], in1=pa2[:, 0:gn], op=ALU.mult), reads=[sb_.b, pa2.b], writes=[mb.b])
                        S.op("dve", lambda e, ma=ma, mb=mb, j=j, gn=gn: e.tensor_tensor(out=mT[:, j, 0:gn], in0=ma[:, 0:gn], in1=mb[:, 0:gn], op=ALU.add), reads=[ma.b, mb.b], writes=[mT.b], add=True)
                    for (tok0, nt) in tl:
                        c0 = tok0 - g0
                        xt = xts[rr.get("xt", 0) % 2]; rr["xt"] = rr.get("xt", 0) + 1
                        src = xp[tok0:tok0 + nt, :] if tok0 < 2048 else xs[:, :]
                        S.dma("sp", lambda e, xt=xt, nt=nt, src=src: e.dma_start(out=xt[:nt, :], in_=src), writes=[xt.b])
                        z = zt[0]; rr["zt"] = rr.get("zt", 0) + 1
                        gbc = g1p if tok0 < 2048 else g1s
                        for hf in range(2):
                            po = psr("mo", [6, 7])
                            for kc in range(8):
                                S.op("pe", lambda e, kc=kc, po=po, hf=hf, c0=c0, nt=nt: e.matmul(po[:nt, :], mT[:, kc, c0:c0 + nt], wout[:, kc, hf * 512:(hf + 1) * 512], start=(kc == 0), stop=(kc == 7)),
                                     reads=[mT.b, wout.b], writes=[po.b], add=(kc > 0))
                            S.op("dve", lambda e, po=po, z=z, hf=hf, nt=nt, gbc=gbc: e.tensor_tensor(out=z[:nt, hf * 512:(hf + 1) * 512], in0=po[:nt, :], in1=gbc[:nt, hf * 512:(hf + 1) * 512], op=ALU.mult),
                                 reads=[po.b, gbc.b], writes=[z.b], add=True)
                        S.op("dve", lambda e, z=z, xt=xt, nt=nt: e.scalar_tensor_tensor(out=z[:nt, :], in0=xt[:nt, :], scalar=ALPHA, in1=z[:nt, :], op0=ALU.mult, op1=ALU.add),
                             reads=[xt.b, z.b], writes=[z.b])
                        x1 = x1t[rr.get("x1", 0) % 2]; rr["x1"] = rr.get("x1", 0) + 1
                        if DBG.get('ydump') == 'z' and tok0 == 0:
                            S.dma("sp", lambda e, z=z: e.dma_start(out=y_p[0:128, :], in_=z[:, :]), reads=[z.b])
                            S.dma("sp", lambda e: e.dma_start(out=y_p[128:256, :], in_=g1p[:, :]), reads=[g1p.b])
                            S.dma("sp", lambda e: e.dma_start(out=y_p[256:384, :], in_=lg1[:, :]), reads=[lg1.b])
                            S.dma("sp", lambda e: e.dma_start(out=y_p[512:640, :], in_=lb1[:, :]), reads=[lb1.b])
                            S.dma("sp", lambda e, xt=xt: e.dma_start(out=y_p[640:768, :], in_=xt[:, :]), reads=[xt.b])
                        layer_norm(lnst, z, nt, lg1, lb1, x1)
                        if DBG.get('ydump') == 'z' and tok0 == 0:
                            S.dma("sp", lambda e, x1=x1: e.dma_start(out=y_p[768:896, :], in_=x1[:, :]), reads=[x1.b])
                        S.dma("sp", lambda e, x1=x1, nt=nt, tok0=tok0: e.dma_start(out=x1_scr[tok0:tok0 + nt, :], in_=x1[:nt, :]), reads=[x1.b], writes=[b_x1scr], sembuf=x1.b, add=True)
                        if DBG.get('ydump') == 'x1':
                            dstx = y_p[tok0:tok0 + nt, :] if tok0 < 2048 else y_s[:, :]
                            S.dma("sp", lambda e, x1=x1, nt=nt, dstx=dstx: e.dma_start(out=dstx, in_=x1[:nt, :]), reads=[x1.b])
                S.barrier(flush=True)
            act_es.close()
            S.barrier()

            stop_check(4)
            with ExitStack() as p5:
                lnrow = sb(p5, [1, 2, D], F32, "lnrow2")
                S.dma("sp", lambda e: e.dma_start(out=lnrow[:, 0, :], in_=ln2_g[:, :]), writes=[lnrow.b])
                S.dma("sp", lambda e: e.dma_start(out=lnrow[:, 1, :], in_=ln2_b[:, :]), writes=[lnrow.b], add=True)
                lg2 = sb(p5, [128, D], F32, "lg2"); lb2 = sb(p5, [128, D], F32, "lb2")
                g2p = sb(p5, [128, D], F32, "g2p"); g2s = sb(p5, [32, D], F32, "g2s")
                bc_rows(lg2, 128, ones_f[0:1, :], ones_f.b, lambda hf: lnrow[0:1, 0, hf * 512:(hf + 1) * 512], lnrow.b)
                bc_rows(lb2, 128, ones_f[0:1, :], ones_f.b, lambda hf: lnrow[0:1, 1, hf * 512:(hf + 1) * 512], lnrow.b)
                bc_rows(g2p, 128, selp_sb[:, :], selp_sb.b, lambda hf: modrow_g[:, 1, hf * 512:(hf + 1) * 512], modrow_g.b)
                bc_rows(g2s, 32, sels_sb[:, :], sels_sb.b, lambda hf: modrow_g[:, 1, hf * 512:(hf + 1) * 512], modrow_g.b)
                wpq = sb(p5, [128, 8, D], BF16, "wpq")
                S.dma("pool", lambda e: e.dma_start(out=wpq[:], in_=peer_wq.rearrange("(c p) n -> p c n", p=128)), writes=[wpq.b])
                iotaA = sb(p5, [128, 32, 128], BF16, "iotaA")
                S.op("pool", lambda e: e.iota(iotaA[:], [[0, 32], [1, 128]], base=0, channel_multiplier=0, allow_small_or_imprecise_dtypes=True), writes=[iotaA.b])
                x1ts = [sb(p5, [128, D], F32, "x1l") for _ in range(2)]
                h2Ts = [sb(p5, [128, 8, 128], BF16, "h2T") for _ in range(2)]
                qpT = sb(p5, [128, 8, 128], BF16, "qpT")
                Spe = sb(p5, [128, 8, 256], F32, "Spe")
                v12 = sb(p5, [128, 8, 2, 16], F32, "v12")
                i12 = sb(p5, [128, 8, 2, 16], U32, "i12")
                i12f = sb(p5, [128, 8, 2, 16], F32, "i12f")
                wk = sb(p5, [128, 256], F32, "wk")
                cand = sb(p5, [128, 8, 256], F32, "cand")
                sv = sb(p5, [128, 8, 16], F32, "sv")
                si = sb(p5, [128, 8, 16], U32, "si")
                sij = sb(p5, [128, 2, 8, 16], U32, "sij")
                sijf = sb(p5, [128, 2, 8, 16], F32, "sijf")
                eq = sb(p5, [128, 16, 16], F32, "eq")
                abw = sb(p5, [128, 3, 128], F32, "abw")
                zs = sb(p5, [128, 8], F32, "zs")
                abwT = sb(p5, [128, 3, 128], F32, "abwT")
                OA = sb(p5, [128, 32, 128], BF16, "OA"); OB = sb(p5, [128, 32, 128], BF16, "OB")
                GT = sb(p5, [128, 128, 128], BF16, "GT")
                utb = [sb(p5, [128, 4, 1024], BF16, "utl") for _ in range(2)]
                vtb = [sb(p5, [128, 4, 1024], BF16, "vtl") for _ in range(2)]
                xs_ = [sb(p5, [128, 512], F32, "gx") for _ in range(2)]
                us_ = [sb(p5, [128, 512], F32, "gu") for _ in range(2)]
                ws_ = [sb(p5, [128, 512], F32, "gw") for _ in range(2)]
                PTs = [sb(p5, [128, 512], BF16, "PT") for _ in range(2)]
                zt = [sb(p5, [128, D], F32, "zt5") for _ in range(1)]
                yt = zt
                jk5 = TL(OA.t[:, 0:8, :].rearrange("p a b -> p (a b)"), "jk5"); jk5.b = OA.b
                lnst = dict(s1=sb(p5, [128, 1], F32, "s1"), s2=sb(p5, [128, 1], F32, "s2"), jk=jk5,
                            mu=sb(p5, [128, 1], F32, "mu"), var=sb(p5, [128, 1], F32, "var"), rstd=sb(p5, [128, 1], F32, "rstd"))
                ut_r = ut_scr.rearrange("(a4 c) p k -> a4 p c k", c=4)
                v_r = v_scr.rearrange("(a4 c p) d -> a4 p c d", c=4, p=128)
                def front(ti):
                    tok0, nt = LNT[ti]
                    x1l = x1ts[ti % 2]; h2T = h2Ts[ti % 2]
                    segs = 0 if tok0 < 2048 else SAMP_SEGS
                    yield
                    load_h_T([x1l], h2T, h2T.b, 0, tok0, nt, segs, 24, 32, x1_scr[tok0:tok0 + nt, :])
                    rr["xt"] -= 1
                    for hd in range(8):
                        pm = psr("pq", [0, 1])
                        for kc in range(8):
                            yield
                            S.op("pe", lambda e, kc=kc, pm=pm, hd=hd, nt=nt: e.matmul(pm[:, 0:nt], wpq[:, kc, hd * 128:(hd + 1) * 128], h2T[:, kc, 0:nt], start=(kc == 0), stop=(kc == 7)),
                                 reads=[wpq.b, h2T.b], writes=[pm.b], add=(kc > 0))
                        yield
                        evac(qpT[:, hd, 0:nt], qpT.b, pm[:, 0:nt], pm.b)
                    for hd in range(8):
                        pm = psr("pq", [0, 1])
                        yield
                        S.op("pe", lambda e, pm=pm, hd=hd, nt=nt: e.matmul(pm[:nt, 0:256], qpT[:, hd, 0:nt], KBD[:, :], start=True, stop=True), reads=[qpT.b, KBD.b], writes=[pm.b])
                        yield
                        evac(Spe[:nt, hd, :], Spe.b, pm[:nt, 0:256], pm.b)
                    for hd in range(8):
                        for hf in range(2):
                            src = Spe[:nt, hd, hf * 128:(hf + 1) * 128]
                            yield
                            S.op("dve", lambda e, src=src, hd=hd, hf=hf, nt=nt: e.max(out=v12[:nt, hd, hf, 0:8], in_=src), reads=[Spe.b], writes=[v12.b], add=True)
                            yield
                            S.op("dve", lambda e, src=src, hd=hd, hf=hf, nt=nt: e.match_replace(out=wk[:nt, 0:128], in_to_replace=v12[:nt, hd, hf, 0:8], in_values=src, imm_value=-1e30),
                                 reads=[Spe.b, v12.b], writes=[wk.b])
                            yield
                            S.op("dve", lambda e, hd=hd, hf=hf, nt=nt: e.max(out=v12[:nt, hd, hf, 8:16], in_=wk[:nt, 0:128]), reads=[wk.b], writes=[v12.b], add=True)
                            yield
                            S.op("dve", lambda e, src=src, hd=hd, hf=hf, nt=nt: e.max_index(out=i12[:nt, hd, hf, 0:8], in_max=v12[:nt, hd, hf, 0:8], in_values=src),
                                 reads=[Spe.b, v12.b], writes=[i12.b], add=True)
                            yield
                            S.op("dve", lambda e, src=src, hd=hd, hf=hf, nt=nt: e.max_index(out=i12[:nt, hd, hf, 8:16], in_max=v12[:nt, hd, hf, 8:16], in_values=src),
                                 reads=[Spe.b, v12.b], writes=[i12.b], add=True)
                    yield
                    S.op("dve", lambda e, nt=nt: e.tensor_copy(i12f[:nt].rearrange("p h f k -> p (h f k)"), i12[:nt].rearrange("p h f k -> p (h f k)")), reads=[i12.b], writes=[i12f.b])
                    for hd in range(8):
                        yield
                        S.op("dve", lambda e, hd=hd, nt=nt: e.tensor_tensor(out=cand[:nt, hd, :].rearrange("p (i j) -> p i j", i=16),
                                                                           in0=v12[:nt, hd, 0, :].unsqueeze(2).to_broadcast([nt, 16, 16]),
                                                                           in1=v12[:nt, hd, 1, :].unsqueeze(1).to_broadcast([nt, 16, 16]), op=ALU.add),
                             reads=[v12.b], writes=[cand.b], add=True)
                    for hd in range(8):
                        src = cand[:nt, hd, :]
                        yield
                        S.op("dve", lambda e, src=src, hd=hd, nt=nt: e.max(out=sv[:nt, hd, 0:8], in_=src), reads=[cand.b], writes=[sv.b], add=True)
                        yield
                        S.op("dve", lambda e, src=src, hd=hd, nt=nt: e.match_replace(out=wk[:nt, :], in_to_replace=sv[:nt, hd, 0:8], in_values=src, imm_value=-1e30),
                             reads=[cand.b, sv.b], writes=[wk.b])
                        yield
                        S.op("dve", lambda e, hd=hd, nt=nt: e.max(out=sv[:nt, hd, 8:16], in_=wk[:nt, :]), reads=[wk.b], writes=[sv.b], add=True)
                        yield
                        S.op("dve", lambda e, src=src, hd=hd, nt=nt: e.max_index(out=si[:nt, hd, 0:8], in_max=sv[:nt, hd, 0:8], in_values=src), reads=[cand.b, sv.b], writes=[si.b], add=True)
                        yield
                        S.op("dve", lambda e, src=src, hd=hd, nt=nt: e.max_index(out=si[:nt, hd, 8:16], in_max=sv[:nt, hd, 8:16], in_values=src), reads=[cand.b, sv.b], writes=[si.b], add=True)
                    wv = abw[:nt, 2, :].rearrange("p (h k) -> p h k", h=8)
                    yield
                    S.op("dve", lambda e, nt=nt, wv=wv: e.tensor_tensor(out=wv, in0=sv[:nt, :, :], in1=sv[:nt, :, 0:1].to_broadcast([nt, 8, 16]), op=ALU.subtract),
                         reads=[sv.b], writes=[abw.b], add=True)
                    yield
                    S.op("act", lambda e, nt=nt: e.activation(abw[:nt, 2, :], abw[:nt, 2, :], AF.Exp), reads=[abw.b], writes=[abw.b])
                    yield
                    S.op("dve", lambda e, nt=nt, wv=wv: e.tensor_reduce(out=zs[:nt, :], in_=wv, axis=AX.X, op=ALU.add), reads=[abw.b], writes=[zs.b])
                    yield
                    S.op("dve", lambda e, nt=nt: e.reciprocal(zs[:nt, :], zs[:nt, :]), reads=[zs.b], writes=[zs.b])
                    yield
                    S.op("dve", lambda e, nt=nt, wv=wv: e.tensor_tensor(out=wv, in0=wv, in1=zs[:nt, :].unsqueeze(2).to_broadcast([nt, 8, 16]), op=ALU.mult),
                         reads=[abw.b, zs.b], writes=[abw.b])
                    yield
                    S.op("dve", lambda e, nt=nt: e.tensor_single_scalar(sij[:nt, 0].rearrange("p h k -> p (h k)"), si[:nt].rearrange("p h k -> p (h k)"), 4, op=ALU.logical_shift_right),
                         reads=[si.b], writes=[sij.b], add=True)
                    yield
                    S.op("dve", lambda e, nt=nt: e.tensor_single_scalar(sij[:nt, 1].rearrange("p h k -> p (h k)"), si[:nt].rearrange("p h k -> p (h k)"), 15, op=ALU.bitwise_and),
                         reads=[si.b], writes=[sij.b], add=True)
                    yield
                    S.op("dve", lambda e, nt=nt: e.tensor_copy(sijf[:nt].rearrange("p t h k -> p (t h k)"), sij[:nt].rearrange("p t h k -> p (t h k)")), reads=[sij.b], writes=[sijf.b])
                    for hd in range(8):
                        for ab in range(2):
                            yield
                            S.op("dve", lambda e, hd=hd, ab=ab, nt=nt: e.tensor_tensor(out=eq[:nt], in0=sijf[:nt, ab, hd, :].unsqueeze(2).to_broadcast([nt, 16, 16]),
                                                                                     in1=iota16[:nt, :].unsqueeze(1).to_broadcast([nt, 16, 16]), op=ALU.is_equal),
                                 reads=[sijf.b, iota16.b], writes=[eq.b])
                            yield
                            S.op("dve", lambda e, hd=hd, ab=ab, nt=nt: e.tensor_tensor(out=eq[:nt], in0=eq[:nt], in1=i12f[:nt, hd, ab, :].unsqueeze(1).to_broadcast([nt, 16, 16]), op=ALU.mult),
                                 reads=[eq.b, i12f.b], writes=[eq.b])
                            yield
                            S.op("dve", lambda e, hd=hd, ab=ab, nt=nt: e.tensor_reduce(out=abw[:nt, ab, hd * 16:(hd + 1) * 16], in_=eq[:nt], axis=AX.X, op=ALU.add),
                                 reads=[eq.b], writes=[abw.b], add=True)
                    pm = psr("pq", [0, 1])
                    for k3 in range(3):
                        yield
                        S.op("pe", lambda e, pm=pm, k3=k3, nt=nt: e.transpose(pm[:, k3 * 128:k3 * 128 + nt], abw[:nt, k3, :], ident[:nt, :nt]), reads=[abw.b, ident.b], writes=[pm.b], add=(k3 > 0))
                    yield
                    S.op("act", lambda e, pm=pm: e.activation(abwT[:].rearrange("p k n -> p (k n)"), pm[:, 0:384], AF.Copy), reads=[pm.b], writes=[abwT.b])
                    yield
                def tail(ti):
                    tok0, nt = LNT[ti]
                    for n0 in range(0, nt, 32):
                        nn = min(32, nt - n0)
                        S.op("dve", lambda e, n0=n0, nn=nn: e.tensor_tensor(out=OA[:, 0:nn, :], in0=iotaA[:, 0:nn, :], in1=abwT[:, 0, n0:n0 + nn].unsqueeze(2).to_broadcast([128, nn, 128]), op=ALU.is_equal),
                             reads=[iotaA.b, abwT.b], writes=[OA.b])
                        S.op("pool", lambda e, n0=n0, nn=nn: e.tensor_tensor(out=OA[:, 0:nn, :], in0=OA[:, 0:nn, :], in1=abwT[:, 2, n0:n0 + nn].unsqueeze(2).to_broadcast([128, nn, 128]), op=ALU.mult),
                             reads=[OA.b, abwT.b], writes=[OA.b])
                        S.op("dve", lambda e, n0=n0, nn=nn: e.tensor_tensor(out=OB[:, 0:nn, :], in0=iotaA[:, 0:nn, :], in1=abwT[:, 1, n0:n0 + nn].unsqueeze(2).to_broadcast([128, nn, 128]), op=ALU.is_equal),
                             reads=[iotaA.b, abwT.b], writes=[OB.b])
                        for n4 in range(0, nn, 4):
                            pg_ = psr("pg", [2, 3])
                            for q in range(4):
                                nl = n4 + q
                                S.op("pe", lambda e, pg_=pg_, q=q, nl=nl: e.matmul(pg_[:, q * 128:(q + 1) * 128], OB[:, nl, :], OA[:, nl, :], start=True, stop=True),
                                     reads=[OA.b, OB.b], writes=[pg_.b], add=(q > 0))
                            evac(GT[:].rearrange("p a n -> p n a")[:, n0 + n4:n0 + n4 + 4, :], GT.b, pg_[:, :].rearrange("p (n a) -> p n a", n=4), pg_.b)
                def main(ti, fr):
                    tok0, nt = LNT[ti]
                    x1l = x1ts[ti % 2]; h2T = h2Ts[ti % 2]
                    oacc = [PS[6], PS[7]]
                    def stage_ab(a4, nt=nt):
                        ub = utb[a4 % 2]; vb = vtb[a4 % 2]
                        S.dma("sp", lambda e, ub=ub, a4=a4: e.dma_start(out=ub[:], in_=ut_r[a4]), reads=[b_utscr], writes=[ub.b])
                        S.dma("sp", lambda e, vb=vb, a4=a4: e.dma_start(out=vb[:], in_=v_r[a4]), reads=[b_vscr], writes=[vb.b])
                        pa = PS[4 + a4 % 2]
                        for c in range(4):
                            for kc in range(8):
                                S.op("pe", lambda e, pa=pa, c=c, kc=kc, ub=ub, nt=nt: e.matmul(pa[:, c * 128:c * 128 + nt], ub[:, c, kc * 128:(kc + 1) * 128], h2T[:, kc, 0:nt], start=(kc == 0), stop=(kc == 7)),
                                     reads=[ub.b, h2T.b], writes=[pa.b], add=(c > 0 or kc > 0))
                        gx = xs_[a4 % 2]; gu = us_[a4 % 2]

                        def v3(t_, nt=nt):
                            return t_[:, :].rearrange("p (c n) -> p c n", c=4)[:, :, 0:nt]
                        pav = v3(pa); gxv = v3(gx); guv = v3(gu)
                        S.op("act", lambda e, gxv=gxv, pav=pav: e.activation(gxv, pav, AF.Copy), reads=[pa.b], writes=[gx.b])
                        S.op("act", lambda e, guv=guv, pav=pav: e.activation(guv, pav, AF.Square), reads=[pa.b], writes=[gu.b])
                        S.op("dve", lambda e, guv=guv: e.tensor_scalar(guv, guv, 0.044715, 1.0, op0=ALU.mult, op1=ALU.add), reads=[gu.b], writes=[gu.b])

                    def stage_cde(a4, nt=nt):
                        vb = vtb[a4 % 2]
                        gx = xs_[a4 % 2]; gu = us_[a4 % 2]; gw = ws_[a4 % 2]; PT = PTs[a4 % 2]

                        def v3(t_, nt=nt):
                            return t_[:, :].rearrange("p (c n) -> p c n", c=4)[:, :, 0:nt]
                        gxv = v3(gx); guv = v3(gu); gwv = v3(gw); PTv = v3(PT)
                        gtv = GT[:, a4 * 4:(a4 + 1) * 4, 0:nt]
                        S.op("pool", lambda e, guv=guv, gxv=gxv, gwv=gwv: e.tensor_tensor(out=gwv, in0=guv, in1=gxv, op=ALU.mult), reads=[gu.b, gx.b], writes=[gw.b])
                        S.op("act", lambda e, gwv=gwv: e.activation(gwv, gwv, AF.Sigmoid, scale=1.5957691216057308), reads=[gw.b], writes=[gw.b])
                        S.op("dve", lambda e, gwv=gwv, gxv=gxv: e.tensor_tensor(out=gxv, in0=gwv, in1=gxv, op=ALU.mult), reads=[gw.b, gx.b], writes=[gx.b])
                        S.op("dve", lambda e, gxv=gxv, PTv=PTv, gtv=gtv: e.tensor_tensor(out=PTv, in0=gxv, in1=gtv, op=ALU.mult),
                             reads=[gx.b, GT.b], writes=[PT.b])
                        for c in range(4):
                            for hf in range(2):
                                first = (a4 == 0 and c == 0)
                                last = (a4 == 31 and c == 3)
                                S.op("pe", lambda e, PT=PT, c=c, hf=hf, vb=vb, first=first, last=last, nt=nt: e.matmul(oacc[hf][:nt, :], PT[:, c * 128:c * 128 + nt], vb[:, c, hf * 512:(hf + 1) * 512], start=first, stop=last),
                                     reads=[PT.b, vb.b], writes=[oacc[hf].b], add=(not first))

                    stage_ab(0)
                    for a4 in range(32):
                        if a4 + 1 < 32:
                            stage_ab(a4 + 1)
                        stage_cde(a4)
                        if fr is not None:
                            for _ in range(16):
                                next(fr, None)
                    if pe_dbg:
                        y = yt[0]
                        for hf in range(2):
                            S.op("act", lambda e, y=y, hf=hf, nt=nt: e.activation(y[:nt, hf * 512:(hf + 1) * 512], oacc[hf][:nt, :], AF.Copy), reads=[oacc[hf].b], writes=[y.b], add=True)
                        dstd = y_p[tok0:tok0 + nt, :] if tok0 < 2048 else y_s[:, :]
                        S.dma("sp", lambda e, y=y, nt=nt, dstd=dstd: e.dma_start(out=dstd, in_=y[:nt, :]), reads=[y.b])
                    z = zt[0]; rr["zt5"] = rr.get("zt5", 0) + 1
                    gbc = g2p if tok0 < 2048 else g2s
                    for hf in range(2):
                        S.op("dve", lambda e, z=z, hf=hf, nt=nt, gbc=gbc: e.tensor_tensor(out=z[:nt, hf * 512:(hf + 1) * 512], in0=oacc[hf][:nt, :], in1=gbc[:nt, hf * 512:(hf + 1) * 512], op=ALU.mult),
                             reads=[oacc[hf].b, gbc.b], writes=[z.b], add=True)
                    S.op("dve", lambda e, z=z, x1l=x1l, nt=nt: e.scalar_tensor_tensor(out=z[:nt, :], in0=x1l[:nt, :], scalar=ALPHA, in1=z[:nt, :], op0=ALU.mult, op1=ALU.add),
                         reads=[x1l.b, z.b], writes=[z.b])
                    y = yt[0]; rr["yt"] = rr.get("yt", 0) + 1
                    layer_norm(lnst, z, nt, lg2, lb2, y)
                    dst = y_p[tok0:tok0 + nt, :] if tok0 < 2048 else y_s[:, :]
                    if not DBG.get('ydump'):
                        S.dma("sp", lambda e, y=y, nt=nt, dst=dst: e.dma_start(out=dst, in_=y[:nt, :]), reads=[y.b])
                def drain(g):
                    if g is not None:
                        for _ in g:
                            pass
                drain(front(0))
                for ti in range(len(LNT)):
                    tail(ti)
                    fr = front(ti + 1) if ti + 1 < len(LNT) else None
                    main(ti, fr)
                    drain(fr)
                S.barrier(flush=True)
        except _Stop:
            for nm in ('att_es', 'act_es'):
                st_ = locals().get(nm)
                if st_ is not None:
                    st_.close()
        S.finish()
        print("instructions:", S.ninstr, "sems:", S.semid)
    return nc


_CACHE = {}


def kernel(**inp):
    f32 = np.float32
    g = lambda k: np.ascontiguousarray(np.asarray(inp[k]))
    if "nc" not in _CACHE:
        _CACHE["nc"] = build_program()
    nc = _CACHE["nc"]
    j = np.arange(383)
    bk = t5_bucket_np(j - 127)
    ohb = np.zeros((32, 383), f32); ohb[bk, j] = 1.0
    negrow = np.where(j < 127, NEG, 0.0).astype(f32)[None, :]
    sel_p = np.zeros((5, 128), f32); sel_p[0, :] = 1.0
    sel_s = np.zeros((5, 32), f32)
    for b in range(4):
        sel_s[1 + b, 8 * b:8 * b + 8] = 1.0
    xpr, xsm, cpr, csm = g("x_prompt"), g("x_sample"), g("c_prompt"), g("c_sample")
    pr = DBG['pool_rows']
    ck = g("cache_k")[0].reshape(2560 * 128, 128)[:pr]; cv = g("cache_v")[0].reshape(2560 * 128, 128)[:pr]
    cki = g("cache_kidx")[0].reshape(2560 * 128, 64)[:pr]
    sc = g("state_conv")[0]; pt = g("page_table")
    shared = dict(ck=ck, cv=cv, cki=cki, rel_bias=g("rel_bias"), w_ada=g("w_ada")[0], b_ada=g("b_ada"), w_in=g("w_in")[0],
                  conv_w=g("conv_w")[0], conv_b=g("conv_b"), w_o_attn=g("w_o_attn")[0], w_o_conv=g("w_o_conv")[0], w_out=g("w_out")[0],
                  ln1_g=g("ln1_g"), ln1_b=g("ln1_b"), ln2_g=g("ln2_g"), ln2_b=g("ln2_b"), peer_wq=g("peer_wq")[0],
                  peer_k1=g("peer_k1")[0], peer_k2=g("peer_k2")[0], peer_u=g("peer_u")[0], peer_v=g("peer_v")[0],
                  ohb=ohb, negrow=negrow, sel_p=sel_p, sel_s=sel_s)
    in_maps = []
    for c in range(8):
        m = dict(shared)
        m["xp"] = xpr[c]
        m["xs"] = np.ascontiguousarray(xsm[4 * c:4 * c + 4].reshape(32, D))
        m["cvec"] = np.ascontiguousarray(np.concatenate([cpr[c:c + 1], csm[4 * c:4 * c + 4]], axis=0))
        m["sconv"] = np.ascontiguousarray(sc[4 * c:4 * c + 4].reshape(8, 512))
        m["ptab"] = np.ascontiguousarray(pt[4 * c:4 * c + 4]).astype(np.int32)
        in_maps.append(m)
    res = run_bass_kernel_spmd(nc, in_maps, core_ids=list(range(8)))
    R = res.results
    y_prompt = np.stack([R[c]["y_p"] for c in range(8)])
    y_sample = np.concatenate([R[c]["y_s"].reshape(4, 8, D) for c in range(8)])
    k_prompt = np.stack([R[c]["k_p"].reshape(SEQ, 2, 64) for c in range(8)])[None]
    v_prompt = np.stack([R[c]["v_p"].reshape(SEQ, 2, 64) for c in range(8)])[None]
    ki_prompt = np.stack([R[c]["ki_p"] for c in range(8)])[None]
    conv_prompt = np.stack([R[c]["conv_o"][0:2] for c in range(8)])[None]
    k_sample = np.concatenate([R[c]["k_s"].reshape(4, 8, 2, 64) for c in range(8)])[None]
    v_sample = np.concatenate([R[c]["v_s"].reshape(4, 8, 2, 64) for c in range(8)])[None]
    ki_sample = np.concatenate([R[c]["ki_s"].reshape(4, 8, 64) for c in range(8)])[None]
    conv_sample = np.concatenate([R[c]["conv_o"][2:10].reshape(4, 2, 512) for c in range(8)])[None]
    outs = (y_prompt, y_sample, k_prompt, v_prompt, ki_prompt, conv_prompt, k_sample, v_sample, ki_sample, conv_sample)
    return tuple(np.ascontiguousarray(o, dtype=f32) for o in outs)
```

```python
import math
from contextlib import ExitStack
import numpy as np
import concourse.bass as bass
import concourse.mybir as mybir
from concourse.bass_utils import run_bass_kernel_spmd

F32 = mybir.dt.float32
BF16 = mybir.dt.bfloat16
U32 = mybir.dt.uint32
I32 = mybir.dt.int32
ALU = mybir.AluOpType
AF = mybir.ActivationFunctionType
AX = mybir.AxisListType

D = 1024
SEQ = 2048
NTOK = 2080
NEG = -30000.0
ATTN_SCALE = 64 ** -0.5
IDX_SCALE = 256 ** -0.5
ALPHA = 2 ** 0.25
LN_EPS = 1e-5
NBIS = 26
STOP_AFTER = [99]
DBG = dict(strict=True, pool_rows=2560 * 128, prep=128, tblocks=list(range(16)), batches=list(range(4)), npages=64)


class _Stop(Exception):
    pass


_DISCARD = [False]


def stop_check(k):
    if STOP_AFTER[0] <= k:
        _DISCARD[0] = True
EPOCH = 16000


class Buf:
    __slots__ = ("name", "w", "r", "dsem", "dcnt")

    def __init__(self, name):
        self.name = name
        self.w = []
        self.r = []
        self.dsem = None
        self.dcnt = 0


class Tok:
    __slots__ = ("sem", "val", "eng", "grp")

    def __init__(self, sem, val, eng, grp=False):
        self.sem, self.val, self.eng, self.grp = sem, val, eng, grp


class Sched:
    ENG = ("pe", "act", "dve", "pool", "sp")

    def __init__(self, nc, es):
        self.nc = nc
        self.es = es
        self.q = {e: [] for e in self.ENG}
        self.n = {e: 0 for e in self.ENG}
        self.esem = {e: None for e in self.ENG}
        self.waited = {e: {} for e in self.ENG}
        self.semid = 0
        self.dma_sems = []
        self.last = {}
        self.ninstr = 0

    def new_sem(self, tag):
        self.semid += 1
        return self.es.enter_context(self.nc.semaphore(f"s{self.semid}_{tag}"))

    def _deps(self, eng, reads, writes, add):
        strict = DBG.get("strict")
        pe_chain = (eng == "pe" and add)
        deps = []
        for b in reads:
            for t in b.w:
                if t.eng == eng and eng == "pe" and not strict:
                    continue
                deps.append(t)
        for b in writes:
            for t in b.r:
                if t.eng == eng and t.eng is not None and (not strict or pe_chain):
                    continue
                deps.append(t)
            for t in b.w:
                if add and t.grp:
                    continue
                if t.eng == eng and t.eng is not None and (not strict or pe_chain):
                    continue
                deps.append(t)
        best = {}
        for t in deps:
            k = id(t.sem)
            if k not in best or best[k].val < t.val:
                best[k] = t
        out = []
        wd = self.waited[eng]
        for k, t in best.items():
            if wd.get(k, -1) >= t.val:
                continue
            wd[k] = t.val
            out.append((t.sem, t.val))
        return out

    def _commit(self, tok, reads, writes, add):
        for b in reads:
            b.r = [t for t in b.r if t.sem is not tok.sem] + [tok]
        for b in writes:
            if add:
                b.w = [t for t in b.w if t.sem is not tok.sem] + [tok]
            else:
                b.w = [tok]
                b.r = []

    def op(self, eng, fn, reads=(), writes=(), add=False):
        if _DISCARD[0]:
            return
        waits = self._deps(eng, reads, writes, add)
        if self.esem[eng] is None or self.n[eng] >= EPOCH:
            self.esem[eng] = self.new_sem(eng)
            self.n[eng] = 0
        self.n[eng] += 1
        sem, val = self.esem[eng], self.n[eng]
        tok = Tok(sem, val, eng, add)
        self.last[eng] = tok
        self._commit(tok, reads, writes, add)

        def thunk(e, fn=fn, waits=waits, sem=sem):
            for (s, v) in waits:
                e.wait_ge(s, v)
            fn(e).then_inc(sem, 1)
        self.q[eng].append(thunk)
        self.ninstr += 1

    def dma(self, eng, fn, reads=(), writes=(), sembuf=None, add=False):
        if _DISCARD[0]:
            return
        waits = self._deps(eng, reads, writes, add)
        sb = sembuf if sembuf is not None else (writes[0] if writes else reads[0])
        kind = "sw" if eng == "pool" else "hw"
        if sb.dsem is None:
            sb.dsem = {}
        if kind not in sb.dsem:
            ent = [self.new_sem("d" + kind), 0]
            sb.dsem[kind] = ent
            self.dma_sems.append(ent)
        ent = sb.dsem[kind]
        ent[1] += 16
        tok = Tok(ent[0], ent[1], None, add)
        self._commit(tok, reads, writes, add)

        def thunk(e, fn=fn, waits=waits, sem=ent[0]):
            for (s, v) in waits:
                e.wait_ge(s, v)
            fn(e).then_inc(sem, 16)
        self.q[eng].append(thunk)
        self.ninstr += 1

    def barrier(self, force=False, flush=False):
        if not (_DISCARD[0] and not force):
            toks = [t for t in self.last.values()]
            toks += [Tok(ent[0], ent[1], None) for ent in self.dma_sems]
            for eng in self.ENG:
                waits = []
                wd = self.waited[eng]
                for t in toks:
                    if t.eng == eng:
                        continue
                    k = id(t.sem)
                    if wd.get(k, -1) >= t.val:
                        continue
                    wd[k] = t.val
                    waits.append((t.sem, t.val))

                def thunk(e, waits=waits):
                    for (s, v) in waits:
                        e.wait_ge(s, v)
                self.q[eng].append(thunk)
        if flush:
            self.flush()

    def flush(self):
        nc = self.nc
        q = self.q
        if not any(q[e] for e in self.ENG):
            return
        with nc.Block() as block:
            @block.tensor
            def _(e):
                for t in q["pe"]:
                    t(e)

            @block.scalar
            def _(e):
                for t in q["act"]:
                    t(e)

            @block.vector
            def _(e):
                for t in q["dve"]:
                    t(e)

            @block.gpsimd
            def _(e):
                for t in q["pool"]:
                    t(e)

            @block.sync
            def _(e):
                for t in q["sp"]:
                    t(e)
        self.q = {e: [] for e in self.ENG}

    def finish(self):
        self.barrier(force=True)
        self.flush()


class TL:
    def __init__(self, t, name):
        self.t = t
        self.b = Buf(name)

    def __getitem__(self, k):
        return self.t[k]


def t5_bucket_np(d):
    n = np.maximum(d, 0)
    nf = np.maximum(n, 1).astype(np.float32)
    large = 16 + (np.log(nf / np.float32(16)) / np.float32(math.log(128 / 16)) * np.float32(16)).astype(np.int32)
    large = np.minimum(large, 31)
    return np.where(n < 16, n, large)


def build_program():
    nc = bass.Bass("TRN2", target_bir_lowering=False)
    _DISCARD[0] = False

    def din(name, shape, dt=F32):
        return nc.dram_tensor(name, list(shape), dt, kind="ExternalInput").ap()

    def dout(name, shape, dt=F32):
        return nc.dram_tensor(name, list(shape), dt, kind="ExternalOutput").ap()

    xp = din("xp", [SEQ, D]); xs = din("xs", [32, D]); cvec = din("cvec", [5, D])
    ck = din("ck", [DBG["pool_rows"], 128]); cvv = din("cv", [DBG["pool_rows"], 128]); cki = din("cki", [DBG["pool_rows"], 64])
    sconv = din("sconv", [8, 512]); ptab = din("ptab", [4, 64], I32)
    rel_bias = din("rel_bias", [32, 8])
    w_ada = din("w_ada", [D, 6 * D]); b_ada = din("b_ada", [1, 6 * D])
    w_in = din("w_in", [D, 4676])
    conv_w = din("conv_w", [3, 512]); conv_b = din("conv_b", [1, 512])
    w_o_attn = din("w_o_attn", [512, D]); w_o_conv = din("w_o_conv", [512, D]); w_out = din("w_out", [D, D])
    ln1_g = din("ln1_g", [1, D]); ln1_b = din("ln1_b", [1, D]); ln2_g = din("ln2_g", [1, D]); ln2_b = din("ln2_b", [1, D])
    peer_wq = din("peer_wq", [D, D]); peer_k1 = din("peer_k1", [128, 64]); peer_k2 = din("peer_k2", [128, 64])
    peer_u = din("peer_u", [16384, D]); peer_v = din("peer_v", [16384, D])
    ohb = din("ohb", [32, 383]); negrow = din("negrow", [1, 383])
    sel_p = din("sel_p", [5, 128]); sel_s = din("sel_s", [5, 32])

    y_p = dout("y_p", [SEQ, D]); y_s = dout("y_s", [32, D])
    k_p = dout("k_p", [SEQ, 128]); v_p = dout("v_p", [SEQ, 128]); ki_p = dout("ki_p", [SEQ, 64])
    conv_o = dout("conv_o", [10, 512])
    k_s = dout("k_s", [32, 128]); v_s = dout("v_s", [32, 128]); ki_s = dout("ki_s", [32, 64])

    x1_scr = nc.dram_tensor("x1_scr", [NTOK, D], F32).ap()
    pe_dbg = DBG.get("ydump") == "pe"
    tsc = nc.dram_tensor("tsc", [8, 128, 383], F32).ap()
    ut_scr = nc.dram_tensor("ut_scr", [128, 128, 1024], BF16).ap()
    v_scr = nc.dram_tensor("v_scr", [16384, D], BF16).ap()
    b_x1scr = Buf("x1scr"); b_tsc = Buf("tsc"); b_utscr = Buf("utscr"); b_vscr = Buf("vscr")
    b_out = Buf("outputs")

    w_in_r = w_in.rearrange("(kc p) n -> p kc n", p=128)

    with ExitStack() as es:
        S = Sched(nc, es)
        cnt = [0]

        def sb(stack, shape, dt, name=None):
            cnt[0] += 1
            nm = f"{name or 't'}{cnt[0]}"
            return TL(stack.enter_context(nc.sbuf_tensor(nm, list(shape), dt)), nm)

        PS = []
        for i in range(8):
            PS.append(TL(es.enter_context(nc.psum_tensor(f"ps{i}", [128, 512], F32)), f"ps{i}"))
        rr = {}

        def psr(key, banks):
            i = rr.get(key, 0)
            rr[key] = i + 1
            return PS[banks[i % len(banks)]]

        evq = [0]

        def evac(out_ap, out_b, in_ap, in_b, engs=("act", "dve")):
            e = engs[evq[0] % len(engs)]
            evq[0] += 1
            if e == "act":
                S.op("act", lambda en: en.activation(out_ap, in_ap, AF.Copy), reads=[in_b], writes=[out_b], add=True)
            else:
                S.op("dve", lambda en: en.tensor_copy(out_ap, in_ap), reads=[in_b], writes=[out_b], add=True)

        cs = es
        io = sb(cs, [128, 128], F32, "io")
        ident = sb(cs, [128, 128], F32, "ident")
        negA = sb(cs, [128, 128], F32, "negA")
        iota16 = sb(cs, [128, 16], F32, "iota16")
        iotap = sb(cs, [128, 1], F32, "iotap")
        ones_f = sb(cs, [128, 128], F32, "ones")
        modT = sb(cs, [128, 48, 5], F32, "modT")
        modrow_g = sb(cs, [5, 2, 1024], F32, "modrowg")
        Bm = sb(cs, [128, 8, 2, 128], F32, "Bm")
        KBD = sb(cs, [128, 256], BF16, "KBD")
        selp_sb = sb(cs, [5, 128], F32, "selp")
        sels_sb = sb(cs, [5, 32], F32, "sels")
        S.op("pool", lambda e: e.iota(io[:], [[1, 128]], base=0, channel_multiplier=-1, allow_small_or_imprecise_dtypes=True), writes=[io.b])
        S.op("pool", lambda e: e.iota(iota16[:], [[1, 16]], base=0, channel_multiplier=0, allow_small_or_imprecise_dtypes=True), writes=[iota16.b])
        S.op("pool", lambda e: e.iota(iotap[:], [[0, 1]], base=0, channel_multiplier=1, allow_small_or_imprecise_dtypes=True), writes=[iotap.b])
        S.op("dve", lambda e: e.tensor_scalar(ident[:], io[:], 0.0, None, op0=ALU.is_equal), reads=[io.b], writes=[ident.b])
        S.op("dve", lambda e: e.tensor_scalar(negA[:], io[:], 0.0, NEG, op0=ALU.is_gt, op1=ALU.mult), reads=[io.b], writes=[negA.b])
        S.op("dve", lambda e: e.memset(ones_f[:], 1.0), writes=[ones_f.b])
        S.op("pool", lambda e: e.memset(KBD[:], 0.0), writes=[KBD.b])
        S.dma("sp", lambda e: e.dma_start(out=selp_sb[:], in_=sel_p[:, :]), writes=[selp_sb.b])
        S.dma("sp", lambda e: e.dma_start(out=sels_sb[:], in_=sel_s[:, :]), writes=[sels_sb.b])

        try:
            with ExitStack() as p0:
                cv_sb = sb(p0, [5, D], F32, "cv")
                cT = sb(p0, [128, 8, 5], F32, "cT")
                S.dma("sp", lambda e: e.dma_start(out=cv_sb[:], in_=cvec[:, :]), writes=[cv_sb.b])
                ps = PS[0]
                for kc in range(8):
                    S.op("pe", lambda e, kc=kc: e.transpose(ps[:, kc * 5:(kc + 1) * 5], cv_sb[:, kc * 128:(kc + 1) * 128], ident[:5, :5]),
                         reads=[cv_sb.b, ident.b], writes=[ps.b], add=True)
                S.op("dve", lambda e: e.tensor_copy(cT[:].rearrange("p k c -> p (k c)"), ps[:, 0:40]), reads=[ps.b], writes=[cT.b])
                modrow = sb(p0, [5, 6 * D], F32, "modrow")
                was = [sb(p0, [128, 8, 512], F32, "wa") for _ in range(2)]
                bas = [sb(p0, [1, 512], F32, "ba") for _ in range(2)]
                w_ada_r = w_ada.rearrange("(kc p) n -> p kc n", p=128)
                for cg in range(12):
                    wa = was[cg % 2]; ba = bas[cg % 2]
                    S.dma("sp", lambda e, wa=wa, cg=cg: e.dma_start(out=wa[:], in_=w_ada_r[:, :, cg * 512:(cg + 1) * 512]), writes=[wa.b])
                    S.dma("sp", lambda e, ba=ba, cg=cg: e.dma_start(out=ba[:], in_=b_ada[:, cg * 512:(cg + 1) * 512]), writes=[ba.b])
                    pm = psr("mod", [1, 2])
                    for kc in range(8):
                        S.op("pe", lambda e, kc=kc, wa=wa, pm=pm: e.matmul(pm[0:5, :], cT[:, kc, :], wa[:, kc, :], start=(kc == 0), stop=False),
                             reads=[cT.b, wa.b], writes=[pm.b], add=(kc > 0))
                    S.op("pe", lambda e, ba=ba, pm=pm: e.matmul(pm[0:5, :], ones_f[0:1, 0:5], ba[0:1, :], start=False, stop=True),
                         reads=[ones_f.b, ba.b], writes=[pm.b], add=True)
                    S.op("act", lambda e, pm=pm, cg=cg: e.activation(modrow[:, cg * 512:(cg + 1) * 512], pm[0:5, :], AF.Copy),
                         reads=[pm.b], writes=[modrow.b], add=True)
                S.op("dve", lambda e: e.tensor_copy(modrow_g[:, 0, :], modrow[:, 2 * D:3 * D]), reads=[modrow.b], writes=[modrow_g.b])
                S.op("dve", lambda e: e.tensor_copy(modrow_g[:, 1, :], modrow[:, 5 * D:6 * D]), reads=[modrow.b], writes=[modrow_g.b], add=True)
                for jg in range(2):
                    pm = psr("mod", [1, 2])
                    for jj in range(24):
                        j = jg * 24 + jj
                        S.op("pe", lambda e, j=j, jj=jj, pm=pm: e.transpose(pm[:, jj * 5:(jj + 1) * 5], modrow[:, j * 128:(j + 1) * 128], ident[:5, :5]),
                             reads=[modrow.b, ident.b], writes=[pm.b], add=(jj > 0))
                    S.op("dve", lambda e, jg=jg, pm=pm: e.tensor_copy(modT[:, jg * 24:(jg + 1) * 24, :].rearrange("p j c -> p (j c)"), pm[:, 0:120]),
                         reads=[pm.b], writes=[modT.b], add=True)
                for j0 in (8, 32):
                    S.op("dve", lambda e, j0=j0: e.tensor_scalar(modT[:, j0:j0 + 8, :], modT[:, j0:j0 + 8, :], 1.0, None, op0=ALU.add),
                         reads=[modT.b], writes=[modT.b])
                rb_sb = sb(p0, [32, 8], F32, "rb")
                rbrep = sb(p0, [32, 8, 128], F32, "rbrep")
                ohb_sb = sb(p0, [32, 383], F32, "ohb")
                neg_sb = sb(p0, [1, 383], F32, "negr")
                tv = sb(p0, [128, 8, 383], F32, "tv")
                b31 = sb(p0, [128, 8], F32, "b31")
                S.dma("sp", lambda e: e.dma_start(out=rb_sb[:], in_=rel_bias[:, :]), writes=[rb_sb.b])
                S.dma("sp", lambda e: e.dma_start(out=ohb_sb[:], in_=ohb[:, :]), writes=[ohb_sb.b])
                S.dma("sp", lambda e: e.dma_start(out=neg_sb[:], in_=negrow[:, :]), writes=[neg_sb.b])
                for h in range(8):
                    S.op("dve", lambda e, h=h: e.tensor_copy(rbrep[:, h, :], rb_sb[:, h:h + 1].to_broadcast([32, 128])),
                         reads=[rb_sb.b], writes=[rbrep.b], add=True)
                for h in range(8):
                    pm = psr("mod", [1, 2])
                    S.op("pe", lambda e, h=h, pm=pm: e.matmul(pm[:, 0:383], rbrep[:, h, :], ohb_sb[:, :], start=True, stop=False),
                         reads=[rbrep.b, ohb_sb.b], writes=[pm.b])
                    S.op("pe", lambda e, pm=pm: e.matmul(pm[:, 0:383], ones_f[0:1, :], neg_sb[0:1, :], start=False, stop=True),
                         reads=[ones_f.b, neg_sb.b], writes=[pm.b], add=True)
                    S.op("act", lambda e, h=h, pm=pm: e.activation(b31[:, h:h + 1], pm[:, 382:383], AF.Copy), reads=[pm.b], writes=[b31.b], add=True)
                    S.op("dve", lambda e, h=h, pm=pm: e.tensor_scalar(tv[:, h, :], pm[:, 0:383], b31[:, h:h + 1], None, op0=ALU.subtract),
                         reads=[pm.b, b31.b], writes=[tv.b], add=True)
                S.dma("sp", lambda e: e.dma_start(out=tsc.rearrange("h p j -> p h j"), in_=tv[:]), reads=[tv.b], writes=[b_tsc])
                for h in range(8):
                    for ci, c in enumerate((0, 128)):
                        src = bass.AP(tsc.tensor, h * 128 * 383 + 127 + c, [[382, 128], [1, 128]])
                        S.dma("sp", lambda e, h=h, ci=ci, src=src: e.dma_start(out=Bm[:, h, ci, :], in_=src), reads=[b_tsc], writes=[Bm.b], add=True)
                k12 = sb(p0, [128, 128], F32, "k12")
                S.dma("sp", lambda e: e.dma_start(out=k12[:, 0:64], in_=peer_k1[:, :]), writes=[k12.b])
                S.dma("sp", lambda e: e.dma_start(out=k12[:, 64:128], in_=peer_k2[:, :]), writes=[k12.b], add=True)
                pm = psr("mod", [1, 2])
                S.op("pe", lambda e, pm=pm: e.transpose(pm[:, 0:128], k12[:], ident[:]), reads=[k12.b, ident.b], writes=[pm.b])
                S.op("dve", lambda e, pm=pm: e.tensor_copy(KBD[0:64, 0:128], pm[0:64, 0:128]), reads=[pm.b], writes=[KBD.b], add=True)
                S.op("dve", lambda e, pm=pm: e.tensor_copy(KBD[64:128, 128:256], pm[64:128, 0:128]), reads=[pm.b], writes=[KBD.b], add=True)

                ust = [sb(p0, [128, D], F32, "ust") for _ in range(2)]
                utb = [sb(p0, [128, 8, 128], BF16, "utb") for _ in range(2)]
                vst = [sb(p0, [128, D], BF16, "vst") for _ in range(2)]
                for a in range(DBG['prep']):
                    us = ust[a % 2]; ub = utb[a % 2]; vs_ = vst[a % 2]
                    S.dma("sp", lambda e, us=us, a=a: e.dma_start(out=us[:], in_=peer_u[a * 128:(a + 1) * 128, :]), writes=[us.b])
                    for hb in range(2):
                        pm = psr("prep", [3, 4, 5, 6])
                        for k4 in range(4):
                            kc = hb * 4 + k4
                            S.op("pe", lambda e, kc=kc, k4=k4, pm=pm, us=us: e.transpose(pm[:, k4 * 128:(k4 + 1) * 128], us[:, kc * 128:(kc + 1) * 128], ident[:]),
                                 reads=[us.b, ident.b], writes=[pm.b], add=(k4 > 0))
                        evac(ub[:, hb * 4:(hb + 1) * 4, :].rearrange("p k e -> p (k e)"), ub.b, pm[:, :], pm.b)
                    S.dma("sp", lambda e, ub=ub, a=a: e.dma_start(out=ut_scr[a], in_=ub[:].rearrange("p k e -> p (k e)")), reads=[ub.b], writes=[b_utscr], sembuf=ub.b, add=True)
                    S.dma("pool", lambda e, vs_=vs_, a=a: e.dma_start(out=vs_[:], in_=peer_v[a * 128:(a + 1) * 128, :]), writes=[vs_.b])
                    S.dma("sp", lambda e, vs_=vs_, a=a: e.dma_start(out=v_scr[a * 128:(a + 1) * 128, :], in_=vs_[:]), reads=[vs_.b], writes=[b_vscr], sembuf=vs_.b, add=True)
                S.barrier(flush=True)
            S.barrier()

            stop_check(0)
            TILES = [(128 * t, 128, 0) for t in range(16)] + [(2048 + 8 * b, 8, 1 + b) for b in range(4)]

            def load_h_T(stack_tiles, dstT, dst_b, col0, tok0, nt, cidx, shj, scj, src_ap):
                xt = stack_tiles[rr.get("xt", 0) % len(stack_tiles)]
                rr["xt"] = rr.get("xt", 0) + 1
                S.dma("sp", lambda e: e.dma_start(out=xt[:nt, :], in_=src_ap), reads=[b_x1scr], writes=[xt.b])
                for hb in range(2):
                    pm = psr("tr", [0, 1])
                    for k4 in range(4):
                        kc = hb * 4 + k4
                        S.op("pe", lambda e, kc=kc, k4=k4, pm=pm: e.transpose(pm[:, k4 * 128:k4 * 128 + nt], xt[:nt, kc * 128:(kc + 1) * 128], ident[:nt, :nt]),
                             reads=[xt.b, ident.b], writes=[pm.b], add=(k4 > 0))
                    for k4 in range(4):
                        kc = hb * 4 + k4
                        if isinstance(cidx, int):
                            segs = [(0, nt, cidx)]
                        else:
                            segs = cidx
                        for (c0, cn, ci) in segs:
                            S.op("act", lambda e, kc=kc, k4=k4, pm=pm, c0=c0, cn=cn, ci=ci: e.activation(
                                dstT[:, kc, col0 + c0:col0 + c0 + cn], pm[:, k4 * 128 + c0:k4 * 128 + c0 + cn], AF.Identity,
                                bias=modT[:, shj + kc, ci:ci + 1], scale=modT[:, scj + kc, ci:ci + 1]),
                                reads=[pm.b, modT.b], writes=[dst_b], add=True)

            act_es = ExitStack()
            o_convT = sb(act_es, [128, 4, NTOK], BF16, "oconvT")
            o_attnT = sb(act_es, [64, 8, NTOK], BF16, "oattnT")
            att_es = ExitStack()
            qT = sb(att_es, [128, 4, NTOK], BF16, "qT")
            kTd = sb(att_es, [128, 2, NTOK], BF16, "kTd")
            qiT = sb(att_es, [128, 2, NTOK], BF16, "qiT")
            kiTd = sb(att_es, [128, NTOK], BF16, "kiTd")
            Vx = sb(att_es, [128, 20, 2, 65], BF16, "Vx")
            wi_sb = sb(att_es, [128, 20, 4], F32, "wi")
            S.op("pool", lambda e: e.memset(Vx[:], 1.0), writes=[Vx.b])

            GROUPS = [(0, 512), (512, 512), (1024, 512), (1536, 512), (2048, 32)]

            with ExitStack() as p1:
                h1T = sb(p1, [128, 8, NTOK], BF16, "h1T")
                xts = [sb(p1, [128, D], F32, "xt") for _ in range(1)]
                for (tok0, nt, ci) in TILES:
                    src = xp[tok0:tok0 + nt, :] if tok0 < 2048 else xs[tok0 - 2048:tok0 - 2048 + nt, :]
                    load_h_T(xts, h1T, h1T.b, tok0, tok0, nt, ci, 0, 8, src)
                wq = sb(p1, [128, 8, 512], BF16, "wq")
                wkd = sb(p1, [128, 8, 2, 128], BF16, "wkd")
                wqi = sb(p1, [128, 8, 256], BF16, "wqi")
                wkid = sb(p1, [128, 8, 128], BF16, "wkid")
                wtm = sb(p1, [128, 8, 324], BF16, "wtm")
                S.dma("pool", lambda e: e.dma_start(out=wq[:], in_=w_in_r[:, :, 0:512]), writes=[wq.b])
                for g in range(2):
                    for hf in range(2):
                        S.dma("pool", lambda e, g=g, hf=hf: e.dma_start(out=wkd[:, :, g, hf * 64:(hf + 1) * 64], in_=w_in_r[:, :, 512 + 64 * g:576 + 64 * g]),
                              writes=[wkd.b], add=True)
                S.dma("pool", lambda e: e.dma_start(out=wqi[:], in_=w_in_r[:, :, 768:1024]), writes=[wqi.b])
                for hf in range(2):
                    S.dma("pool", lambda e, hf=hf: e.dma_start(out=wkid[:, :, hf * 64:(hf + 1) * 64], in_=w_in_r[:, :, 1028:1092]), writes=[wkid.b], add=True)
                S.dma("pool", lambda e: e.dma_start(out=wtm[:, :, 0:256], in_=w_in_r[:, :, 512:768]), writes=[wtm.b], add=True)
                S.dma("pool", lambda e: e.dma_start(out=wtm[:, :, 256:320], in_=w_in_r[:, :, 1028:1092]), writes=[wtm.b], add=True)
                S.dma("pool", lambda e: e.dma_start(out=wtm[:, :, 320:324], in_=w_in_r[:, :, 1024:1028]), writes=[wtm.b], add=True)
                fm_sets = []
                for j in range(4):
                    fm_sets.append((lambda kc, j=j: wq[:, kc, j * 128:(j + 1) * 128], wq.b, lambda c0, n, j=j: qT[:, j, c0:c0 + n], qT.b))
                for g in range(2):
                    fm_sets.append((lambda kc, g=g: wkd[:, kc, g, :], wkd.b, lambda c0, n, g=g: kTd[:, g, c0:c0 + n], kTd.b))
                for j in range(2):
                    fm_sets.append((lambda kc, j=j: wqi[:, kc, j * 128:(j + 1) * 128], wqi.b, lambda c0, n, j=j: qiT[:, j, c0:c0 + n], qiT.b))
                fm_sets.append((lambda kc: wkid[:, kc, :], wkid.b, lambda c0, n: kiTd[:, c0:c0 + n], kiTd.b))
                for (wf, wb, df, db) in fm_sets:
                    for (g0, gn) in GROUPS:
                        pm = psr("fm", [2, 3, 4, 5])
                        for kc in range(8):
                            S.op("pe", lambda e, kc=kc, pm=pm, wf=wf, g0=g0, gn=gn: e.matmul(pm[:, 0:gn], wf(kc), h1T[:, kc, g0:g0 + gn], start=(kc == 0), stop=(kc == 7)),
                                 reads=[wb, h1T.b], writes=[pm.b], add=(kc > 0))
                        evac(df(g0, gn), db, pm[:, 0:gn], pm.b)
                kvst = [sb(p1, [128, 324], F32, "kvst") for _ in range(2)]
                for ti, (tok0, nt, ci) in enumerate(TILES):
                    pm = psr("fm", [2, 3, 4, 5])
                    kv = kvst[ti % 2]
                    for kc in range(8):
                        S.op("pe", lambda e, kc=kc, pm=pm, tok0=tok0, nt=nt: e.matmul(pm[:nt, 0:324], h1T[:, kc, tok0:tok0 + nt], wtm[:, kc, :], start=(kc == 0), stop=(kc == 7)),
                             reads=[wtm.b, h1T.b], writes=[pm.b], add=(kc > 0))
                    S.op("act", lambda e, pm=pm, kv=kv, nt=nt: e.activation(kv[:nt, :], pm[:nt, 0:324], AF.Copy), reads=[pm.b], writes=[kv.b])
                    if tok0 < 2048:
                        ko, vo, kio, r0 = k_p, v_p, ki_p, tok0
                    else:
                        ko, vo, kio, r0 = k_s, v_s, ki_s, tok0 - 2048
                    S.dma("sp", lambda e, kv=kv, nt=nt, ko=ko, r0=r0: e.dma_start(out=ko[r0:r0 + nt, :], in_=kv[:nt, 0:128]), reads=[kv.b])
                    S.dma("sp", lambda e, kv=kv, nt=nt, vo=vo, r0=r0: e.dma_start(out=vo[r0:r0 + nt, :], in_=kv[:nt, 128:256]), reads=[kv.b])
                    S.dma("sp", lambda e, kv=kv, nt=nt, kio=kio, r0=r0: e.dma_start(out=kio[r0:r0 + nt, :], in_=kv[:nt, 256:320]), reads=[kv.b])
                    S.op("dve", lambda e, kv=kv, nt=nt, ti=ti: e.tensor_copy(Vx[:nt, ti, :, 0:64], kv[:nt, 128:256].rearrange("p (g d) -> p g d", g=2)),
                         reads=[kv.b], writes=[Vx.b], add=True)
                    S.op("dve", lambda e, kv=kv, nt=nt, ti=ti: e.tensor_copy(wi_sb[:nt, ti, :], kv[:nt, 320:324]), reads=[kv.b], writes=[wi_sb.b], add=True)

                cw = sb(p1, [128, 4, 3], F32, "cw"); cb = sb(p1, [128, 4], F32, "cb")
                for cc in range(4):
                    S.dma("sp", lambda e, cc=cc: e.dma_start(out=cw[:, cc, :], in_=conv_w[:, cc * 128:(cc + 1) * 128].rearrange("j p -> p j"), allow_slow_non_contiguous=True), writes=[cw.b], add=True)
                    S.dma("sp", lambda e, cc=cc: e.dma_start(out=cb[:, cc:cc + 1], in_=conv_b[:, cc * 128:(cc + 1) * 128].rearrange("o p -> p o"), allow_slow_non_contiguous=True), writes=[cb.b], add=True)
                Up = sb(p1, [128, 2050], F32, "Up"); Us = sb(p1, [128, 4, 10], F32, "Us")
                Useq = lambda sq: (Up[:, :] if sq == 0 else Us[:, sq - 1, :])
                Ub = lambda sq: (Up.b if sq == 0 else Us.b)
                bgT = sb(p1, [128, NTOK], BF16, "bgT")
                ycv = sb(p1, [128, 2048], F32, "ycv")
                cgs = [sb(p1, [128, 512], F32, "cgs") for _ in range(1)]
                lastU = sb(p1, [128, 4, 10], F32, "lastU")
                wcv = [sb(p1, [128, 8, 3, 128], BF16, "wcv") for _ in range(1)]
                for cc in range(4):
                    wc = wcv[0]
                    for k3 in range(3):
                        S.dma("pool", lambda e, wc=wc, k3=k3, cc=cc: e.dma_start(out=wc[:, :, k3, :], in_=w_in_r[:, :, 1092 + 512 * k3 + 128 * cc:1092 + 512 * k3 + 128 * (cc + 1)]),
                              writes=[wc.b], add=(k3 > 0))
                    S.op("pool", lambda e: e.memset(Up[:, 0:2], 0.0), writes=[Up.b], add=True)
                    for b in range(4):
                        S.dma("sp", lambda e, cc=cc, b=b: e.dma_start(out=Us[:, b, 0:2], in_=sconv[2 * b:2 * b + 2, cc * 128:(cc + 1) * 128].rearrange("t p -> p t"), allow_slow_non_contiguous=True),
                              writes=[Us.b], add=True)
                    for (g0, gn) in GROUPS:
                        pb = psr("fm", [2, 3, 4, 5]); pc = psr("fm", [2, 3, 4, 5]); px = psr("fm", [2, 3, 4, 5])
                        for k3, pp in enumerate((pb, pc, px)):
                            for kc in range(8):
                                S.op("pe", lambda e, kc=kc, pp=pp, k3=k3, wc=wc, g0=g0, gn=gn: e.matmul(pp[:, 0:gn], wc[:, kc, k3, :], h1T[:, kc, g0:g0 + gn], start=(kc == 0), stop=(kc == 7)),
                                     reads=[wc.b, h1T.b], writes=[pp.b], add=(kc > 0))
                        cgx = cgs[0]; rr["cg"] = rr.get("cg", 0) + 1
                        S.op("act", lambda e, cgx=cgx, pc=pc, gn=gn: e.activation(cgx[:, 0:gn], pc[:, 0:gn], AF.Copy), reads=[pc.b], writes=[cgx.b])
                        S.op("act", lambda e, pb=pb, g0=g0, gn=gn: e.activation(bgT[:, g0:g0 + gn], pb[:, 0:gn], AF.Copy), reads=[pb.b], writes=[bgT.b], add=True)
                        if g0 < 2048:
                            S.op("dve", lambda e, cgx=cgx, px=px, g0=g0, gn=gn: e.tensor_tensor(out=Up[:, 2 + g0:2 + g0 + gn], in0=cgx[:, 0:gn], in1=px[:, 0:gn], op=ALU.mult),
                                 reads=[cgx.b, px.b], writes=[Up.b], add=True)
                        else:
                            S.op("dve", lambda e, cgx=cgx, px=px: e.tensor_tensor(out=Us[:, :, 2:10], in0=cgx[:, 0:32].rearrange("p (b t) -> p b t", b=4),
                                                                                 in1=px[:, 0:32].rearrange("p (b t) -> p b t", b=4), op=ALU.mult),
                                 reads=[cgx.b, px.b], writes=[Us.b], add=True)
                    for (sq, T_, c0) in [(0, 2048, 0)] + [(1 + b, 8, 2048 + 8 * b) for b in range(4)]:
                        S.op("dve", lambda e, sq=sq, T_=T_, cc=cc: e.tensor_scalar(ycv[:, 0:T_], Useq(sq)[:, 0:T_], cw[:, cc, 0:1], cb[:, cc:cc + 1], op0=ALU.mult, op1=ALU.add),
                             reads=[Ub(sq), cw.b, cb.b], writes=[ycv.b])
                        for j in (1, 2):
                            S.op("dve", lambda e, sq=sq, T_=T_, cc=cc, j=j: e.scalar_tensor_tensor(out=ycv[:, 0:T_], in0=Useq(sq)[:, j:j + T_], scalar=cw[:, cc, j:j + 1], in1=ycv[:, 0:T_], op0=ALU.mult, op1=ALU.add),
                                 reads=[Ub(sq), cw.b, ycv.b], writes=[ycv.b])
                        S.op("dve", lambda e, T_=T_, cc=cc, c0=c0: e.tensor_tensor(out=o_convT[:, cc, c0:c0 + T_], in0=ycv[:, 0:T_], in1=bgT[:, c0:c0 + T_], op=ALU.mult),
                             reads=[ycv.b, bgT.b], writes=[o_convT.b], add=True)
                        S.op("act", lambda e, sq=sq, T_=T_, cc=cc: e.activation(lastU[:, cc, 2 * sq:2 * sq + 2], Useq(sq)[:, T_:T_ + 2], AF.Copy), reads=[Ub(sq)], writes=[lastU.b], add=True)
                pm = psr("fm", [2, 3, 4, 5])
                for cc in range(4):
                    S.op("pe", lambda e, cc=cc, pm=pm: e.transpose(pm[0:10, cc * 128:(cc + 1) * 128], lastU[:, cc, :], ident[:, :]),
                         reads=[lastU.b, ident.b], writes=[pm.b], add=(cc > 0))
                cst = sb(p1, [10, 512], F32, "cst")
                S.op("act", lambda e, pm=pm: e.activation(cst[:], pm[0:10, :], AF.Copy), reads=[pm.b], writes=[cst.b])
                S.dma("sp", lambda e: e.dma_start(out=conv_o[:, :], in_=cst[:]), reads=[cst.b])
                S.barrier(flush=True)
            S.barrier()

            stop_check(2)
            def attention(st, blocks, nq, qtok0, wi_ap, L, need_topk, S_sc, kiT_ap_fn):
                rt, cntt, lo, W, mid, tt, mn, mx_ = st["rt"], st["cnt"], st["lo"], st["W"], st["mid"], st["tt"], st["mn"], st["mx"]
                junk = st["junk"]
                k0 = 0
                while k0 < L:
                    n = min(512, L - k0)
                    kap, kb = kiT_ap_fn(k0, n)
                    for h in range(4):
                        pm = PS[h % 2]
                        hp = (h % 2) * 64
                        S.op("pe", lambda e, pm=pm, h=h, hp=hp, kap=kap, n=n: e.matmul(pm[:nq, 0:n], qiT[hp:hp + 64, h // 2, qtok0:qtok0 + nq], kap[hp:hp + 64, :], start=True, stop=True),
                             reads=[qiT.b, kb], writes=[pm.b])
                        r = rt[rr.get("rt", 0) % 2]; rr["rt"] = rr.get("rt", 0) + 1
                        S.op("act", lambda e, pm=pm, r=r, n=n: e.activation(r[:nq, 0:n], pm[:nq, 0:n], AF.Relu, scale=IDX_SCALE), reads=[pm.b], writes=[r.b])
                        if h == 0:
                            S.op("dve", lambda e, r=r, n=n, k0=k0: e.tensor_scalar(S_sc[:nq, k0:k0 + n], r[:nq, 0:n], wi_ap[:, 0:1], None, op0=ALU.mult),
                                 reads=[r.b, wi_sb.b], writes=[S_sc.b], add=True)
                        else:
                            S.op("dve", lambda e, r=r, n=n, k0=k0, h=h: e.scalar_tensor_tensor(out=S_sc[:nq, k0:k0 + n], in0=r[:nq, 0:n], scalar=wi_ap[:, h:h + 1], in1=S_sc[:nq, k0:k0 + n], op0=ALU.mult, op1=ALU.add),
                                 reads=[r.b, wi_sb.b, S_sc.b], writes=[S_sc.b])
                    k0 += n
                stop_check(2.1)
                nk0 = blocks[-1]["nk"]
                if need_topk:
                    S.op("dve", lambda e: e.tensor_reduce(out=mx_[:nq, :], in_=S_sc[:nq, 0:L], axis=AX.X, op=ALU.max), reads=[S_sc.b], writes=[mx_.b])
                    S.op("dve", lambda e: e.tensor_reduce(out=mn[:nq, :], in_=S_sc[:nq, 0:L], axis=AX.X, op=ALU.min), reads=[S_sc.b], writes=[mn.b])
                    S.op("dve", lambda e: e.tensor_scalar(lo[:nq, :], mn[:nq, :], -1.0, None, op0=ALU.add), reads=[mn.b], writes=[lo.b])
                    S.op("dve", lambda e: e.scalar_tensor_tensor(out=W[:nq, :], in0=mx_[:nq, :], scalar=2.0, in1=mn[:nq, :], op0=ALU.add, op1=ALU.subtract),
                         reads=[mx_.b, mn.b], writes=[W.b])
                S.op("dve", lambda e: e.tensor_tensor(out=S_sc[:nq, L - nk0:L], in0=S_sc[:nq, L - nk0:L], in1=negA[:nq, :nk0], op=ALU.add),
                     reads=[S_sc.b, negA.b], writes=[S_sc.b])
                if need_topk:
                    S.op("dve", lambda e: e.memset(cntt[:], 0.0), writes=[cntt.b])
                    for it in range(NBIS):
                        ck_ = 2.0 ** -(it + 1)
                        S.op("dve", lambda e, ck_=ck_: e.scalar_tensor_tensor(out=mid[:nq, :], in0=W[:nq, :], scalar=ck_, in1=lo[:nq, :], op0=ALU.mult, op1=ALU.add),
                             reads=[W.b, lo.b], writes=[mid.b])
                        S.op("dve", lambda e, it=it: e.tensor_scalar(junk[:nq, 0:L], S_sc[:nq, 0:L], mid[:nq, 0:1], 0.0, op0=ALU.is_gt, op1=ALU.add, accum_out=cntt[:nq, it:it + 1]),
                             reads=[S_sc.b, mid.b, cntt.b], writes=[junk.b, cntt.b])
                        S.op("dve", lambda e, it=it, ck_=ck_: e.tensor_scalar(tt[:nq, :], cntt[:nq, it:it + 1], 256.0, ck_, op0=ALU.is_ge, op1=ALU.mult),
                             reads=[cntt.b], writes=[tt.b])
                        S.op("dve", lambda e: e.scalar_tensor_tensor(out=lo[:nq, :], in0=tt[:nq, :], scalar=W[:nq, 0:1], in1=lo[:nq, :], op0=ALU.mult, op1=ALU.add),
                             reads=[tt.b, W.b, lo.b], writes=[lo.b])
                    S.op("dve", lambda e: e.tensor_scalar(S_sc[:nq, 0:L], S_sc[:nq, 0:L], lo[:nq, 0:1], None, op0=ALU.is_gt), reads=[S_sc.b, lo.b], writes=[S_sc.b])
                else:
                    S.op("dve", lambda e: e.tensor_scalar(S_sc[:nq, 0:L], S_sc[:nq, 0:L], -10000.0, None, op0=ALU.is_gt), reads=[S_sc.b], writes=[S_sc.b])
                stop_check(2.2)
                oacc = [PS[6], PS[7]]
                kpos = 0
                nb = len(blocks)
                for bi, blk in enumerate(blocks):
                    nk = blk["nk"]
                    if blk.get("prep"):
                        blk["prep"]()
                    pmk = psr("mk", [2, 3])
                    S.op("pe", lambda e, pmk=pmk, kpos=kpos, nk=nk: e.transpose(pmk[:nk, 0:nq], S_sc[:nq, kpos:kpos + nk], ident[:nq, :nq]),
                         reads=[S_sc.b, ident.b], writes=[pmk.b])
                    stop_check(2.22)
                    addm = st["addm"][bi % 2]
                    S.op("dve", lambda e, pmk=pmk, addm=addm, nk=nk: e.tensor_scalar(addm[:nk, 0:nq], pmk[:nk, 0:nq], -1.0, -NEG, op0=ALU.add, op1=ALU.mult),
                         reads=[pmk.b], writes=[addm.b])
                    stop_check(2.25)
                    pqE = PS[4]; pqO = PS[5]
                    for h in (0, 2, 4, 6, 1, 3, 5, 7):
                        g = h // 4
                        hp = (h % 2) * 64
                        pq = pqE if hp == 0 else pqO
                        col = (h // 2) * nq
                        kap, kb = blk["KT"](g)
                        S.op("pe", lambda e, pq=pq, col=col, h=h, hp=hp, kap=kap, nk=nk: e.matmul(pq[:nk, col:col + nq], kap[hp:hp + 64, :], qT[hp:hp + 64, h // 2, qtok0:qtok0 + nq], start=True, stop=True),
                             reads=[kb, qT.b], writes=[pq.b], add=True)
                    stop_check(2.3)
                    for g in range(2):
                        hs = (4 * g, 4 * g + 2, 4 * g + 1, 4 * g + 3)
                        lg = st["lg"][rr.get("lg", 0) % 2]; rr["lg"] = rr.get("lg", 0) + 1
                        lgv = lg[:nk, 0:4 * nq].rearrange("p (s q) -> p s q", s=4)
                        for half, pq in enumerate((pqE, pqO)):
                            S.op("dve", lambda e, pq=pq, lgv=lgv, half=half, g=g, addm=addm, nk=nk: e.scalar_tensor_tensor(
                                out=lgv[:, 2 * half:2 * half + 2, :], in0=pq[:nk, 2 * g * nq:(2 * g + 2) * nq].rearrange("p (s q) -> p s q", s=2),
                                scalar=ATTN_SCALE, in1=addm[:nk, 0:nq].unsqueeze(1).to_broadcast([nk, 2, nq]), op0=ALU.mult, op1=ALU.add),
                                reads=[pq.b, addm.b], writes=[lg.b], add=True)
                        stop_check(2.32)
                        if blk["kind"] != "far":
                            ci = 0 if blk["kind"] == "near0" else 1
                            for sl in range(4):
                                S.op("pool", lambda e, lg=lg, sl=sl, hh_=hs[sl], ci=ci, nk=nk: e.tensor_tensor(out=lg[:nk, sl * nq:(sl + 1) * nq], in0=lg[:nk, sl * nq:(sl + 1) * nq], in1=Bm[:nk, hh_, ci, 0:nq], op=ALU.add),
                                     reads=[lg.b, Bm.b], writes=[lg.b])
                        stop_check(2.34)
                        pT = st["pT"][rr.get("pT", 0) % 2]; rr["pT"] = rr.get("pT", 0) + 1
                        S.op("act", lambda e, lg=lg, pT=pT, nk=nk: e.activation(pT[:nk, 0:4 * nq], lg[:nk, 0:4 * nq], AF.Exp), reads=[lg.b], writes=[pT.b])
                        stop_check(2.36)
                        vap, vb = blk["V"](g)
                        S.op("pe", lambda e, g=g, pT=pT, vap=vap, nk=nk, bi=bi: e.matmul(oacc[g][0:65, 0:4 * nq], vap, pT[:nk, 0:4 * nq], start=(bi == 0), stop=(bi == nb - 1)),
                             reads=[vb, pT.b], writes=[oacc[g].b], add=(bi > 0))
                    kpos += nk
                stop_check(2.4)
                for g in range(2):
                    den = st["den"]
                    S.op("act", lambda e, g=g: e.activation(den[64:65, 0:4 * nq], oacc[g][64:65, 0:4 * nq], AF.Copy), reads=[oacc[g].b], writes=[den.b])
                    pb_ = psr("mk", [2, 3])
                    S.op("pe", lambda e, pb_=pb_: e.matmul(pb_[0:64, 0:4 * nq], ones_f[64:65, 0:64], den[64:65, 0:4 * nq], start=True, stop=True),
                         reads=[ones_f.b, den.b], writes=[pb_.b])
                    rec = st["rec"]
                    S.op("dve", lambda e, pb_=pb_: e.reciprocal(rec[0:64, 0:4 * nq], pb_[0:64, 0:4 * nq]), reads=[pb_.b], writes=[rec.b])
                    for half in range(2):
                        S.op("dve", lambda e, g=g, half=half: e.tensor_tensor(out=o_attnT[:, 4 * g + half:4 * g + 4:2, qtok0:qtok0 + nq],
                                                                              in0=oacc[g][0:64, 0:4 * nq].rearrange("p (h q) -> p h q", h=4)[:, 2 * half:2 * half + 2, :],
                                                                              in1=rec[0:64, 0:4 * nq].rearrange("p (h q) -> p h q", h=4)[:, 2 * half:2 * half + 2, :], op=ALU.mult),
                             reads=[oacc[g].b, rec.b], writes=[o_attnT.b], add=True)

            with ExitStack() as p3:
                st = dict(
                    rt=[sb(p3, [128, 512], F32, "rt") for _ in range(2)],
                    cnt=sb(p3, [128, NBIS], F32, "cnt"), lo=sb(p3, [128, 1], F32, "lo"), W=sb(p3, [128, 1], F32, "W"),
                    mid=sb(p3, [128, 1], F32, "mid"), tt=sb(p3, [128, 1], F32, "tt"), mn=sb(p3, [128, 1], F32, "mn"), mx=sb(p3, [128, 1], F32, "mx"),
                    addm=[sb(p3, [128, 128], F32, "addm") for _ in range(2)],
                    lg=[sb(p3, [128, 512], F32, "lg") for _ in range(2)],
                    pT=[sb(p3, [128, 512], BF16, "pT") for _ in range(2)],
                    den=sb(p3, [65, 512], F32, "den"), rec=sb(p3, [64, 512], F32, "rec"),
                )
                with ExitStack() as p3a:
                    S_p = sb(p3a, [128, 2048], F32, "S_p")
                    st["junk"] = sb(p3a, [128, 2048], BF16, "junkp")
                    for t in DBG['tblocks']:
                        blocks = []
                        for j in range(t + 1):
                            kind = "near0" if j == t else ("near1" if j == t - 1 else "far")
                            blocks.append(dict(nk=128, kind=kind,
                                               KT=lambda g, j=j: (kTd[:, g, j * 128:(j + 1) * 128], kTd.b),
                                               V=lambda g, j=j: (Vx[:, j, g, :], Vx.b)))
                        attention(st, blocks, 128, 128 * t, wi_sb[:, t, :], 128 * (t + 1), t >= 2, S_p,
                                  lambda k0, n: (kiTd[:, k0:k0 + n], kiTd.b))
                    S.barrier(flush=True)
                S.barrier()
                stop_check(2.5)
                with ExitStack() as p3b:
                    S_s = sb(p3b, [8, 8208], F32, "S_s")
                    st["junk"] = sb(p3b, [8, 8208], BF16, "junks")
                    kiT_s = sb(p3b, [128, 8200], BF16, "kiT_s")
                    ptb = sb(p3b, [128, 64], I32, "ptb")
                    idx = sb(p3b, [128, 64], I32, "idx")
                    kst = [sb(p3b, [128, 64], F32, "kst") for _ in range(3)]
                    kstd = [sb(p3b, [128, 2, 64], F32, "kstd") for _ in range(3)]
                    Kdd = [sb(p3b, [128, 2, 2, 64], F32, "Kdd") for _ in range(3)]
                    Kst = [sb(p3b, [128, 128], F32, "Kst") for _ in range(3)]
                    Vst = [sb(p3b, [128, 128], F32, "Vst") for _ in range(3)]
                    KTb = [sb(p3b, [128, 2, 128], BF16, "KTb") for _ in range(3)]
                    Vxb = [sb(p3b, [128, 2, 65], BF16, "Vxb") for _ in range(3)]
                    for vx in Vxb:
                        S.op("pool", lambda e, vx=vx: e.memset(vx[:], 1.0), writes=[vx.b])
                    for b in DBG['batches']:
                        src = bass.AP(ptab.tensor, b * 64, [[0, 128], [1, 64]])
                        S.dma("sp", lambda e, src=src: e.dma_start(out=ptb[:], in_=src), writes=[ptb.b])
                        S.op("dve", lambda e: e.tensor_scalar(idx[:], ptb[:], 128.0, iotap[:, 0:1], op0=ALU.mult, op1=ALU.add), reads=[ptb.b, iotap.b], writes=[idx.b])
                        for pg in range(64):
                            ks = kst[pg % 3]
                            S.dma("pool", lambda e, ks=ks, pg=pg: e.indirect_dma_start(out=ks[:], out_offset=None, in_=cki[:, :], in_offset=bass.IndirectOffsetOnAxis(ap=idx[:, pg:pg + 1], axis=0)),
                                  reads=[idx.b], writes=[ks.b])
                            ksd = kstd[pg % 3]
                            S.op("dve", lambda e, ks=ks, ksd=ksd: e.tensor_copy(ksd[:, :, :], ks[:].unsqueeze(1).to_broadcast([128, 2, 64])), reads=[ks.b], writes=[ksd.b])
                            pm = psr("ix", [0, 1])
                            S.op("pe", lambda e, pm=pm, ksd=ksd: e.transpose(pm[:, 0:128], ksd[:].rearrange("p r d -> p (r d)"), ident[:]),
                                 reads=[ksd.b, ident.b], writes=[pm.b])
                            evac(kiT_s[:, pg * 128:(pg + 1) * 128], kiT_s.b, pm[:, 0:128], pm.b)
                        S.op("dve", lambda e, b=b: e.tensor_copy(kiT_s[:, 8192:8200], kiTd[:, 2048 + 8 * b:2056 + 8 * b]), reads=[kiTd.b], writes=[kiT_s.b], add=True)
                        blocks = []
                        for pg in range(64):
                            def prep(pg=pg):
                                Ks = Kst[pg % 3]; Vs = Vst[pg % 3]; KT_ = KTb[pg % 3]; Vb = Vxb[pg % 3]
                                S.dma("pool", lambda e: e.indirect_dma_start(out=Ks[:], out_offset=None, in_=ck[:, :], in_offset=bass.IndirectOffsetOnAxis(ap=idx[:, pg:pg + 1], axis=0)),
                                      reads=[idx.b], writes=[Ks.b])
                                S.dma("pool", lambda e: e.indirect_dma_start(out=Vs[:], out_offset=None, in_=cvv[:, :], in_offset=bass.IndirectOffsetOnAxis(ap=idx[:, pg:pg + 1], axis=0)),
                                      reads=[idx.b], writes=[Vs.b])
                                Kd = Kdd[pg % 3]
                                S.op("dve", lambda e: e.tensor_copy(Kd[:, :, :, :], Ks[:].rearrange("p (g d) -> p g d", g=2).unsqueeze(2).to_broadcast([128, 2, 2, 64])), reads=[Ks.b], writes=[Kd.b])
                                for g in range(2):
                                    pm = psr("ix", [0, 1])
                                    S.op("pe", lambda e, pm=pm, g=g: e.transpose(pm[:, 0:128], Kd[:, g, :, :].rearrange("p r d -> p (r d)"), ident[:]),
                                         reads=[Kd.b, ident.b], writes=[pm.b])
                                    evac(KT_[:, g, :], KT_.b, pm[:, 0:128], pm.b)
                                S.op("dve", lambda e: e.tensor_copy(Vb[:, :, 0:64], Vs[:].rearrange("p (g d) -> p g d", g=2)), reads=[Vs.b], writes=[Vb.b], add=True)
                            blocks.append(dict(nk=128, kind=("near1" if pg == 63 else "far"), prep=prep,
                                               KT=lambda g, pg=pg: (KTb[pg % 3][:, g, :], KTb[pg % 3].b),
                                               V=lambda g, pg=pg: (Vxb[pg % 3][:, g, :], Vxb[pg % 3].b)))
                        blocks.append(dict(nk=8, kind="near0",
                                           KT=lambda g, b=b: (kTd[:, g, 2048 + 8 * b:2056 + 8 * b], kTd.b),
                                           V=lambda g, b=b: (Vx[0:8, 16 + b, g, :], Vx.b)))
                        attention(st, blocks, 8, 2048 + 8 * b, wi_sb[0:8, 16 + b, :], 8200, True, S_s,
                                  lambda k0, n: (kiT_s[:, k0:k0 + n], kiT_s.b))
                    S.barrier(flush=True)
            att_es.close()
            S.barrier()

            stop_check(3)
            if DBG.get('ydump') == 'oc':
                for cc in range(4):
                    dsto = bass.AP(y_p.tensor, cc * 128 * 2048, [[2048, 128], [1, 2048]])
                    S.dma("pool", lambda e, cc=cc, dsto=dsto: e.dma_start(out=dsto, in_=o_convT[:, cc, 0:2048]), reads=[o_convT.b])
                for h in range(8):
                    dsto = bass.AP(y_p.tensor, (512 + h * 64) * 2048, [[2048, 64], [1, 2048]])
                    S.dma("pool", lambda e, h=h, dsto=dsto: e.dma_start(out=dsto, in_=o_attnT[:, h, 0:2048]), reads=[o_attnT.b])
                S.barrier()
            stop_check(3.5)
            def bc_rows(dst, nrows, lhsT_ap, lhs_b, rhs_fn, rhs_b):
                for hf in range(2):
                    pm = psr("bc", [0, 1])
                    S.op("pe", lambda e, pm=pm, hf=hf: e.matmul(pm[:nrows, :], lhsT_ap, rhs_fn(hf), start=True, stop=True), reads=[lhs_b, rhs_b], writes=[pm.b])
                    S.op("act", lambda e, pm=pm, hf=hf: e.activation(dst[:nrows, hf * 512:(hf + 1) * 512], pm[:nrows, :], AF.Copy), reads=[pm.b], writes=[dst.b], add=True)

            def layer_norm(stk, z, nt, lg_bc, lb_bc, out_t):
                s1 = stk["s1"]; s2 = stk["s2"]; jk = stk["jk"]
                S.op("dve", lambda e: e.memset(s1[:], 0.0), writes=[s1.b])
                S.op("dve", lambda e: e.memset(s2[:], 0.0), writes=[s2.b])
                S.op("act", lambda e: e.activation(jk[:nt, :], z[:nt, :], AF.Identity, accum_out=s1[:nt, 0:1]), reads=[z.b, s1.b], writes=[jk.b, s1.b])
                S.op("act", lambda e: e.activation(jk[:nt, :], z[:nt, :], AF.Square, accum_out=s2[:nt, 0:1]), reads=[z.b, s2.b], writes=[jk.b, s2.b])
                mu = stk["mu"]; var = stk["var"]; rstd = stk["rstd"]
                S.op("dve", lambda e: e.tensor_scalar(mu[:nt, :], s1[:nt, :], 1.0 / D, None, op0=ALU.mult), reads=[s1.b], writes=[mu.b])
                S.op("dve", lambda e: e.tensor_tensor(out=var[:nt, :], in0=mu[:nt, :], in1=mu[:nt, :], op=ALU.mult), reads=[mu.b], writes=[var.b])
                S.op("dve", lambda e: e.scalar_tensor_tensor(out=var[:nt, :], in0=s2[:nt, :], scalar=1.0 / D, in1=var[:nt, :], op0=ALU.mult, op1=ALU.subtract),
                     reads=[s2.b, var.b], writes=[var.b])
                S.op("dve", lambda e: e.tensor_scalar(var[:nt, :], var[:nt, :], LN_EPS, None, op0=ALU.add), reads=[var.b], writes=[var.b])
                S.op("act", lambda e: e.activation(rstd[:nt, :], var[:nt, :], AF.Sqrt), reads=[var.b], writes=[rstd.b])
                S.op("dve", lambda e: e.reciprocal(rstd[:nt, :], rstd[:nt, :]), reads=[rstd.b], writes=[rstd.b])
                S.op("dve", lambda e: e.tensor_scalar(out_t[:nt, :], z[:nt, :], mu[:nt, 0:1], rstd[:nt, 0:1], op0=ALU.subtract, op1=ALU.mult),
                     reads=[z.b, mu.b, rstd.b], writes=[out_t.b])
                S.op("dve", lambda e: e.tensor_tensor(out=out_t[:nt, :], in0=out_t[:nt, :], in1=lg_bc[:nt, :], op=ALU.mult), reads=[out_t.b, lg_bc.b], writes=[out_t.b])
                S.op("dve", lambda e: e.tensor_tensor(out=out_t[:nt, :], in0=out_t[:nt, :], in1=lb_bc[:nt, :], op=ALU.add), reads=[out_t.b, lb_bc.b], writes=[out_t.b])

            LNT = [(128 * t, 128) for t in range(16)] + [(2048, 32)]
            SAMP_SEGS = [(8 * b, 8, 1 + b) for b in range(4)]

            with ExitStack() as p4:
                lnrow = sb(p4, [1, 2, D], F32, "lnrow")
                S.dma("sp", lambda e: e.dma_start(out=lnrow[:, 0, :], in_=ln1_g[:, :]), writes=[lnrow.b])
                S.dma("sp", lambda e: e.dma_start(out=lnrow[:, 1, :], in_=ln1_b[:, :]), writes=[lnrow.b], add=True)
                lg1 = sb(p4, [128, D], F32, "lg1"); lb1 = sb(p4, [128, D], F32, "lb1")
                g1p = sb(p4, [128, D], F32, "g1p"); g1s = sb(p4, [32, D], F32, "g1s")
                bc_rows(lg1, 128, ones_f[0:1, :], ones_f.b, lambda hf: lnrow[0:1, 0, hf * 512:(hf + 1) * 512], lnrow.b)
                bc_rows(lb1, 128, ones_f[0:1, :], ones_f.b, lambda hf: lnrow[0:1, 1, hf * 512:(hf + 1) * 512], lnrow.b)
                bc_rows(g1p, 128, selp_sb[:, :], selp_sb.b, lambda hf: modrow_g[:, 0, hf * 512:(hf + 1) * 512], modrow_g.b)
                bc_rows(g1s, 32, sels_sb[:, :], sels_sb.b, lambda hf: modrow_g[:, 0, hf * 512:(hf + 1) * 512], modrow_g.b)
                woa = sb(p4, [64, 8, D], BF16, "woa"); woc = sb(p4, [128, 4, D], BF16, "woc"); wout = sb(p4, [128, 8, D], BF16, "wout")
                S.dma("pool", lambda e: e.dma_start(out=woa[:], in_=w_o_attn.rearrange("(h p) n -> p h n", p=64)), writes=[woa.b])
                S.dma("pool", lambda e: e.dma_start(out=woc[:], in_=w_o_conv.rearrange("(c p) n -> p c n", p=128)), writes=[woc.b])
                S.dma("pool", lambda e: e.dma_start(out=wout[:], in_=w_out.rearrange("(c p) n -> p c n", p=128)), writes=[wout.b])
                wgs = [sb(p4, [128, 8, 2, 128], BF16, "wg") for _ in range(2)]
                h1g = sb(p4, [128, 8, 512], BF16, "h1g")
                mT = sb(p4, [128, 8, 512], BF16, "mT")
                xts = [sb(p4, [128, D], F32, "xt4") for _ in range(2)]
                sg = [sb(p4, [128, 512], F32, "sg") for _ in range(2)]
                m1 = [sb(p4, [128, 512], F32, "m1") for _ in range(2)]
                zt = [sb(p4, [128, D], F32, "zt") for _ in range(1)]
                x1t = [sb(p4, [128, D], F32, "x1t") for _ in range(2)]
                lnst = dict(s1=sb(p4, [128, 1], F32, "s1"), s2=sb(p4, [128, 1], F32, "s2"), jk=sb(p4, [128, D], BF16, "jk"),
                            mu=sb(p4, [128, 1], F32, "mu"), var=sb(p4, [128, 1], F32, "var"), rstd=sb(p4, [128, 1], F32, "rstd"))
                for (g0, gn) in GROUPS:
                    if g0 < 2048:
                        tl = [(g0 + 128 * i, 128) for i in range(4)]
                        for (tok0, nt) in tl:
                            load_h_T(xts, h1g, h1g.b, tok0 - g0, tok0, nt, 0, 0, 8, xp[tok0:tok0 + nt, :])
                    else:
                        tl = [(2048, 32)]
                        load_h_T(xts, h1g, h1g.b, 0, 2048, 32, SAMP_SEGS, 0, 8, xs[:, :])
                    for j in range(8):
                        wg = wgs[j % 2]
                        for k2 in range(2):
                            S.dma("pool", lambda e, wg=wg, k2=k2, j=j: e.dma_start(out=wg[:, :, k2, :], in_=w_in_r[:, :, 2628 + 1024 * k2 + 128 * j:2628 + 1024 * k2 + 128 * (j + 1)]),
                                  writes=[wg.b], add=(k2 > 0))
                        pa1 = psr("mg", [2, 3, 4, 5]); pa2 = psr("mg", [2, 3, 4, 5]); pga = psr("mg", [2, 3, 4, 5]); pgb = psr("mg", [2, 3, 4, 5])
                        for h in range(8):
                            S.op("pe", lambda e, h=h, j=j, pa1=pa1, g0=g0, gn=gn: e.matmul(pa1[:, 0:gn], woa[:, h, j * 128:(j + 1) * 128], o_attnT[:, h, g0:g0 + gn], start=(h == 0), stop=(h == 7)),
                                 reads=[woa.b, o_attnT.b], writes=[pa1.b], add=(h > 0))
                        for cc in range(4):
                            S.op("pe", lambda e, cc=cc, j=j, pa2=pa2, g0=g0, gn=gn: e.matmul(pa2[:, 0:gn], woc[:, cc, j * 128:(j + 1) * 128], o_convT[:, cc, g0:g0 + gn], start=(cc == 0), stop=(cc == 3)),
                                 reads=[woc.b, o_convT.b], writes=[pa2.b], add=(cc > 0))
                        for k2, pg_ in enumerate((pga, pgb)):
                            for kc in range(8):
                                S.op("pe", lambda e, kc=kc, k2=k2, pg_=pg_, wg=wg, gn=gn: e.matmul(pg_[:, 0:gn], wg[:, kc, k2, :], h1g[:, kc, 0:gn], start=(kc == 0), stop=(kc == 7)),
                                     reads=[wg.b, h1g.b], writes=[pg_.b], add=(kc > 0))
                        sa = sg[0]; sb_ = sg[1]; ma = m1[0]; mb = m1[1]
                        S.op("act", lambda e, pga=pga, sa=sa, gn=gn: e.activation(sa[:, 0:gn], pga[:, 0:gn], AF.Sigmoid), reads=[pga.b], writes=[sa.b])
                        S.op("act", lambda e, pgb=pgb, sb_=sb_, gn=gn: e.activation(sb_[:, 0:gn], pgb[:, 0:gn], AF.Sigmoid), reads=[pgb.b], writes=[sb_.b])
                        S.op("dve", lambda e, sa=sa, pa1=pa1, ma=ma, gn=gn: e.tensor_tensor(out=ma[:, 0:gn], in0=sa[:, 0:gn], in1=pa1[:, 0:gn], op=ALU.mult), reads=[sa.b, pa1.b], writes=[ma.b])
                        S.op("dve", lambda e, sb_=sb_, pa2=pa2, mb=mb, gn=gn: e.tensor_tensor(out=mb[:, 0:gn], in0=sb_[:, 0:gn], in1=pa2[:, 0:gn], op=ALU.mult), reads=[sb_.b, pa2.b], writes=[mb.b])
                        S.op("dve", lambda e, ma=ma, mb=mb, j=j, gn=gn: e.tensor_tensor(out=mT[:, j, 0:gn], in0=ma[:, 0:gn], in1=mb[:, 0:gn], op=ALU.add), reads=[ma.b, mb.b], writes=[mT.b], add=True)
                    for (tok0, nt) in tl:
                        c0 = tok0 - g0
                        xt = xts[rr.get("xt", 0) % 2]; rr["xt"] = rr.get("xt", 0) + 1
                        src = xp[tok0:tok0 + nt, :] if tok0 < 2048 else xs[:, :]
                        S.dma("sp", lambda e, xt=xt, nt=nt, src=src: e.dma_start(out=xt[:nt, :], in_=src), writes=[xt.b])
                        z = zt[0]; rr["zt"] = rr.get("zt", 0) + 1
                        gbc = g1p if tok0 < 2048 else g1s
                        for hf in range(2):
                            po = psr("mo", [6, 7])
                            for kc in range(8):
                                S.op("pe", lambda e, kc=kc, po=po, hf=hf, c0=c0, nt=nt: e.matmul(po[:nt, :], mT[:, kc, c0:c0 + nt], wout[:, kc, hf * 512:(hf + 1) * 512], start=(kc == 0), stop=(kc == 7)),
                                     reads=[mT.b, wout.b], writes=[po.b], add=(kc > 0))
                            S.op("dve", lambda e, po=po, z=z, hf=hf, nt=nt, gbc=gbc: e.tensor_tensor(out=z[:nt, hf * 512:(hf + 1) * 512], in0=po[:nt, :], in1=gbc[:nt, hf * 512:(hf + 1) * 512], op=ALU.mult),
                                 reads=[po.b, gbc.b], writes=[z.b], add=True)
                        S.op("dve", lambda e, z=z, xt=xt, nt=nt: e.scalar_tensor_tensor(out=z[:nt, :], in0=xt[:nt, :], scalar=ALPHA, in1=z[:nt, :], op0=ALU.mult, op1=ALU.add),
                             reads=[xt.b, z.b], writes=[z.b])
                        x1 = x1t[rr.get("x1", 0) % 2]; rr["x1"] = rr.get("x1", 0) + 1
                        if DBG.get('ydump') == 'z' and tok0 == 0:
                            S.dma("sp", lambda e, z=z: e.dma_start(out=y_p[0:128, :], in_=z[:, :]), reads=[z.b])
                            S.dma("sp", lambda e: e.dma_start(out=y_p[128:256, :], in_=g1p[:, :]), reads=[g1p.b])
                            S.dma("sp", lambda e: e.dma_start(out=y_p[256:384, :], in_=lg1[:, :]), reads=[lg1.b])
                            S.dma("sp", lambda e: e.dma_start(out=y_p[512:640, :], in_=lb1[:, :]), reads=[lb1.b])
                            S.dma("sp", lambda e, xt=xt: e.dma_start(out=y_p[640:768, :], in_=xt[:, :]), reads=[xt.b])
                        layer_norm(lnst, z, nt, lg1, lb1, x1)
                        if DBG.get('ydump') == 'z' and tok0 == 0:
                            S.dma("sp", lambda e, x1=x1: e.dma_start(out=y_p[768:896, :], in_=x1[:, :]), reads=[x1.b])
                        S.dma("sp", lambda e, x1=x1, nt=nt, tok0=tok0: e.dma_start(out=x1_scr[tok0:tok0 + nt, :], in_=x1[:nt, :]), reads=[x1.b], writes=[b_x1scr], sembuf=x1.b, add=True)
                        if DBG.get('ydump') == 'x1':
                            dstx = y_p[tok0:tok0 + nt, :] if tok0 < 2048 else y_s[:, :]
                            S.dma("sp", lambda e, x1=x1, nt=nt, dstx=dstx: e.dma_start(out=dstx, in_=x1[:nt, :]), reads=[x1.b])
                S.barrier(flush=True)
            act_es.close()
            S.barrier()

            stop_check(4)
            with ExitStack() as p5:
                lnrow = sb(p5, [1, 2, D], F32, "lnrow2")
                S.dma("sp", lambda e: e.dma_start(out=lnrow[:, 0, :], in_=ln2_g[:, :]), writes=[lnrow.b])
                S.dma("sp", lambda e: e.dma_start(out=lnrow[:, 1, :], in_=ln2_b[:, :]), writes=[lnrow.b], add=True)
                lg2 = sb(p5, [128, D], F32, "lg2"); lb2 = sb(p5, [128, D], F32, "lb2")
                g2p = sb(p5, [128, D], F32, "g2p"); g2s = sb(p5, [32, D], F32, "g2s")
                bc_rows(lg2, 128, ones_f[0:1, :], ones_f.b, lambda hf: lnrow[0:1, 0, hf * 512:(hf + 1) * 512], lnrow.b)
                bc_rows(lb2, 128, ones_f[0:1, :], ones_f.b, lambda hf: lnrow[0:1, 1, hf * 512:(hf + 1) * 512], lnrow.b)
                bc_rows(g2p, 128, selp_sb[:, :], selp_sb.b, lambda hf: modrow_g[:, 1, hf * 512:(hf + 1) * 512], modrow_g.b)
                bc_rows(g2s, 32, sels_sb[:, :], sels_sb.b, lambda hf: modrow_g[:, 1, hf * 512:(hf + 1) * 512], modrow_g.b)
                wpq = sb(p5, [128, 8, D], BF16, "wpq")
                S.dma("pool", lambda e: e.dma_start(out=wpq[:], in_=peer_wq.rearrange("(c p) n -> p c n", p=128)), writes=[wpq.b])
                iotaA = sb(p5, [128, 32, 128], BF16, "iotaA")
                S.op("pool", lambda e: e.iota(iotaA[:], [[0, 32], [1, 128]], base=0, channel_multiplier=0, allow_small_or_imprecise_dtypes=True), writes=[iotaA.b])
                x1ts = [sb(p5, [128, D], F32, "x1l") for _ in range(2)]
                h2Ts = [sb(p5, [128, 8, 128], BF16, "h2T") for _ in range(2)]
                qpT = sb(p5, [128, 8, 128], BF16, "qpT")
                Spe = sb(p5, [128, 8, 256], F32, "Spe")
                v12 = sb(p5, [128, 8, 2, 16], F32, "v12")
                i12 = sb(p5, [128, 8, 2, 16], U32, "i12")
                i12f = sb(p5, [128, 8, 2, 16], F32, "i12f")
                wk = sb(p5, [128, 256], F32, "wk")
                cand = sb(p5, [128, 8, 256], F32, "cand")
                sv = sb(p5, [128, 8, 16], F32, "sv")
                si = sb(p5, [128, 8, 16], U32, "si")
                sij = sb(p5, [128, 2, 8, 16], U32, "sij")
                sijf = sb(p5, [128, 2, 8, 16], F32, "sijf")
                eq = sb(p5, [128, 16, 16], F32, "eq")
                abw = sb(p5, [128, 3, 128], F32, "abw")
                zs = sb(p5, [128, 8], F32, "zs")
                abwT = sb(p5, [128, 3, 128], F32, "abwT")
                OA = sb(p5, [128, 32, 128], BF16, "OA"); OB = sb(p5, [128, 32, 128], BF16, "OB")
                GT = sb(p5, [128, 128, 128], BF16, "GT")
                utb = [sb(p5, [128, 4, 1024], BF16, "utl") for _ in range(2)]
                vtb = [sb(p5, [128, 4, 1024], BF16, "vtl") for _ in range(2)]
                xs_ = [sb(p5, [128, 512], F32, "gx") for _ in range(2)]
                us_ = [sb(p5, [128, 512], F32, "gu") for _ in range(2)]
                ws_ = [sb(p5, [128, 512], F32, "gw") for _ in range(2)]
                PTs = [sb(p5, [128, 512], BF16, "PT") for _ in range(2)]
                zt = [sb(p5, [128, D], F32, "zt5") for _ in range(1)]
                yt = zt
                jk5 = TL(OA.t[:, 0:8, :].rearrange("p a b -> p (a b)"), "jk5"); jk5.b = OA.b
                lnst = dict(s1=sb(p5, [128, 1], F32, "s1"), s2=sb(p5, [128, 1], F32, "s2"), jk=jk5,
                            mu=sb(p5, [128, 1], F32, "mu"), var=sb(p5, [128, 1], F32, "var"), rstd=sb(p5, [128, 1], F32, "rstd"))
                ut_r = ut_scr.rearrange("(a4 c) p k -> a4 p c k", c=4)
                v_r = v_scr.rearrange("(a4 c p) d -> a4 p c d", c=4, p=128)
                def front(ti):
                    tok0, nt = LNT[ti]
                    x1l = x1ts[ti % 2]; h2T = h2Ts[ti % 2]
                    segs = 0 if tok0 < 2048 else SAMP_SEGS
                    yield
                    load_h_T([x1l], h2T, h2T.b, 0, tok0, nt, segs, 24, 32, x1_scr[tok0:tok0 + nt, :])
                    rr["xt"] -= 1
                    for hd in range(8):
                        pm = psr("pq", [0, 1])
                        for kc in range(8):
                            yield
                            S.op("pe", lambda e, kc=kc, pm=pm, hd=hd, nt=nt: e.matmul(pm[:, 0:nt], wpq[:, kc, hd * 128:(hd + 1) * 128], h2T[:, kc, 0:nt], start=(kc == 0), stop=(kc == 7)),
                                 reads=[wpq.b, h2T.b], writes=[pm.b], add=(kc > 0))
                        yield
                        evac(qpT[:, hd, 0:nt], qpT.b, pm[:, 0:nt], pm.b)
                    for hd in range(8):
                        pm = psr("pq", [0, 1])
                        yield
                        S.op("pe", lambda e, pm=pm, hd=hd, nt=nt: e.matmul(pm[:nt, 0:256], qpT[:, hd, 0:nt], KBD[:, :], start=True, stop=True), reads=[qpT.b, KBD.b], writes=[pm.b])
                        yield
                        evac(Spe[:nt, hd, :], Spe.b, pm[:nt, 0:256], pm.b)
                    for hd in range(8):
                        for hf in range(2):
                            src = Spe[:nt, hd, hf * 128:(hf + 1) * 128]
                            yield
                            S.op("dve", lambda e, src=src, hd=hd, hf=hf, nt=nt: e.max(out=v12[:nt, hd, hf, 0:8], in_=src), reads=[Spe.b], writes=[v12.b], add=True)
                            yield
                            S.op("dve", lambda e, src=src, hd=hd, hf=hf, nt=nt: e.match_replace(out=wk[:nt, 0:128], in_to_replace=v12[:nt, hd, hf, 0:8], in_values=src, imm_value=-1e30),
                                 reads=[Spe.b, v12.b], writes=[wk.b])
                            yield
                            S.op("dve", lambda e, hd=hd, hf=hf, nt=nt: e.max(out=v12[:nt, hd, hf, 8:16], in_=wk[:nt, 0:128]), reads=[wk.b], writes=[v12.b], add=True)
                            yield
                            S.op("dve", lambda e, src=src, hd=hd, hf=hf, nt=nt: e.max_index(out=i12[:nt, hd, hf, 0:8], in_max=v12[:nt, hd, hf, 0:8], in_values=src),
                                 reads=[Spe.b, v12.b], writes=[i12.b], add=True)
                            yield
                            S.op("dve", lambda e, src=src, hd=hd, hf=hf, nt=nt: e.max_index(out=i12[:nt, hd, hf, 8:16], in_max=v12[:nt, hd, hf, 8:16], in_values=src),
                                 reads=[Spe.b, v12.b], writes=[i12.b], add=True)
                    yield
                    S.op("dve", lambda e, nt=nt: e.tensor_copy(i12f[:nt].rearrange("p h f k -> p (h f k)"), i12[:nt].rearrange("p h f k -> p (h f k)")), reads=[i12.b], writes=[i12f.b])
                    for hd in range(8):
                        yield
                        S.op("dve", lambda e, hd=hd, nt=nt: e.tensor_tensor(out=cand[:nt, hd, :].rearrange("p (i j) -> p i j", i=16),
                                                                           in0=v12[:nt, hd, 0, :].unsqueeze(2).to_broadcast([nt, 16, 16]),
                                                                           in1=v12[:nt, hd, 1, :].unsqueeze(1).to_broadcast([nt, 16, 16]), op=ALU.add),
                             reads=[v12.b], writes=[cand.b], add=True)
                    for hd in range(8):
                        src = cand[:nt, hd, :]
                        yield
                        S.op("dve", lambda e, src=src, hd=hd, nt=nt: e.max(out=sv[:nt, hd, 0:8], in_=src), reads=[cand.b], writes=[sv.b], add=True)
                        yield
                        S.op("dve", lambda e, src=src, hd=hd, nt=nt: e.match_replace(out=wk[:nt, :], in_to_replace=sv[:nt, hd, 0:8], in_values=src, imm_value=-1e30),
                             reads=[cand.b, sv.b], writes=[wk.b])
                        yield
                        S.op("dve", lambda e, hd=hd, nt=nt: e.max(out=sv[:nt, hd, 8:16], in_=wk[:nt, :]), reads=[wk.b], writes=[sv.b], add=True)
                        yield
                        S.op("dve", lambda e, src=src, hd=hd, nt=nt: e.max_index(out=si[:nt, hd, 0:8], in_max=sv[:nt, hd, 0:8], in_values=src), reads=[cand.b, sv.b], writes=[si.b], add=True)
                        yield
                        S.op("dve", lambda e, src=src, hd=hd, nt=nt: e.max_index(out=si[:nt, hd, 8:16], in_max=sv[:nt, hd, 8:16], in_values=src), reads=[cand.b, sv.b], writes=[si.b], add=True)
                    wv = abw[:nt, 2, :].rearrange("p (h k) -> p h k", h=8)
                    yield
                    S.op("dve", lambda e, nt=nt, wv=wv: e.tensor_tensor(out=wv, in0=sv[:nt, :, :], in1=sv[:nt, :, 0:1].to_broadcast([nt, 8, 16]), op=ALU.subtract),
                         reads=[sv.b], writes=[abw.b], add=True)
                    yield
                    S.op("act", lambda e, nt=nt: e.activation(abw[:nt, 2, :], abw[:nt, 2, :], AF.Exp), reads=[abw.b], writes=[abw.b])
                    yield
                    S.op("dve", lambda e, nt=nt, wv=wv: e.tensor_reduce(out=zs[:nt, :], in_=wv, axis=AX.X, op=ALU.add), reads=[abw.b], writes=[zs.b])
                    yield
                    S.op("dve", lambda e, nt=nt: e.reciprocal(zs[:nt, :], zs[:nt, :]), reads=[zs.b], writes=[zs.b])
                    yield
                    S.op("dve", lambda e, nt=nt, wv=wv: e.tensor_tensor(out=wv, in0=wv, in1=zs[:nt, :].unsqueeze(2).to_broadcast([nt, 8, 16]), op=ALU.mult),
                         reads=[abw.b, zs.b], writes=[abw.b])
                    yield
                    S.op("dve", lambda e, nt=nt: e.tensor_single_scalar(sij[:nt, 0].rearrange("p h k -> p (h k)"), si[:nt].rearrange("p h k -> p (h k)"), 4, op=ALU.logical_shift_right),
                         reads=[si.b], writes=[sij.b], add=True)
                    yield
                    S.op("dve", lambda e, nt=nt: e.tensor_single_scalar(sij[:nt, 1].rearrange("p h k -> p (h k)"), si[:nt].rearrange("p h k -> p (h k)"), 15, op=ALU.bitwise_and),
                         reads=[si.b], writes=[sij.b], add=True)
                    yield
                    S.op("dve", lambda e, nt=nt: e.tensor_copy(sijf[:nt].rearrange("p t h k -> p (t h k)"), sij[:nt].rearrange("p t h k -> p (t h k)")), reads=[sij.b], writes=[sijf.b])
                    for hd in range(8):
                        for ab in range(2):
                            yield
                            S.op("dve", lambda e, hd=hd, ab=ab, nt=nt: e.tensor_tensor(out=eq[:nt], in0=sijf[:nt, ab, hd, :].unsqueeze(2).to_broadcast([nt, 16, 16]),
                                                                                     in1=iota16[:nt, :].unsqueeze(1).to_broadcast([nt, 16, 16]), op=ALU.is_equal),
                                 reads=[sijf.b, iota16.b], writes=[eq.b])
                            yield
                            S.op("dve", lambda e, hd=hd, ab=ab, nt=nt: e.tensor_tensor(out=eq[:nt], in0=eq[:nt], in1=i12f[:nt, hd, ab, :].unsqueeze(1).to_broadcast([nt, 16, 16]), op=ALU.mult),
                                 reads=[eq.b, i12f.b], writes=[eq.b])
                            yield
                            S.op("dve", lambda e, hd=hd, ab=ab, nt=nt: e.tensor_reduce(out=abw[:nt, ab, hd * 16:(hd + 1) * 16], in_=eq[:nt], axis=AX.X, op=ALU.add),
                                 reads=[eq.b], writes=[abw.b], add=True)
                    pm = psr("pq", [0, 1])
                    for k3 in range(3):
                        yield
                        S.op("pe", lambda e, pm=pm, k3=k3, nt=nt: e.transpose(pm[:, k3 * 128:k3 * 128 + nt], abw[:nt, k3, :], ident[:nt, :nt]), reads=[abw.b, ident.b], writes=[pm.b], add=(k3 > 0))
                    yield
                    S.op("act", lambda e, pm=pm: e.activation(abwT[:].rearrange("p k n -> p (k n)"), pm[:, 0:384], AF.Copy), reads=[pm.b], writes=[abwT.b])
                    yield
                def tail(ti):
                    tok0, nt = LNT[ti]
                    for n0 in range(0, nt, 32):
                        nn = min(32, nt - n0)
                        S.op("dve", lambda e, n0=n0, nn=nn: e.tensor_tensor(out=OA[:, 0:nn, :], in0=iotaA[:, 0:nn, :], in1=abwT[:, 0, n0:n0 + nn].unsqueeze(2).to_broadcast([128, nn, 128]), op=ALU.is_equal),
                             reads=[iotaA.b, abwT.b], writes=[OA.b])
                        S.op("pool", lambda e, n0=n0, nn=nn: e.tensor_tensor(out=OA[:, 0:nn, :], in0=OA[:, 0:nn, :], in1=abwT[:, 2, n0:n0 + nn].unsqueeze(2).to_broadcast([128, nn, 128]), op=ALU.mult),
                             reads=[OA.b, abwT.b], writes=[OA.b])
                        S.op("dve", lambda e, n0=n0, nn=nn: e.tensor_tensor(out=OB[:, 0:nn, :], in0=iotaA[:, 0:nn, :], in1=abwT[:, 1, n0:n0 + nn].unsqueeze(2).to_broadcast([128, nn, 128]), op=ALU.is_equal),
                             reads=[iotaA.b, abwT.b], writes=[OB.b])
                        for n4 in range(0, nn, 4):
                            pg_ = psr("pg", [2, 3])
                            for q in range(4):
                                nl = n4 + q
                                S.op("pe", lambda e, pg_=pg_, q=q, nl=nl: e.matmul(pg_[:, q * 128:(q + 1) * 128], OB[:, nl, :], OA[:, nl, :], start=True, stop=True),
                                     reads=[OA.b, OB.b], writes=[pg_.b], add=(q > 0))
                            evac(GT[:].rearrange("p a n -> p n a")[:, n0 + n4:n0 + n4 + 4, :], GT.b, pg_[:, :].rearrange("p (n a) -> p n a", n=4), pg_.b)
                def main(ti, fr):
                    tok0, nt = LNT[ti]
                    x1l = x1ts[ti % 2]; h2T = h2Ts[ti % 2]
                    oacc = [PS[6], PS[7]]
                    def stage_ab(a4, nt=nt):
                        ub = utb[a4 % 2]; vb = vtb[a4 % 2]
                        S.dma("sp", lambda e, ub=ub, a4=a4: e.dma_start(out=ub[:], in_=ut_r[a4]), reads=[b_utscr], writes=[ub.b])
                        S.dma("sp", lambda e, vb=vb, a4=a4: e.dma_start(out=vb[:], in_=v_r[a4]), reads=[b_vscr], writes=[vb.b])
                        pa = PS[4 + a4 % 2]
                        for c in range(4):
                            for kc in range(8):
                                S.op("pe", lambda e, pa=pa, c=c, kc=kc, ub=ub, nt=nt: e.matmul(pa[:, c * 128:c * 128 + nt], ub[:, c, kc * 128:(kc + 1) * 128], h2T[:, kc, 0:nt], start=(kc == 0), stop=(kc == 7)),
                                     reads=[ub.b, h2T.b], writes=[pa.b], add=(c > 0 or kc > 0))
                        gx = xs_[a4 % 2]; gu = us_[a4 % 2]

                        def v3(t_, nt=nt):
                            return t_[:, :].rearrange("p (c n) -> p c n", c=4)[:, :, 0:nt]
                        pav = v3(pa); gxv = v3(gx); guv = v3(gu)
                        S.op("act", lambda e, gxv=gxv, pav=pav: e.activation(gxv, pav, AF.Copy), reads=[pa.b], writes=[gx.b])
                        S.op("act", lambda e, guv=guv, pav=pav: e.activation(guv, pav, AF.Square), reads=[pa.b], writes=[gu.b])
                        S.op("dve", lambda e, guv=guv: e.tensor_scalar(guv, guv, 0.044715, 1.0, op0=ALU.mult, op1=ALU.add), reads=[gu.b], writes=[gu.b])

                    def stage_cde(a4, nt=nt):
                        vb = vtb[a4 % 2]
                        gx = xs_[a4 % 2]; gu = us_[a4 % 2]; gw = ws_[a4 % 2]; PT = PTs[a4 % 2]

                        def v3(t_, nt=nt):
                            return t_[:, :].rearrange("p (c n) -> p c n", c=4)[:, :, 0:nt]
                        gxv = v3(gx); guv = v3(gu); gwv = v3(gw); PTv = v3(PT)
                        gtv = GT[:, a4 * 4:(a4 + 1) * 4, 0:nt]
                        S.op("pool", lambda e, guv=guv, gxv=gxv, gwv=gwv: e.tensor_tensor(out=gwv, in0=guv, in1=gxv, op=ALU.mult), reads=[gu.b, gx.b], writes=[gw.b])
                        S.op("act", lambda e, gwv=gwv: e.activation(gwv, gwv, AF.Sigmoid, scale=1.5957691216057308), reads=[gw.b], writes=[gw.b])
                        S.op("dve", lambda e, gwv=gwv, gxv=gxv: e.tensor_tensor(out=gxv, in0=gwv, in1=gxv, op=ALU.mult), reads=[gw.b, gx.b], writes=[gx.b])
                        S.op("dve", lambda e, gxv=gxv, PTv=PTv, gtv=gtv: e.tensor_tensor(out=PTv, in0=gxv, in1=gtv, op=ALU.mult),
                             reads=[gx.b, GT.b], writes=[PT.b])
                        for c in range(4):
                            for hf in range(2):
                                first = (a4 == 0 and c == 0)
                                last = (a4 == 31 and c == 3)
                                S.op("pe", lambda e, PT=PT, c=c, hf=hf, vb=vb, first=first, last=last, nt=nt: e.matmul(oacc[hf][:nt, :], PT[:, c * 128:c * 128 + nt], vb[:, c, hf * 512:(hf + 1) * 512], start=first, stop=last),
                                     reads=[PT.b, vb.b], writes=[oacc[hf].b], add=(not first))

                    stage_ab(0)
                    for a4 in range(32):
                        if a4 + 1 < 32:
                            stage_ab(a4 + 1)
                        stage_cde(a4)
                        if fr is not None:
                            for _ in range(16):
                                next(fr, None)
                    if pe_dbg:
                        y = yt[0]
                        for hf in range(2):
                            S.op("act", lambda e, y=y, hf=hf, nt=nt: e.activation(y[:nt, hf * 512:(hf + 1) * 512], oacc[hf][:nt, :], AF.Copy), reads=[oacc[hf].b], writes=[y.b], add=True)
                        dstd = y_p[tok0:tok0 + nt, :] if tok0 < 2048 else y_s[:, :]
                        S.dma("sp", lambda e, y=y, nt=nt, dstd=dstd: e.dma_start(out=dstd, in_=y[:nt, :]), reads=[y.b])
                    z = zt[0]; rr["zt5"] = rr.get("zt5", 0) + 1
                    gbc = g2p if tok0 < 2048 else g2s
                    for hf in range(2):
                        S.op("dve", lambda e, z=z, hf=hf, nt=nt, gbc=gbc: e.tensor_tensor(out=z[:nt, hf * 512:(hf + 1) * 512], in0=oacc[hf][:nt, :], in1=gbc[:nt, hf * 512:(hf + 1) * 512], op=ALU.mult),
                             reads=[oacc[hf].b, gbc.b], writes=[z.b], add=True)
                    S.op("dve", lambda e, z=z, x1l=x1l, nt=nt: e.scalar_tensor_tensor(out=z[:nt, :], in0=x1l[:nt, :], scalar=ALPHA, in1=z[:nt, :], op0=ALU.mult, op1=ALU.add),
                         reads=[x1l.b, z.b], writes=[z.b])
                    y = yt[0]; rr["yt"] = rr.get("yt", 0) + 1
                    layer_norm(lnst, z, nt, lg2, lb2, y)
                    dst = y_p[tok0:tok0 + nt, :] if tok0 < 2048 else y_s[:, :]
                    if not DBG.get('ydump'):
                        S.dma("sp", lambda e, y=y, nt=nt, dst=dst: e.dma_start(out=dst, in_=y[:nt, :]), reads=[y.b])
                def drain(g):
                    if g is not None:
                        for _ in g:
                            pass
                drain(front(0))
                for ti in range(len(LNT)):
                    tail(ti)
                    fr = front(ti + 1) if ti + 1 < len(LNT) else None
                    main(ti, fr)
                    drain(fr)
                S.barrier(flush=True)
        except _Stop:
            for nm in ('att_es', 'act_es'):
                st_ = locals().get(nm)
                if st_ is not None:
                    st_.close()
        S.finish()
        print("instructions:", S.ninstr, "sems:", S.semid)
    return nc


_CACHE = {}


def kernel(**inp):
    f32 = np.float32
    g = lambda k: np.ascontiguousarray(np.asarray(inp[k]))
    if "nc" not in _CACHE:
        _CACHE["nc"] = build_program()
    nc = _CACHE["nc"]
    j = np.arange(383)
    bk = t5_bucket_np(j - 127)
    ohb = np.zeros((32, 383), f32); ohb[bk, j] = 1.0
    negrow = np.where(j < 127, NEG, 0.0).astype(f32)[None, :]
    sel_p = np.zeros((5, 128), f32); sel_p[0, :] = 1.0
    sel_s = np.zeros((5, 32), f32)
    for b in range(4):
        sel_s[1 + b, 8 * b:8 * b + 8] = 1.0
    xpr, xsm, cpr, csm = g("x_prompt"), g("x_sample"), g("c_prompt"), g("c_sample")
    pr = DBG['pool_rows']
    ck = g("cache_k")[0].reshape(2560 * 128, 128)[:pr]; cv = g("cache_v")[0].reshape(2560 * 128, 128)[:pr]
    cki = g("cache_kidx")[0].reshape(2560 * 128, 64)[:pr]
    sc = g("state_conv")[0]; pt = g("page_table")
    shared = dict(ck=ck, cv=cv, cki=cki, rel_bias=g("rel_bias"), w_ada=g("w_ada")[0], b_ada=g("b_ada"), w_in=g("w_in")[0],
                  conv_w=g("conv_w")[0], conv_b=g("conv_b"), w_o_attn=g("w_o_attn")[0], w_o_conv=g("w_o_conv")[0], w_out=g("w_out")[0],
                  ln1_g=g("ln1_g"), ln1_b=g("ln1_b"), ln2_g=g("ln2_g"), ln2_b=g("ln2_b"), peer_wq=g("peer_wq")[0],
                  peer_k1=g("peer_k1")[0], peer_k2=g("peer_k2")[0], peer_u=g("peer_u")[0], peer_v=g("peer_v")[0],
                  ohb=ohb, negrow=negrow, sel_p=sel_p, sel_s=sel_s)
    in_maps = []
    for c in range(8):
        m = dict(shared)
        m["xp"] = xpr[c]
        m["xs"] = np.ascontiguousarray(xsm[4 * c:4 * c + 4].reshape(32, D))
        m["cvec"] = np.ascontiguousarray(np.concatenate([cpr[c:c + 1], csm[4 * c:4 * c + 4]], axis=0))
        m["sconv"] = np.ascontiguousarray(sc[4 * c:4 * c + 4].reshape(8, 512))
        m["ptab"] = np.ascontiguousarray(pt[4 * c:4 * c + 4]).astype(np.int32)
        in_maps.append(m)
    res = run_bass_kernel_spmd(nc, in_maps, core_ids=list(range(8)))
    R = res.results
    y_prompt = np.stack([R[c]["y_p"] for c in range(8)])
    y_sample = np.concatenate([R[c]["y_s"].reshape(4, 8, D) for c in range(8)])
    k_prompt = np.stack([R[c]["k_p"].reshape(SEQ, 2, 64) for c in range(8)])[None]
    v_prompt = np.stack([R[c]["v_p"].reshape(SEQ, 2, 64) for c in range(8)])[None]
    ki_prompt = np.stack([R[c]["ki_p"] for c in range(8)])[None]
    conv_prompt = np.stack([R[c]["conv_o"][0:2] for c in range(8)])[None]
    k_sample = np.concatenate([R[c]["k_s"].reshape(4, 8, 2, 64) for c in range(8)])[None]
    v_sample = np.concatenate([R[c]["v_s"].reshape(4, 8, 2, 64) for c in range(8)])[None]
    ki_sample = np.concatenate([R[c]["ki_s"].reshape(4, 8, 64) for c in range(8)])[None]
    conv_sample = np.concatenate([R[c]["conv_o"][2:10].reshape(4, 2, 512) for c in range(8)])[None]
    outs = (y_prompt, y_sample, k_prompt, v_prompt, ki_prompt, conv_prompt, k_sample, v_sample, ki_sample, conv_sample)
    return tuple(np.ascontiguousarray(o, dtype=f32) for o in outs)
```
